# Optimizing a Trainium2 kernel written in Bass

```python
import jax
import jax.numpy as jnp
from jax import lax
import numpy as np

D_MODEL = 2048
BATCH = 8
SEQ = 2048
DEPTH = 1

N_META = 16
HEAD_DIM = 128
GDN_HEADS = 8
FOX_HEADS = 8
GDN_WIDTH = GDN_HEADS * HEAD_DIM
FOX_WIDTH = FOX_HEADS * HEAD_DIM
MIX_WIDTH = GDN_WIDTH + FOX_WIDTH
CONV_WIDTH = 4
CHUNK = 64
Q_BLOCK = 128
EPS = 1e-6

IN_SPLITS = (GDN_WIDTH, GDN_WIDTH, GDN_WIDTH, GDN_WIDTH, GDN_HEADS, GDN_HEADS,
             FOX_WIDTH, FOX_WIDTH, FOX_WIDTH, FOX_WIDTH, FOX_HEADS)
IN_WIDTH = sum(IN_SPLITS)
SPLIT_POINTS = tuple(int(s) for s in np.cumsum(IN_SPLITS)[:-1])

kernel_name = 'hymba_gdn_fox_sandwich_meta'


def rmsnorm(x, w):
    xf = x.astype(jnp.float32)
    y = xf * lax.rsqrt(jnp.mean(xf * xf, axis=-1, keepdims=True) + EPS)
    return y * w.astype(jnp.float32)


def l2norm(x):
    return x * lax.rsqrt(jnp.sum(x * x, axis=-1, keepdims=True) + EPS)


def causal_depthwise_conv(x, w):
    L = x.shape[1]
    xp = jnp.pad(x, ((0, 0), (CONV_WIDTH - 1, 0), (0, 0)))
    y = xp[:, 0:L, :] * w[:, 0]
    for j in range(1, CONV_WIDTH):
        y = y + xp[:, j:j + L, :] * w[:, j]
    return y


def gdn_chunk_prep(q, k, v, beta, g):
    C = q.shape[-2]
    G = jnp.cumsum(g, axis=-1)
    causal = jnp.tril(jnp.ones((C, C), dtype=bool))
    strict = jnp.tril(jnp.ones((C, C), dtype=bool), -1)
    diff = G[..., :, None] - G[..., None, :]
    D = jnp.where(causal, jnp.exp(jnp.where(causal, diff, 0.0)), 0.0)
    kk = jnp.einsum('bhncd,bhnsd->bhncs', k, k)
    n_mat = jnp.where(strict, beta[..., :, None] * kk * D, 0.0)
    eye = jnp.eye(C, dtype=q.dtype)
    T = lax.linalg.triangular_solve(eye + n_mat, jnp.broadcast_to(eye, n_mat.shape),
                                    left_side=True, lower=True, unit_diagonal=True)
    U = jnp.einsum('bhncs,bhnsd->bhncd', T, beta[..., None] * v)
    W = jnp.einsum('bhncs,bhnsd->bhncd', T, (beta * jnp.exp(G))[..., None] * k)
    a_qk = jnp.where(causal, jnp.einsum('bhncd,bhnsd->bhncs', q, k) * D, 0.0)
    q_dec = q * jnp.exp(G)[..., None]
    k_dec = k * jnp.exp(G[..., -1:] - G)[..., None]
    chunk_decay = jnp.exp(G[..., -1])
    return (q_dec, k_dec, U, W, a_qk, chunk_decay)


def gdn_chunk_step(S, xs):
    q_dec, k_dec, U, W, a_qk, decay = xs
    v_new = U - jnp.einsum('bhcd,bhde->bhce', W, S)
    o = jnp.einsum('bhcd,bhde->bhce', q_dec, S) + jnp.einsum('bhcs,bhse->bhce', a_qk, v_new)
    S = S * decay[..., None, None] + jnp.einsum('bhcd,bhce->bhde', k_dec, v_new)
    return S, o


def gated_delta_rule(q, k, v, beta, g):
    B, H, L, d = q.shape
    n_chunks = (L - N_META) // CHUNK

    def split(t):
        meta = t[:, :, :N_META][:, :, None]
        real = t[:, :, N_META:].reshape((B, H, n_chunks, CHUNK) + t.shape[3:])
        return meta, real

    parts = [split(t) for t in (q, k, v, beta, g)]
    meta_in = gdn_chunk_prep(*[p[0] for p in parts])
    real_in = gdn_chunk_prep(*[p[1] for p in parts])
    S0 = jnp.zeros((B, H, d, d), jnp.float32)
    S, o_meta = gdn_chunk_step(S0, tuple(t[:, :, 0] for t in meta_in))
    real_xs = tuple(jnp.moveaxis(t, 2, 0) for t in real_in)
    _, o_real = lax.scan(gdn_chunk_step, S, real_xs)
    o_real = jnp.moveaxis(o_real, 0, 2).reshape(B, H, L - N_META, d)
    return jnp.concatenate([o_meta, o_real], axis=2)


def forgetting_attention(q, k, v, logf):
    L = q.shape[2]
    scale = HEAD_DIM ** -0.5
    c = jnp.cumsum(logf, axis=-1)
    starts = [0] + [N_META + i * Q_BLOCK for i in range((L - N_META) // Q_BLOCK)]
    ends = [N_META] + [N_META + (i + 1) * Q_BLOCK for i in range((L - N_META) // Q_BLOCK)]
    outs = []
    for s0, s1 in zip(starts, ends):
        qb = q[:, :, s0:s1]
        kb = k[:, :, :s1]
        vb = v[:, :, :s1]
        logits = (jnp.einsum('bhqd,bhkd->bhqk', qb, kb) * scale
                  + (c[:, :, s0:s1, None] - c[:, :, None, :s1]))
        qpos = s0 + jnp.arange(s1 - s0)
        kpos = jnp.arange(s1)
        mask = kpos[None, :] <= qpos[:, None]
        logits = jnp.where(mask, logits, -jnp.inf)
        p = jax.nn.softmax(logits, axis=-1)
        outs.append(jnp.einsum('bhqk,bhkd->bhqd', p, vb))
    return jnp.concatenate(outs, axis=2)


def hybrid_layer(h, pre_w, w_in, conv_w, a_log, dt_bias, gdn_norm_w,
                 fox_q_norm_w, fox_k_norm_w, fox_f_bias, w_out, post_w):
    B, L, _ = h.shape
    f32 = jnp.float32
    xn = rmsnorm(h, pre_w).astype(h.dtype)
    proj = xn @ w_in
    gq, gk, gv, gz, gb, ga, fq, fk, fv, fg, ff = jnp.split(proj, SPLIT_POINTS, axis=-1)

    def heads(t, n):
        return t.reshape(B, L, n, HEAD_DIM).transpose(0, 2, 1, 3)

    qkv = jax.nn.silu(causal_depthwise_conv(jnp.concatenate([gq, gk, gv], axis=-1), conv_w))
    gq, gk, gv = jnp.split(qkv.astype(f32), 3, axis=-1)
    gq = l2norm(heads(gq, GDN_HEADS)) * (HEAD_DIM ** -0.5)
    gk = l2norm(heads(gk, GDN_HEADS))
    gv = heads(gv, GDN_HEADS)
    beta = jax.nn.sigmoid(gb.astype(f32)).transpose(0, 2, 1)
    g = (-jnp.exp(a_log.astype(f32))
         * jax.nn.softplus(ga.astype(f32) + dt_bias.astype(f32))).transpose(0, 2, 1)
    o_gdn = gated_delta_rule(gq, gk, gv, beta, g)
    o_gdn = rmsnorm(o_gdn, gdn_norm_w) * jax.nn.silu(heads(gz.astype(f32), GDN_HEADS))

    fq = rmsnorm(heads(fq, FOX_HEADS), fox_q_norm_w)
    fk = rmsnorm(heads(fk, FOX_HEADS), fox_k_norm_w)
    fv = heads(fv.astype(f32), FOX_HEADS)
    logf = jax.nn.log_sigmoid(ff.astype(f32) + fox_f_bias.astype(f32)).transpose(0, 2, 1)
    o_fox = forgetting_attention(fq, fk, fv, logf) * jax.nn.silu(heads(fg.astype(f32), FOX_HEADS))

    merged = jnp.concatenate([o_gdn, o_fox], axis=1)
    merged = merged.transpose(0, 2, 1, 3).reshape(B, L, MIX_WIDTH).astype(h.dtype)
    out = merged @ w_out
    return h + rmsnorm(out, post_w).astype(h.dtype)


def setup_inputs(seed: int = 0) -> dict:
    key = jax.random.key(seed)
    ks = jax.random.split(key, 14)
    f32 = jnp.float32
    x = jax.random.normal(ks[0], (BATCH, SEQ, D_MODEL), f32)
    meta_tokens = jax.random.normal(ks[1], (N_META, D_MODEL), f32)
    pre_norm_w = 1.0 + 0.01 * jax.random.normal(ks[2], (DEPTH, D_MODEL), f32)
    w_in = jax.random.normal(ks[3], (DEPTH, D_MODEL, IN_WIDTH), f32) * (D_MODEL ** -0.5)
    conv_w = jax.random.normal(ks[4], (DEPTH, 3 * GDN_WIDTH, CONV_WIDTH), f32) * (CONV_WIDTH ** -0.5)
    a_log = jnp.log(jax.random.uniform(ks[5], (DEPTH, GDN_HEADS), f32, 1.0, 16.0))
    dt = jnp.exp(jax.random.uniform(ks[6], (DEPTH, GDN_HEADS), f32,
                                    float(np.log(1e-3)), float(np.log(1e-1))))
    dt_bias = dt + jnp.log(-jnp.expm1(-dt))
    gdn_norm_w = 1.0 + 0.01 * jax.random.normal(ks[7], (DEPTH, HEAD_DIM), f32)
    fox_q_norm_w = 1.0 + 0.01 * jax.random.normal(ks[8], (DEPTH, HEAD_DIM), f32)
    fox_k_norm_w = 1.0 + 0.01 * jax.random.normal(ks[9], (DEPTH, HEAD_DIM), f32)
    fox_f_bias = jax.random.uniform(ks[10], (DEPTH, FOX_HEADS), f32, 1.0, 4.0)
    w_out = jax.random.normal(ks[11], (DEPTH, MIX_WIDTH, D_MODEL), f32) * (MIX_WIDTH ** -0.5)
    post_norm_w = 1.0 + 0.01 * jax.random.normal(ks[12], (DEPTH, D_MODEL), f32)
    return {'x': x, 'meta_tokens': meta_tokens, 'pre_norm_w': pre_norm_w, 'w_in': w_in,
            'conv_w': conv_w, 'a_log': a_log, 'dt_bias': dt_bias, 'gdn_norm_w': gdn_norm_w,
            'fox_q_norm_w': fox_q_norm_w, 'fox_k_norm_w': fox_k_norm_w, 'fox_f_bias': fox_f_bias,
            'w_out': w_out, 'post_norm_w': post_norm_w}


def reference(x, meta_tokens, pre_norm_w, w_in, conv_w, a_log, dt_bias, gdn_norm_w,
              fox_q_norm_w, fox_k_norm_w, fox_f_bias, w_out, post_norm_w):
    B = x.shape[0]
    meta = jnp.broadcast_to(meta_tokens.astype(x.dtype)[None], (B, N_META, x.shape[-1]))
    h = jnp.concatenate([meta, x], axis=1)
    for l in range(DEPTH):
        h = hybrid_layer(h, pre_norm_w[l], w_in[l], conv_w[l], a_log[l], dt_bias[l],
                         gdn_norm_w[l], fox_q_norm_w[l], fox_k_norm_w[l], fox_f_bias[l],
                         w_out[l], post_norm_w[l])
    return h[:, N_META:]
```

```python
import numpy as np
from contextlib import ExitStack
import concourse.bass as bass
import concourse.mybir as mybir
from concourse.bass_utils import run_bass_kernel_spmd

F32 = mybir.dt.float32
BF16 = mybir.dt.bfloat16
AF = mybir.ActivationFunctionType
ALU = mybir.AluOpType
EPS = 1e-6
NEG = -30000.0

FULL_CFG = dict(DM=2048, NT=16, GT=4, HG=8, HF=8)


class Dep:
    __slots__ = ("w", "rs")

    def __init__(self):
        self.w = None
        self.rs = []


class Buf:
    def __init__(self, t):
        self.t = t
        self.d = Dep()

    def __getitem__(self, k):
        return self.t[k]


class Sched:
    def __init__(self, nc, ctx):
        self.nc = nc
        self.ctx = ctx
        self.eng = {"pe": nc.tensor, "act": nc.scalar, "dve": nc.vector, "pool": nc.gpsimd, "sp": nc.sync}
        self.sem = {k: ctx.enter_context(nc.semaphore("s_" + k)) for k in self.eng}
        self.cnt = {k: 0 for k in self.eng}
        self.seen = {k: {} for k in self.eng}
        self.dsems = []

    def new_dma_sem(self, name):
        d = {"sem": self.ctx.enter_context(self.nc.semaphore(name)), "val": 0}
        self.dsems.append(d)
        return d

    def _wait(self, e, tok):
        sem, val = tok
        if e == "pe" and sem.name == "s_pe":
            return
        if self.seen[e].get(sem.name, 0) >= val:
            return
        self.eng[e].wait_ge(sem, val)
        self.seen[e][sem.name] = val

    def _waits(self, e, reads, writes):
        best = {}
        for b in reads:
            if b.d.w is not None:
                t = b.d.w
                if t[0].name not in best or best[t[0].name][1] < t[1]:
                    best[t[0].name] = t
        for b in writes:
            for t in ([b.d.w] if b.d.w is not None else []) + b.d.rs:
                if t[0].name not in best or best[t[0].name][1] < t[1]:
                    best[t[0].name] = t
        for t in best.values():
            self._wait(e, t)

    def op(self, e, fn, reads=(), writes=(), inc=True):
        self._waits(e, reads, writes)
        ins = fn()
        tok = (self.sem[e], self.cnt[e] + 1)
        if inc:
            ins.then_inc(self.sem[e], 1)
            self.cnt[e] += 1
        for b in reads:
            b.d.rs.append(tok)
            if len(b.d.rs) > 64:
                b.d.rs = self._compact(b.d.rs)
        for b in writes:
            b.d.w = tok
            b.d.rs = []
        return ins

    @staticmethod
    def _compact(rs):
        best = {}
        for t in rs:
            if t[0].name not in best or best[t[0].name][1] < t[1]:
                best[t[0].name] = t
        return list(best.values())

    def dma(self, e, out, in_, dsem, reads=(), writes=()):
        self._waits(e, reads, writes)
        ins = self.eng[e].dma_start(out=out, in_=in_)
        ins.then_inc(dsem["sem"], 16)
        dsem["val"] += 16
        tok = (dsem["sem"], dsem["val"])
        for b in reads:
            b.d.rs.append(tok)
        for b in writes:
            b.d.w = tok
            b.d.rs = []
        return ins

    def barrier(self):
        toks = [(self.sem[k], self.cnt[k]) for k in self.eng if self.cnt[k] > 0]
        toks += [(d["sem"], d["val"]) for d in self.dsems if d["val"] > 0]
        for e in self.eng:
            for t in toks:
                if t[0].name == "s_" + e:
                    continue
                self._wait(e, t)
        for e in ("act", "dve", "pool"):
            if self.cnt[e] > 0 and self.seen[e].get("s_" + e, 0) < self.cnt[e]:
                self.eng[e].wait_ge(self.sem[e], self.cnt[e])
                self.seen[e]["s_" + e] = self.cnt[e]


def build(cfg):
    DM, NT, GT, HG, HF = cfg["DM"], cfg["NT"], cfg["GT"], cfg["HG"], cfg["HF"]
    KC = DM // 128
    NU = HG + HF
    L = 16 + NT * 128
    NTI = NT + 1
    NG = NT // GT + 1
    NS = 2 * HG + HF
    GW = GT * 128
    HM = max(HG, HF)
    NDT = F32

    def tcol(i):
        return (0, 16) if i == 0 else (16 + (i - 1) * 128, 16 + i * 128)

    def tP(i):
        return 16 if i == 0 else 128

    def gtiles(g):
        return [0] if g == 0 else list(range((g - 1) * GT + 1, g * GT + 1))

    def gcol(g):
        ts = gtiles(g)
        return tcol(ts[0])[0], tcol(ts[-1])[1]

    nc = bass.Bass("TRN2", target_bir_lowering=False)
    xin = nc.dram_tensor("xin", [NT * 128, DM], F32, kind="ExternalInput").ap()
    meta = nc.dram_tensor("meta", [16, DM], F32, kind="ExternalInput").ap()
    wu = nc.dram_tensor("wu", [NU, 4, 128, KC, 128], F32, kind="ExternalInput").ap()
    wsm_d = nc.dram_tensor("wsm", [128, KC, NS], F32, kind="ExternalInput").ap()
    convw_d = nc.dram_tensor("convw", [128, HG * 12], F32, kind="ExternalInput").ap()
    hp_d = nc.dram_tensor("hp", [128, NS], F32, kind="ExternalInput").ap()
    ncols_d = nc.dram_tensor("ncols", [128, 3], F32, kind="ExternalInput").ap()
    prew_d = nc.dram_tensor("prew", [128, DM], F32, kind="ExternalInput").ap()
    postw_d = nc.dram_tensor("postw", [128, DM], F32, kind="ExternalInput").ap()
    wo_d = nc.dram_tensor("wo", [128, NU, DM], F32, kind="ExternalInput").ap()
    yout = nc.dram_tensor("y", [NT * 128, DM], F32, kind="ExternalOutput").ap()
    mscr = nc.dram_tensor("mscr", [NU, 128, L], BF16).ap()
    mdbg = nc.dram_tensor("mdbg", [NU, 128, L], BF16, kind="ExternalOutput").ap() if cfg.get("DBG") else None

    with ExitStack() as ctx:
        ctx.enter_context(nc.Block())
        S = Sched(nc, ctx)
        pe, act, dve, pool = nc.tensor, nc.scalar, nc.vector, nc.gpsimd

        def sb(c, name, shape, dt=F32):
            return Buf(c.enter_context(nc.sbuf_tensor("sb_" + name, shape, dt)))

        def ps(c, name, shape, dt=F32):
            return Buf(c.enter_context(nc.psum_tensor("ps_" + name, shape, dt)))

        xnT = sb(ctx, "xnT", [128, max(KC, NU), L], BF16)
        ones_f = sb(ctx, "ones_f", [128, 128])
        zeros_f = sb(ctx, "zeros_f", [128, 128])
        ident_f = sb(ctx, "ident_f", [128, 128])
        ident_b = sb(ctx, "ident_b", [128, 128], BF16)
        ones_b = sb(ctx, "ones_b", [128, 128], BF16)
        mask_le = sb(ctx, "mask_le", [128, 128])
        bst = sb(ctx, "bst", [128, 128])
        mneg = sb(ctx, "mneg", [128, 128])
        mask_gt = sb(ctx, "mask_gt", [128, 128])
        negh = sb(ctx, "negh", [128, 512])
        convw = sb(ctx, "convw", [128, HG * 12])
        hp = sb(ctx, "hp", [128, NS])
        ncols = sb(ctx, "ncols", [128, 3])
        ncs = sb(ctx, "ncs", [128, 2])
        wsm = sb(ctx, "wsm", [128, KC, NS], BF16)
        sm = sb(ctx, "sm", [128, NTI, NS])
        beta = sb(ctx, "beta", [128, NTI, HG])
        gg = sb(ctx, "gg", [128, NTI, HG])
        nbeta = sb(ctx, "nbeta", [128, NTI, HG])
        bEl = sb(ctx, "bEl", [128, NTI, HG])
        eG = sb(ctx, "eG", [128, NTI, HG])
        neG = sb(ctx, "neG", [128, NTI, HG])
        eGl = sb(ctx, "eGl", [128, NTI, HG])
        dec = sb(ctx, "dec", [128, NTI, HG])
        logf = sb(ctx, "logf", [128, NTI, HF])
        cl = sb(ctx, "cl", [128, NTI, HF])
        rbb = sb(ctx, "rbb", [128, NTI, HF])
        cab = sb(ctx, "cab", [128, NTI, HF])
        afac = sb(ctx, "afac", [128, NTI, HF])
        biasm = sb(ctx, "biasm", [128, HF, NTI, NTI])

        def P_(e, fn, r=(), w=(), inc=True):
            return S.op(e, fn, reads=r, writes=w, inc=inc)

        P_("pool", lambda: pool.memset(ones_f[:], 1.0), w=[ones_f])
        P_("pool", lambda: pool.memset(zeros_f[:], 0.0), w=[zeros_f])
        P_("pool", lambda: pool.memset(ones_b[:], 1.0), w=[ones_b])
        P_("pool", lambda: pool.memset(negh[:], -0.5), w=[negh])
        P_("pool", lambda: pool.affine_select(out=ident_f[:], in_=ones_f[:], pattern=[[-1, 128]], compare_op=ALU.is_equal,
                                             fill=0.0, base=0, channel_multiplier=1), r=[ones_f], w=[ident_f])
        P_("pool", lambda: pool.tensor_copy(out=ident_b[:], in_=ident_f[:]), r=[ident_f], w=[ident_b])
        P_("pool", lambda: pool.affine_select(out=mask_le[:], in_=ones_f[:], pattern=[[1, 128]], compare_op=ALU.is_ge,
                                             fill=0.0, base=0, channel_multiplier=-1), r=[ones_f], w=[mask_le])
        P_("pool", lambda: pool.affine_select(out=bst[:], in_=ones_f[:], pattern=[[-1, 128]], compare_op=ALU.is_gt,
                                             fill=0.0, base=0, channel_multiplier=1), r=[ones_f], w=[bst])
        P_("pool", lambda: pool.affine_select(out=mneg[:], in_=zeros_f[:], pattern=[[1, 128]], compare_op=ALU.is_ge,
                                             fill=NEG, base=0, channel_multiplier=-1), r=[zeros_f], w=[mneg])
        P_("pool", lambda: pool.affine_select(out=mask_gt[:], in_=ones_f[:], pattern=[[1, 128]], compare_op=ALU.is_gt,
                                             fill=0.0, base=0, channel_multiplier=-1), r=[ones_f], w=[mask_gt])

        dpar = S.new_dma_sem("dpar")
        S.dma("sp", convw[:], convw_d, dpar, writes=[convw])
        S.dma("sp", hp[:], hp_d, dpar, writes=[hp])
        S.dma("sp", ncols[:], ncols_d, dpar, writes=[ncols])
        dwsm = S.new_dma_sem("dwsm")
        S.dma("pool", wsm[:], wsm_d, dwsm, writes=[wsm])
        for b in (convw, hp, ncols):
            b.d.w = (dpar["sem"], dpar["val"])
        P_("dve", lambda: dve.tensor_scalar(out=ncs[:, 0:1], in0=ncols[:, 0:1], scalar1=0.5, scalar2=None, op0=ALU.mult),
           r=[ncols], w=[ncs])
        P_("dve", lambda: dve.tensor_scalar(out=ncs[:, 1:2], in0=ncols[:, 1:2], scalar1=float(128 ** -0.5), scalar2=None,
                                            op0=ALU.mult), r=[ncols], w=[ncs])

        with ExitStack() as pa:
            prew = sb(pa, "prew", [128, DM])
            xts = [sb(pa, "xt%d" % i, [128, DM]) for i in range(2)]
            xns = [sb(pa, "xn%d" % i, [128, DM], BF16) for i in range(2)]
            junk = sb(pa, "junkA", [128, DM], BF16)
            ssA = sb(pa, "ssA", [128, NTI])
            rsA = sb(pa, "rsA", [128, NTI])
            ptr = [ps(pa, "ptrA%d" % i, [128, 8, 128], BF16) for i in range(2)]
            psm = ps(pa, "psmA", [128, 512])
            dxs = [S.new_dma_sem("dx%d" % i) for i in range(2)]
            S.dma("sp", prew[:], prew_d, S.new_dma_sem("dprew"), writes=[prew])
            P_("dve", lambda: dve.memset(sm[:], 0.0), w=[sm])
            P_("dve", lambda: dve.memset(psm[:], 0.0), w=[psm])
            nev = 0
            for i in range(NTI):
                P = tP(i)
                c0, c1 = tcol(i)
                xt, xn = xts[i % 2], xns[i % 2]
                src = meta if i == 0 else xin[(i - 1) * 128:i * 128, :]
                S.dma("sp", xt[0:P, :], src, dxs[i % 2], writes=[xt])
                P_("act", lambda: act.activation(out=junk[0:P, :], in_=xt[0:P, :], func=AF.Square,
                                                 accum_out=ssA[0:P, i:i + 1]), r=[xt], w=[junk, ssA])
                P_("dve", lambda: dve.tensor_scalar(out=rsA[0:P, i:i + 1], in0=ssA[0:P, i:i + 1], scalar1=1.0 / DM,
                                                    scalar2=EPS, op0=ALU.mult, op1=ALU.add), r=[ssA], w=[rsA])
                P_("pool", lambda: pool.tensor_tensor(out=rsA[0:P, i:i + 1], in0=rsA[0:P, i:i + 1], in1=negh[0:P, 0:1],
                                                      op=ALU.pow), r=[rsA, negh], w=[rsA])
                P_("dve", lambda: dve.scalar_tensor_tensor(out=xn[0:P, :], in0=xt[0:P, :], scalar=rsA[0:P, i:i + 1],
                                                           in1=prew[0:P, :], op0=ALU.mult, op1=ALU.mult),
                   r=[xt, rsA, prew], w=[xn])
                for k0 in range(0, KC, 8):
                    kn = min(8, KC - k0)
                    pt = ptr[nev % 2]
                    for k in range(kn):
                        P_("pe", lambda: pe.transpose(pt[:, k, 0:P], xn[0:P, (k0 + k) * 128:(k0 + k + 1) * 128],
                                                      ident_b[0:P, 0:P]), r=[xn, ident_b], w=[pt], inc=(k == kn - 1))
                    ev = "act" if nev % 2 == 0 else "dve"
                    if ev == "act":
                        P_("act", lambda: act.copy(out=xnT[:, k0:k0 + kn, c0:c1], in_=pt[:, 0:kn, 0:P]), r=[pt], w=[xnT])
                    else:
                        P_("dve", lambda: dve.tensor_copy(out=xnT[:, k0:k0 + kn, c0:c1], in_=pt[:, 0:kn, 0:P]), r=[pt], w=[xnT])
                    nev += 1
                for k in range(KC):
                    P_("pe", lambda: pe.matmul(psm[0:P, i * NS:(i + 1) * NS], xnT[:, k, c0:c1], wsm[:, k, :],
                                               start=(k == 0), stop=(k == KC - 1)), r=[xnT, wsm], w=[psm], inc=(k == KC - 1))
            P_("dve", lambda: dve.tensor_copy(out=sm[:].rearrange("p t s -> p (t s)"), in_=psm[:, 0:NTI * NS]), r=[psm], w=[sm])

            t1 = sb(pa, "t1s", [128, NTI, HM])
            t2 = sb(pa, "t2s", [128, NTI, HM])
            t3 = sb(pa, "t3s", [128, NTI, HM])
            nea = sb(pa, "nea", [128, HG])
            gc = sb(pa, "gc", [128, NTI, HG])
            glb = sb(pa, "glb", [128, NTI, HG])
            tot = sb(pa, "tot", [128, NTI, HF])
            pcs = ps(pa, "pcs", [128, 512])
            pcf = ps(pa, "pcf", [128, 512])

            def bc(ap, h):
                return ap.unsqueeze(1).to_broadcast([128, NTI, h])

            P_("act", lambda: act.activation(out=t1[:, :, 0:HG], in_=sm[:, :, 0:HG], func=AF.Tanh, scale=0.5), r=[sm], w=[t1])
            P_("dve", lambda: dve.tensor_scalar(out=beta[:], in0=t1[:, :, 0:HG], scalar1=0.5, scalar2=0.5, op0=ALU.mult,
                                                op1=ALU.add), r=[t1], w=[beta])
            P_("act", lambda: act.activation(out=nea[:], in_=hp[:, 0:HG], func=AF.Exp), r=[hp], w=[nea])
            P_("dve", lambda: dve.tensor_scalar(out=nea[:], in0=nea[:], scalar1=-1.0, scalar2=None, op0=ALU.mult), r=[nea], w=[nea])
            P_("dve", lambda: dve.tensor_tensor(out=t1[:, :, 0:HG], in0=sm[:, :, HG:2 * HG], in1=bc(hp[:, HG:2 * HG], HG),
                                                op=ALU.add), r=[sm, hp], w=[t1])
            P_("act", lambda: act.activation(out=t2[:, :, 0:HG], in_=t1[:, :, 0:HG], func=AF.Abs), r=[t1], w=[t2])
            P_("act", lambda: act.activation(out=t2[:, :, 0:HG], in_=t2[:, :, 0:HG], func=AF.Exp, scale=-1.0), r=[t2], w=[t2])
            P_("act", lambda: act.activation(out=t2[:, :, 0:HG], in_=t2[:, :, 0:HG], func=AF.Ln, bias=1.0), r=[t2], w=[t2])
            P_("dve", lambda: dve.scalar_tensor_tensor(out=t3[:, :, 0:HG], in0=t1[:, :, 0:HG], scalar=0.0, in1=t2[:, :, 0:HG],
                                                       op0=ALU.max, op1=ALU.add), r=[t1, t2], w=[t3])
            P_("dve", lambda: dve.tensor_tensor(out=gg[:], in0=t3[:, :, 0:HG], in1=bc(nea[:], HG), op=ALU.mult),
               r=[t3, nea], w=[gg])
            P_("dve", lambda: dve.tensor_tensor(out=t1[:, :, 0:HF], in0=sm[:, :, 2 * HG:NS], in1=bc(hp[:, 2 * HG:NS], HF),
                                                op=ALU.add), r=[sm, hp], w=[t1])
            P_("act", lambda: act.activation(out=t2[:, :, 0:HF], in_=t1[:, :, 0:HF], func=AF.Abs), r=[t1], w=[t2])
            P_("act", lambda: act.activation(out=t2[:, :, 0:HF], in_=t2[:, :, 0:HF], func=AF.Exp, scale=-1.0), r=[t2], w=[t2])
            P_("act", lambda: act.activation(out=t2[:, :, 0:HF], in_=t2[:, :, 0:HF], func=AF.Ln, bias=1.0), r=[t2], w=[t2])
            P_("dve", lambda: dve.tensor_scalar(out=t1[:, :, 0:HF], in0=t1[:, :, 0:HF], scalar1=-1.0, scalar2=None, op0=ALU.mult),
               r=[t1], w=[t1])
            P_("dve", lambda: dve.scalar_tensor_tensor(out=t3[:, :, 0:HF], in0=t1[:, :, 0:HF], scalar=0.0, in1=t2[:, :, 0:HF],
                                                       op0=ALU.max, op1=ALU.add), r=[t1, t2], w=[t3])
            P_("dve", lambda: dve.tensor_scalar(out=logf[:], in0=t3[:, :, 0:HF], scalar1=-1.0, scalar2=None, op0=ALU.mult),
               r=[t3], w=[logf])

            P_("dve", lambda: dve.memset(pcs[:], 0.0), w=[pcs])
            P_("dve", lambda: dve.memset(pcf[:], 0.0), w=[pcf])
            for i in range(NTI):
                P = tP(i)
                o = i * 2 * HG
                o2 = i * 2 * HF
                P_("pe", lambda: pe.matmul(pcs[0:P, o:o + HG], mask_le[0:P, 0:P], gg[0:P, i, :], start=True, stop=True),
                   r=[mask_le, gg], w=[pcs], inc=False)
                P_("pe", lambda: pe.matmul(pcs[:, o + HG:o + 2 * HG], ones_f[0:P, :], gg[0:P, i, :], start=True, stop=True),
                   r=[ones_f, gg], w=[pcs], inc=False)
                P_("pe", lambda: pe.matmul(pcf[0:P, o2:o2 + HF], mask_le[0:P, 0:P], logf[0:P, i, :],
                                           start=True, stop=True), r=[mask_le, logf], w=[pcf], inc=False)
                P_("pe", lambda: pe.matmul(pcf[:, o2 + HF:o2 + 2 * HF], ones_f[0:P, :], logf[0:P, i, :], start=True,
                                           stop=True), r=[ones_f, logf], w=[pcf], inc=True)
            pv = pcs[:, 0:NTI * 2 * HG].rearrange("p (t w) -> p t w", w=2 * HG)
            pv2 = pcf[:, 0:NTI * 2 * HF].rearrange("p (t w) -> p t w", w=2 * HF)
            P_("dve", lambda: dve.tensor_copy(out=gc[:], in_=pv[:, :, 0:HG]), r=[pcs], w=[gc])
            P_("dve", lambda: dve.tensor_copy(out=glb[:], in_=pv[:, :, HG:2 * HG]), r=[pcs], w=[glb])
            P_("dve", lambda: dve.tensor_copy(out=cl[:], in_=pv2[:, :, 0:HF]), r=[pcf], w=[cl])
            P_("dve", lambda: dve.tensor_copy(out=tot[:], in_=pv2[:, :, HF:2 * HF]), r=[pcf], w=[tot])
            P_("act", lambda: act.activation(out=eG[:], in_=gc[:], func=AF.Exp), r=[gc], w=[eG])
            P_("dve", lambda: dve.tensor_scalar(out=neG[:], in0=eG[:], scalar1=-1.0, scalar2=None, op0=ALU.mult), r=[eG], w=[neG])
            P_("dve", lambda: dve.tensor_tensor(out=gc[:], in0=glb[:], in1=gc[:], op=ALU.subtract), r=[glb, gc], w=[gc])
            P_("act", lambda: act.activation(out=eGl[:], in_=gc[:], func=AF.Exp), r=[gc], w=[eGl])
            P_("act", lambda: act.activation(out=dec[:], in_=glb[:], func=AF.Exp), r=[glb], w=[dec])
            P_("dve", lambda: dve.tensor_scalar(out=nbeta[:], in0=beta[:], scalar1=-1.0, scalar2=None, op0=ALU.mult), r=[beta], w=[nbeta])
            P_("dve", lambda: dve.tensor_tensor(out=bEl[:], in0=beta[:], in1=eGl[:], op=ALU.mult), r=[beta, eGl], w=[bEl])
            P_("dve", lambda: dve.memset(rbb[:], 0.0), w=[rbb])
            for i in range(1, NTI):
                P_("dve", lambda: dve.tensor_tensor(out=rbb[:, i, :], in0=rbb[:, i - 1, :], in1=tot[:, i - 1, :], op=ALU.add),
                   r=[rbb, tot], w=[rbb])
            P_("dve", lambda: dve.tensor_tensor(out=cab[:], in0=rbb[:], in1=cl[:], op=ALU.add), r=[rbb, cl], w=[cab])
            P_("act", lambda: act.activation(out=afac[:], in_=cl[:], func=AF.Exp), r=[cl], w=[afac])
            for h in range(HF):
                P_("dve", lambda: dve.tensor_tensor(out=biasm[:, h, :, :],
                                                    in0=rbb[:, :, h].unsqueeze(2).to_broadcast([128, NTI, NTI]),
                                                    in1=cab[:, :, h].unsqueeze(1).to_broadcast([128, NTI, NTI]),
                                                    op=ALU.subtract), r=[rbb, cab], w=[biasm])
            S.barrier()

        with ExitStack() as pb:
            wsl = [[sb(pb, "w%d_%d" % (s, b), [128, KC, 128], BF16) for b in range(4)] for s in range(2)]
            dws = [[S.new_dma_sem("dw%d_%d" % (s, b)) for b in range(4)] for s in range(2)]
            mts = [sb(pb, "mts%d" % i, [128, L], BF16) for i in range(2)]
            dms = [S.new_dma_sem("dm%d" % i) for i in range(2)]
            pbank = [ps(pb, "pbank%d" % i, [128, 512]) for i in range(2)]
            nproj = [0]

            def load_w(u):
                for b in range(4):
                    S.dma("pool", wsl[u % 2][b][:], wu[u, b], dws[u % 2][b], writes=[wsl[u % 2][b]])

            def project(u, b, g):
                c0, c1 = gcol(g)
                n = c1 - c0
                pbk = pbank[nproj[0] % 2]
                nproj[0] += 1
                w = wsl[u % 2][b]
                for k in range(KC):
                    P_("pe", lambda: pe.matmul(pbk[:, 0:n], w[:, k, :], xnT[:, k, c0:c1], start=(k == 0), stop=(k == KC - 1)),
                       r=[w, xnT], w=[pbk], inc=(k == KC - 1))
                return pbk, n

            load_w(0)

            with ExitStack() as pg:
                raw = [sb(pg, "raw%d" % b, [128, 3 + GW]) for b in range(3)]
                ycv = [sb(pg, "ycv%d" % b, [128, GW]) for b in range(3)]
                tnh = sb(pg, "tnh", [128, GW])
                sqs = [sb(pg, "sq%d" % i, [128, GW], BF16) for i in range(2)]
                rs8 = sb(pg, "rs8", [128, 2 * GT])
                dg = [sb(pg, "dg%d" % i, [128, 128]) for i in range(2)]
                qT = [sb(pg, "qT%d" % i, [128, GW], BF16) for i in range(2)]
                kT = [sb(pg, "kT%d" % i, [128, GW], BF16) for i in range(2)]
                vT = sb(pg, "vT", [128, GW], BF16)
                szT = [sb(pg, "szT%d" % i, [128, GW], BF16) for i in range(2)]
                ktok = [sb(pg, "ktok%d" % i, [128, GT, 128], BF16) for i in range(2)]
                vtok = [sb(pg, "vtok%d" % i, [128, GT, 128], BF16) for i in range(2)]
                NSL = 2 * GT
                Ysl = [sb(pg, "Y%d" % i, [128, 128], BF16) for i in range(NSL)]
                Asl = [sb(pg, "A%d" % i, [128, 128], BF16) for i in range(NSL)]
                Ag = sb(pg, "Ag", [128, 128])
                Dge = sb(pg, "Dge", [128, 128])
                Dgt = sb(pg, "Dgt", [128, 128])
                Nn = [sb(pg, "Nn%d" % i, [128, 2, 128], NDT) for i in range(2)]
                Pm = [sb(pg, "Pm%d" % i, [128, 128], NDT) for i in range(2)]
                Rb = sb(pg, "Rb", [128, 128], BF16)
                vnew = sb(pg, "vnew", [128, 128], BF16)
                vnd = sb(pg, "vnd", [128, 128], BF16)
                t1o = sb(pg, "t1o", [128, 128])
                oo = sb(pg, "oo", [128, 128])
                onb = sb(pg, "onb", [128, 128], BF16)
                junkg = sb(pg, "junkg", [128, 128], BF16)
                sso = sb(pg, "sso", [128, 2])
                Sf = sb(pg, "Sf", [128, 128])
                Sb = sb(pg, "Sb", [128, 128], BF16)
                pT = ps(pg, "pT", [128, 2 * GT, 128], BF16)
                pU = ps(pg, "pU", [128, 512])
                pV = ps(pg, "pV", [128, 4, 128])
                pW = ps(pg, "pW", [128, 3, 128])
                pC = [ps(pg, "pC0", [128, 4, 128])]
                pOT = ps(pg, "pOT", [128, 128], BF16)
                ident_n = ident_f

                def gdn_proj(h, g):
                    u = h
                    c0, c1 = gcol(g)
                    n = c1 - c0
                    par = g % 2
                    for b in range(3):
                        pbk, _ = project(u, b, g)
                        P_("act", lambda: act.copy(out=raw[b][:, 3:3 + n], in_=pbk[:, 0:n]), r=[pbk], w=[raw[b]])
                    pbk, _ = project(u, 3, g)
                    P_("act", lambda: act.activation(out=tnh[:, 0:n], in_=pbk[:, 0:n], func=AF.Tanh, scale=0.5), r=[pbk], w=[tnh])
                    P_("dve", lambda: dve.scalar_tensor_tensor(out=szT[par][:, 0:n], in0=tnh[:, 0:n], scalar=1.0, in1=pbk[:, 0:n],
                                                               op0=ALU.add, op1=ALU.mult), r=[tnh, pbk], w=[szT[par]])

                def gdn_post(h, g):
                    c0, c1 = gcol(g)
                    n = c1 - c0
                    par = g % 2
                    tiles = gtiles(g)
                    for b in range(3):
                        cw = lambda j: convw[:, h * 12 + b * 4 + j:h * 12 + b * 4 + j + 1]
                        y = ycv[b]
                        P_("dve", lambda: dve.tensor_scalar(out=y[:, 0:n], in0=raw[b][:, 3:3 + n], scalar1=cw(3), scalar2=None,
                                                            op0=ALU.mult), r=[raw[b], convw], w=[y])
                        for j in (2, 1, 0):
                            P_("dve", lambda: dve.scalar_tensor_tensor(out=y[:, 0:n], in0=raw[b][:, j:j + n], scalar=cw(j),
                                                                       in1=y[:, 0:n], op0=ALU.mult, op1=ALU.add),
                               r=[raw[b], convw, y], w=[y])
                        P_("pool", lambda: pool.tensor_copy(out=raw[b][:, 0:3], in_=raw[b][:, n:n + 3]), r=[raw[b]], w=[raw[b]])
                        P_("act", lambda: act.activation(out=tnh[:, 0:n], in_=y[:, 0:n], func=AF.Tanh, scale=0.5), r=[y], w=[tnh])
                        if b < 2:
                            P_("dve", lambda: dve.scalar_tensor_tensor(out=y[:, 0:n], in0=tnh[:, 0:n], scalar=1.0, in1=y[:, 0:n],
                                                                       op0=ALU.add, op1=ALU.mult), r=[tnh, y], w=[y])
                            P_("pool", lambda: pool.tensor_tensor(out=sqs[b][:, 0:n], in0=y[:, 0:n], in1=y[:, 0:n], op=ALU.mult),
                               r=[y], w=[sqs[b]])
                            for ti, i in enumerate(tiles):
                                P = tP(i)
                                lc = tcol(i)[0] - c0
                                P_("pe", lambda: pe.matmul(pV[0:P, 3, b * GT + ti:b * GT + ti + 1], sqs[b][:, lc:lc + P],
                                                           ones_b[:, 0:1], start=True, stop=True), r=[sqs[b], ones_b], w=[pV],
                                   inc=(ti == len(tiles) - 1))
                        else:
                            P_("dve", lambda: dve.scalar_tensor_tensor(out=vT[:, 0:n], in0=tnh[:, 0:n], scalar=1.0, in1=y[:, 0:n],
                                                                       op0=ALU.add, op1=ALU.mult), r=[tnh, y], w=[vT])
                    PG = tP(tiles[0])
                    ncol8 = 2 * GT
                    P_("dve", lambda: dve.tensor_scalar(out=rs8[0:PG, 0:ncol8], in0=pV[0:PG, 3, 0:ncol8], scalar1=1.0,
                                                        scalar2=4.0 * EPS, op0=ALU.mult, op1=ALU.add), r=[pV], w=[rs8])
                    P_("pool", lambda: pool.tensor_tensor(out=rs8[0:PG, 0:ncol8], in0=rs8[0:PG, 0:ncol8], in1=negh[0:PG, 0:ncol8],
                                                          op=ALU.pow), r=[rs8, negh], w=[rs8])
                    for b in range(2):
                        y = ycv[b]
                        for ti, i in enumerate(tiles):
                            P = tP(i)
                            lc = tcol(i)[0] - c0
                            dgb = dg[(b * GT + ti) % 2]
                            P_("dve", lambda: dve.tensor_scalar(out=dgb[0:P, 0:P], in0=ident_f[0:P, 0:P],
                                                                scalar1=rs8[0:P, b * GT + ti:b * GT + ti + 1], scalar2=None,
                                                                op0=ALU.mult), r=[ident_f, rs8], w=[dgb])
                            P_("pe", lambda: pe.matmul(pU[:, lc:lc + P], ones_f[0:P, :], dgb[0:P, 0:P], start=True, stop=True),
                               r=[ones_f, dgb], w=[pU])
                        dst = qT[par] if b == 0 else kT[par]
                        sc = float(128 ** -0.5) if b == 0 else 1.0
                        P_("dve", lambda: dve.scalar_tensor_tensor(out=dst[:, 0:n], in0=y[:, 0:n], scalar=sc, in1=pU[:, 0:n],
                                                                   op0=ALU.mult, op1=ALU.mult), r=[y, pU], w=[dst])
                    for ti, i in enumerate(tiles):
                        P = tP(i)
                        lc = tcol(i)[0] - c0
                        P_("pe", lambda: pe.transpose(pT[0:P, 2 * ti, :], kT[par][:, lc:lc + P], ident_b[:]),
                           r=[kT[par], ident_b], w=[pT], inc=False)
                        P_("pe", lambda: pe.transpose(pT[0:P, 2 * ti + 1, :], vT[:, lc:lc + P], ident_b[:]),
                           r=[vT, ident_b], w=[pT], inc=True)
                    for ti, i in enumerate(tiles):
                        P = tP(i)
                        P_("act", lambda: act.copy(out=ktok[par][0:P, ti, :], in_=pT[0:P, 2 * ti, :]), r=[pT], w=[ktok[par]])
                        P_("act", lambda: act.activation(out=vtok[par][0:P, ti, :], in_=pT[0:P, 2 * ti + 1, :], func=AF.Copy,
                                                         scale=0.5), r=[pT], w=[vtok[par]])
                    for ti, i in enumerate(tiles):
                        P = tP(i)
                        lc = tcol(i)[0] - c0
                        sl = par * GT + ti
                        P_("dve", lambda: dve.tensor_scalar(out=Ag[0:P, 0:P], in0=mask_le[0:P, 0:P], scalar1=gg[0:P, i, h:h + 1],
                                                            scalar2=None, op0=ALU.mult), r=[mask_le, gg], w=[Ag])
                        P_("pe", lambda: pe.matmul(pV[0:P, 0, 0:P], bst[0:P, 0:P], Ag[0:P, 0:P], start=True, stop=False),
                           r=[bst, Ag], w=[pV], inc=False)
                        P_("pe", lambda: pe.matmul(pV[0:P, 0, 0:P], ident_f[0:P, 0:P], mneg[0:P, 0:P], start=False, stop=True),
                           r=[ident_f, mneg], w=[pV], inc=False)
                        P_("pe", lambda: pe.matmul(pV[0:P, 1, 0:P], kT[par][:, lc:lc + P], kT[par][:, lc:lc + P], start=True,
                                                   stop=True), r=[kT[par]], w=[pV], inc=False)
                        P_("pe", lambda: pe.matmul(pV[0:P, 2, 0:P], kT[par][:, lc:lc + P], qT[par][:, lc:lc + P], start=True,
                                                   stop=True), r=[kT[par], qT[par]], w=[pV], inc=True)
                        P_("act", lambda: act.activation(out=Dge[0:P, 0:P], in_=pV[0:P, 0, 0:P], func=AF.Exp), r=[pV], w=[Dge])
                        N0 = Nn[0]
                        P_("dve", lambda: dve.scalar_tensor_tensor(out=Dgt[0:P, 0:P], in0=pV[0:P, 1, 0:P],
                                                                   scalar=nbeta[0:P, i, h:h + 1], in1=Dge[0:P, 0:P],
                                                                   op0=ALU.mult, op1=ALU.mult), r=[pV, nbeta, Dge], w=[Dgt])
                        P_("dve", lambda: dve.tensor_tensor(out=N0[0:P, 1, 0:P], in0=Dgt[0:P, 0:P], in1=mask_gt[0:P, 0:P],
                                                            op=ALU.mult), r=[Dgt, mask_gt], w=[N0])
                        P_("dve", lambda: dve.tensor_tensor(out=Asl[sl][0:P, 0:P], in0=pV[0:P, 2, 0:P], in1=Dge[0:P, 0:P],
                                                            op=ALU.mult), r=[pV, Dge], w=[Asl[sl]])
                        P_("pe", lambda: pe.transpose(pW[0:P, 2, 0:P], N0[0:P, 1, 0:P], ident_n[0:P, 0:P]), r=[N0, ident_n], w=[pW])
                        P_("act", lambda: act.copy(out=N0[0:P, 0, 0:P], in_=pW[0:P, 2, 0:P]), r=[pW], w=[N0])
                        P_("dve", lambda: dve.tensor_tensor(out=Pm[0][0:P, 0:P], in0=N0[0:P, 1, 0:P], in1=ident_n[0:P, 0:P],
                                                            op=ALU.add), r=[N0, ident_n], w=[Pm[0]])
                        nlev = 6 if P == 128 else 3
                        cur, pcur = 0, 0
                        for lv in range(1, nlev + 1):
                            last = lv == nlev
                            Nc, Nx = Nn[cur], Nn[1 - cur]
                            P_("pe", lambda: pe.matmul(pW[0:P, 0, 0:P], Nc[0:P, 1, 0:P], Nc[0:P, 0, 0:P], start=True, stop=True),
                               r=[Nc], w=[pW], inc=last)
                            if not last:
                                P_("pe", lambda: pe.matmul(pW[0:P, 1, 0:P], Nc[0:P, 0, 0:P], Nc[0:P, 1, 0:P], start=True,
                                                           stop=True), r=[Nc], w=[pW], inc=True)
                                P_("act", lambda: act.copy(out=Nx[0:P, :, 0:P], in_=pW[0:P, 0:2, 0:P]), r=[pW], w=[Nx])
                            else:
                                P_("act", lambda: act.copy(out=Nx[0:P, 0, 0:P], in_=pW[0:P, 0, 0:P]), r=[pW], w=[Nx])
                            Pc = Pm[pcur]
                            Pn = Ysl[sl] if last else Pm[1 - pcur]
                            P_("pe", lambda: pe.matmul(pW[0:P, 2, 0:P], Nx[0:P, 0, 0:P], Pc[0:P, 0:P], start=True, stop=True),
                               r=[Nx, Pc], w=[pW])
                            P_("dve", lambda: dve.tensor_tensor(out=Pn[0:P, 0:P], in0=pW[0:P, 2, 0:P], in1=Pc[0:P, 0:P],
                                                                op=ALU.add), r=[pW, Pc], w=[Pn])
                            cur, pcur = 1 - cur, 1 - pcur

                def gdn_chain(h, g):
                    c0, c1 = gcol(g)
                    par = g % 2
                    tiles = gtiles(g)
                    mt = mts[h % 2]
                    for ti, i in enumerate(tiles):
                        P = tP(i)
                        lc = tcol(i)[0] - c0
                        tc0, tc1 = tcol(i)
                        sl = par * GT + ti
                        pc = pC[0]
                        kTt = kT[par][:, lc:lc + P]
                        qTt = qT[par][:, lc:lc + P]
                        P_("pe", lambda: pe.matmul(pc[0:P, 0, :], kTt, Sb[:], start=True, stop=True), r=[kT[par], Sb], w=[pc],
                           inc=False)
                        P_("pe", lambda: pe.matmul(pc[0:P, 2, :], qTt, Sb[:], start=True, stop=True), r=[qT[par], Sb], w=[pc])
                        P_("dve", lambda: dve.scalar_tensor_tensor(out=Rb[0:P, :], in0=pc[0:P, 0, :], scalar=neG[0:P, i, h:h + 1],
                                                                   in1=vtok[par][0:P, ti, :], op0=ALU.mult, op1=ALU.add),
                           r=[pc, neG, vtok[par]], w=[Rb])
                        P_("pe", lambda: pe.matmul(pc[0:P, 1, :], Ysl[sl][0:P, 0:P], Rb[0:P, :], start=True, stop=True),
                           r=[Ysl[sl], Rb], w=[pc])
                        P_("act", lambda: act.activation(out=vnew[0:P, :], in_=pc[0:P, 1, :], func=AF.Copy,
                                                         scale=beta[0:P, i, h:h + 1]), r=[pc, beta], w=[vnew])
                        P_("act", lambda: act.activation(out=vnd[0:P, :], in_=pc[0:P, 1, :], func=AF.Copy,
                                                         scale=bEl[0:P, i, h:h + 1]), r=[pc, bEl], w=[vnd])
                        P_("act", lambda: act.activation(out=t1o[0:P, :], in_=pc[0:P, 2, :], func=AF.Copy,
                                                         scale=eG[0:P, i, h:h + 1]), r=[pc, eG], w=[t1o])
                        P_("pe", lambda: pe.matmul(pc[0:P, 3, :], Asl[sl][0:P, 0:P], vnew[0:P, :], start=True, stop=True),
                           r=[Asl[sl], vnew], w=[pc], inc=False)
                        P_("pe", lambda: pe.matmul(pc[:, 0, :], ktok[par][0:P, ti, :], vnd[0:P, :], start=True, stop=True),
                           r=[ktok[par], vnd], w=[pc])
                        P_("dve", lambda: dve.scalar_tensor_tensor(out=Sf[:], in0=Sf[:], scalar=dec[:, i, h:h + 1], in1=pc[:, 0, :],
                                                                   op0=ALU.mult, op1=ALU.add), r=[Sf, dec, pc], w=[Sf])
                        P_("act", lambda: act.copy(out=Sb[:], in_=Sf[:]), r=[Sf], w=[Sb])
                        P_("dve", lambda: dve.tensor_tensor(out=oo[0:P, :], in0=pc[0:P, 3, :], in1=t1o[0:P, :], op=ALU.add),
                           r=[pc, t1o], w=[oo])
                        P_("act", lambda: act.activation(out=junkg[0:P, :], in_=oo[0:P, :], func=AF.Square,
                                                         accum_out=sso[0:P, 0:1]), r=[oo], w=[junkg, sso])
                        P_("dve", lambda: dve.tensor_scalar(out=sso[0:P, 1:2], in0=sso[0:P, 0:1], scalar1=1.0 / 128, scalar2=EPS,
                                                            op0=ALU.mult, op1=ALU.add), r=[sso], w=[sso])
                        P_("pool", lambda: pool.tensor_tensor(out=sso[0:P, 1:2], in0=sso[0:P, 1:2], in1=negh[0:P, 0:1], op=ALU.pow),
                           r=[sso, negh], w=[sso])
                        P_("act", lambda: act.activation(out=onb[0:P, :], in_=oo[0:P, :], func=AF.Copy, scale=sso[0:P, 1:2]),
                           r=[oo, sso], w=[onb])
                        P_("pe", lambda: pe.transpose(pOT[:, 0:P], onb[0:P, :], ident_b[0:P, 0:P]), r=[onb, ident_b], w=[pOT])
                        P_("dve", lambda: dve.scalar_tensor_tensor(out=mt[:, tc0:tc1], in0=pOT[:, 0:P], scalar=ncs[:, 0:1],
                                                                   in1=szT[par][:, lc:lc + P], op0=ALU.mult, op1=ALU.mult),
                           r=[pOT, ncs, szT[par]], w=[mt])

                P_("dve", lambda: dve.memset(pV[:, 3, :], 1.0), w=[pV])
                for h in range(HG):
                    u = h
                    if u + 1 < NU:
                        load_w(u + 1)
                    P_("dve", lambda: dve.memset(Sf[:], 0.0), w=[Sf])
                    P_("dve", lambda: dve.memset(Sb[:], 0.0), w=[Sb])
                    for b in range(3):
                        P_("dve", lambda: dve.memset(raw[b][:, 0:3], 0.0), w=[raw[b]])
                    gdn_proj(h, 0)
                    for g in range(NG):
                        gdn_post(h, g)
                        if g + 1 < NG:
                            gdn_proj(h, g + 1)
                        gdn_chain(h, g)
                    S.dma("sp", mscr[u], mts[h % 2][:], dms[h % 2], reads=[mts[h % 2]])
                    if mdbg is not None:
                        S.dma("sp", mdbg[u], mts[h % 2][:], dms[h % 2], reads=[mts[h % 2]])
                S.barrier()

            with ExitStack() as pf:
                qTf = [sb(pf, "qTf%d" % i, [128, GW], BF16) for i in range(2)]
                kTf = sb(pf, "kTf", [128, L], BF16)
                vTf = sb(pf, "vTf", [128, GW], BF16)
                vext = sb(pf, "vext", [128, NTI, 132], BF16)
                sgT = [sb(pf, "sgT%d" % i, [128, GW], BF16) for i in range(2)]
                rawq = sb(pf, "rawq", [128, GW])
                rawk = sb(pf, "rawk", [128, GW])
                sqf = sb(pf, "sqf", [128, GW], BF16)
                rs8f = sb(pf, "rs8f", [128, 2 * GT])
                dgf = [sb(pf, "dgf%d" % i, [128, 128]) for i in range(2)]
                tnf = sb(pf, "tnf", [128, GW])
                Af = sb(pf, "Af", [128, 128])
                pTs = [sb(pf, "pTs%d" % i, [128, 128], BF16) for i in range(4)]
                tof = sb(pf, "tof", [128, 132])
                of = sb(pf, "of", [128, 132])
                rinv = sb(pf, "rinv", [128, 1])
                onf = sb(pf, "onf", [128, 128], BF16)
                pUf = ps(pf, "pUf", [128, 512])
                pTf = ps(pf, "pTf", [128, GT + 1, 128], BF16)
                pSSf = ps(pf, "pSSf", [128, 2 * GT])
                pS = [ps(pf, "pS%d" % i, [128, 128]) for i in range(2)]
                pO = ps(pf, "pO", [128, 2, 132])

                def fox_proj_post(h, g):
                    u = HG + h
                    c0, c1 = gcol(g)
                    n = c1 - c0
                    par = g % 2
                    tiles = gtiles(g)
                    for b in range(2):
                        pbk, _ = project(u, b, g)
                        rw = rawq if b == 0 else rawk
                        P_("act", lambda: act.copy(out=rw[:, 0:n], in_=pbk[:, 0:n]), r=[pbk], w=[rw])
                        P_("pool", lambda: pool.tensor_tensor(out=sqf[:, 0:n], in0=rw[:, 0:n], in1=rw[:, 0:n], op=ALU.mult),
                           r=[rw], w=[sqf])
                        for ti, i in enumerate(tiles):
                            P = tP(i)
                            lc = tcol(i)[0] - c0
                            P_("pe", lambda: pe.matmul(pSSf[0:P, b * GT + ti:b * GT + ti + 1], sqf[:, lc:lc + P], ones_b[:, 0:1],
                                                       start=True, stop=True), r=[sqf, ones_b], w=[pSSf],
                               inc=(ti == len(tiles) - 1))
                    PG = tP(tiles[0])
                    ncol8 = 2 * GT
                    P_("dve", lambda: dve.tensor_scalar(out=rs8f[0:PG, 0:ncol8], in0=pSSf[0:PG, 0:ncol8], scalar1=1.0 / 128,
                                                        scalar2=EPS, op0=ALU.mult, op1=ALU.add), r=[pSSf], w=[rs8f])
                    P_("pool", lambda: pool.tensor_tensor(out=rs8f[0:PG, 0:ncol8], in0=rs8f[0:PG, 0:ncol8],
                                                          in1=negh[0:PG, 0:ncol8], op=ALU.pow), r=[rs8f, negh], w=[rs8f])
                    for b in range(2):
                        rw = rawq if b == 0 else rawk
                        for ti, i in enumerate(tiles):
                            P = tP(i)
                            lc = tcol(i)[0] - c0
                            dgb = dgf[(b * GT + ti) % 2]
                            P_("dve", lambda: dve.tensor_scalar(out=dgb[0:P, 0:P], in0=ident_f[0:P, 0:P],
                                                                scalar1=rs8f[0:P, b * GT + ti:b * GT + ti + 1], scalar2=None,
                                                                op0=ALU.mult), r=[ident_f, rs8f], w=[dgb])
                            P_("pe", lambda: pe.matmul(pUf[:, lc:lc + P], ones_f[0:P, :], dgb[0:P, 0:P], start=True, stop=True),
                               r=[ones_f, dgb], w=[pUf])
                        if b == 0:
                            P_("dve", lambda: dve.scalar_tensor_tensor(out=qTf[par][:, 0:n], in0=rw[:, 0:n], scalar=ncs[:, 1:2],
                                                                       in1=pUf[:, 0:n], op0=ALU.mult, op1=ALU.mult),
                               r=[rw, ncs, pUf], w=[qTf[par]])
                        else:
                            P_("dve", lambda: dve.scalar_tensor_tensor(out=kTf[:, c0:c1], in0=rw[:, 0:n], scalar=ncols[:, 2:3],
                                                                       in1=pUf[:, 0:n], op0=ALU.mult, op1=ALU.mult),
                               r=[rw, ncols, pUf], w=[kTf])
                    pbk, _ = project(u, 2, g)
                    P_("act", lambda: act.copy(out=vTf[:, 0:n], in_=pbk[:, 0:n]), r=[pbk], w=[vTf])
                    pbk, _ = project(u, 3, g)
                    P_("act", lambda: act.activation(out=tnf[:, 0:n], in_=pbk[:, 0:n], func=AF.Tanh, scale=0.5), r=[pbk], w=[tnf])
                    P_("dve", lambda: dve.scalar_tensor_tensor(out=sgT[par][:, 0:n], in0=tnf[:, 0:n], scalar=1.0, in1=pbk[:, 0:n],
                                                               op0=ALU.add, op1=ALU.mult), r=[tnf, pbk], w=[sgT[par]])
                    for ti, i in enumerate(tiles):
                        P = tP(i)
                        lc = tcol(i)[0] - c0
                        P_("pe", lambda: pe.transpose(pTf[0:P, ti, :], vTf[:, lc:lc + P], ident_b[:]), r=[vTf, ident_b], w=[pTf],
                           inc=(ti == len(tiles) - 1))
                    for ti, i in enumerate(tiles):
                        P = tP(i)
                        P_("act", lambda: act.copy(out=vext[0:P, i, 0:128], in_=pTf[0:P, ti, :]), r=[pTf], w=[vext])

                def fox_attn(h, g):
                    c0, c1 = gcol(g)
                    par = g % 2
                    tiles = gtiles(g)
                    mt = mts[(HG + h) % 2]
                    nS = [0]
                    for ti, i in enumerate(tiles):
                        P = tP(i)
                        lc = tcol(i)[0] - c0
                        tc0, tc1 = tcol(i)
                        qt = qTf[par][:, lc:lc + P]
                        for j in range(i):
                            Pj = tP(j)
                            jc0, jc1 = tcol(j)
                            psj = pS[nS[0] % 2]
                            pts = pTs[nS[0] % 4]
                            nS[0] += 1
                            P_("pe", lambda: pe.matmul(psj[0:Pj, 0:P], kTf[:, jc0:jc1], qt, start=True, stop=True),
                               r=[kTf, qTf[par]], w=[psj])
                            P_("act", lambda: act.activation(out=pts[0:Pj, 0:P], in_=psj[0:Pj, 0:P], func=AF.Exp,
                                                             bias=biasm[0:Pj, h, i, j:j + 1]), r=[psj, biasm], w=[pts])
                            P_("pe", lambda: pe.matmul(pO[0:P, 0, 0:129], pts[0:Pj, 0:P], vext[0:Pj, j, 0:129], start=(j == 0),
                                                       stop=(j == i - 1)), r=[pts, vext], w=[pO], inc=(j == i - 1))
                        P_("dve", lambda: dve.tensor_scalar(out=Af[0:P, 0:P], in0=mask_le[0:P, 0:P], scalar1=logf[0:P, i, h:h + 1],
                                                            scalar2=None, op0=ALU.mult), r=[mask_le, logf], w=[Af])
                        psj = pS[nS[0] % 2]
                        pts = pTs[nS[0] % 4]
                        nS[0] += 1
                        P_("pe", lambda: pe.matmul(psj[0:P, 0:P], kTf[:, tc0:tc1], qt, start=True, stop=False),
                           r=[kTf, qTf[par]], w=[psj], inc=False)
                        P_("pe", lambda: pe.matmul(psj[0:P, 0:P], bst[0:P, 0:P], Af[0:P, 0:P], start=False, stop=False),
                           r=[bst, Af], w=[psj], inc=False)
                        P_("pe", lambda: pe.matmul(psj[0:P, 0:P], ident_f[0:P, 0:P], mneg[0:P, 0:P], start=False, stop=True),
                           r=[ident_f, mneg], w=[psj])
                        P_("act", lambda: act.activation(out=pts[0:P, 0:P], in_=psj[0:P, 0:P], func=AF.Exp), r=[psj], w=[pts])
                        P_("pe", lambda: pe.matmul(pO[0:P, 1, 0:129], pts[0:P, 0:P], vext[0:P, i, 0:129], start=True, stop=True),
                           r=[pts, vext], w=[pO])
                        if i > 0:
                            P_("act", lambda: act.activation(out=tof[0:P, 0:129], in_=pO[0:P, 0, 0:129], func=AF.Copy,
                                                             scale=afac[0:P, i, h:h + 1]), r=[pO, afac], w=[tof])
                            P_("dve", lambda: dve.tensor_tensor(out=of[0:P, 0:129], in0=pO[0:P, 1, 0:129], in1=tof[0:P, 0:129],
                                                                op=ALU.add), r=[pO, tof], w=[of])
                        else:
                            P_("dve", lambda: dve.tensor_copy(out=of[0:P, 0:129], in_=pO[0:P, 1, 0:129]), r=[pO], w=[of])
                        P_("dve", lambda: dve.reciprocal(out=rinv[0:P, :], in_=of[0:P, 128:129]), r=[of], w=[rinv])
                        P_("act", lambda: act.activation(out=onf[0:P, :], in_=of[0:P, 0:128], func=AF.Copy, scale=rinv[0:P, 0:1]),
                           r=[of, rinv], w=[onf])
                        P_("pe", lambda: pe.transpose(pTf[:, GT, 0:P], onf[0:P, :], ident_b[0:P, 0:P]), r=[onf, ident_b], w=[pTf])
                        P_("dve", lambda: dve.scalar_tensor_tensor(out=mt[:, tc0:tc1], in0=pTf[:, GT, 0:P], scalar=0.5,
                                                                   in1=sgT[par][:, lc:lc + P], op0=ALU.mult, op1=ALU.mult),
                           r=[pTf, sgT[par]], w=[mt])

                P_("pool", lambda: pool.memset(vext[:], 1.0), w=[vext])
                P_("dve", lambda: dve.memset(pSSf[:], 1.0), w=[pSSf])
                for h in range(HF):
                    u = HG + h
                    if u + 1 < NU:
                        load_w(u + 1)
                    fox_proj_post(h, 0)
                    for g in range(NG):
                        if g + 1 < NG:
                            fox_proj_post(h, g + 1)
                        fox_attn(h, g)
                    S.dma("sp", mscr[u], mts[u % 2][:], dms[u % 2], reads=[mts[u % 2]])
                    if mdbg is not None:
                        S.dma("sp", mdbg[u], mts[u % 2][:], dms[u % 2], reads=[mts[u % 2]])
                S.barrier()

        with ExitStack() as pc_:
            wo = sb(pc_, "wo", [128, NU, DM], BF16)
            postw = sb(pc_, "postw", [128, DM])
            ots = [sb(pc_, "ot%d" % i, [128, DM]) for i in range(2)]
            xrs = [sb(pc_, "xr%d" % i, [128, DM]) for i in range(2)]
            junkc = sb(pc_, "junkc", [128, 512], BF16)
            ss4 = sb(pc_, "ss4", [128, 8])
            rsc = sb(pc_, "rsc", [128, 2])
            pob = [ps(pc_, "pob%d" % i, [128, 512]) for i in range(2)]
            dwo = S.new_dma_sem("dwo")
            dmt = S.new_dma_sem("dmt")
            dxr = [S.new_dma_sem("dxr%d" % i) for i in range(2)]
            dst = [S.new_dma_sem("dst%d" % i) for i in range(2)]
            mT = xnT
            for u in range(NU):
                S.dma("sp", mT[:, u, :], mscr[u], dmt, writes=[mT])
            mT.d.w = (dmt["sem"], dmt["val"])
            S.dma("sp", postw[:], postw_d, S.new_dma_sem("dpostw"), writes=[postw])
            NQ = max(1, DM // 512)
            for q in range(NU):
                S.dma("pool", wo[:, q, :], wo_d[:, q, :], dwo, writes=[wo])
            wo.d.w = (dwo["sem"], dwo["val"])
            ncg = (DM + 511) // 512
            nmm = 0
            for i in range(1, NTI):
                tc0, tc1 = tcol(i)
                ot, xr = ots[i % 2], xrs[i % 2]
                S.dma("sp", xr[:], xin[(i - 1) * 128:i * 128, :], dxr[i % 2], writes=[xr])
                for cgi in range(ncg):
                    n0 = cgi * 512
                    nn = min(512, DM - n0)
                    pb_ = pob[nmm % 2]
                    nmm += 1
                    for k in range(NU):
                        P_("pe", lambda: pe.matmul(pb_[:, 0:nn], mT[:, k, tc0:tc1], wo[:, k, n0:n0 + nn], start=(k == 0),
                                                   stop=(k == NU - 1)), r=[mT, wo], w=[pb_], inc=(k == NU - 1))
                    P_("act", lambda: act.copy(out=ot[:, n0:n0 + nn], in_=pb_[:, 0:nn]), r=[pb_], w=[ot])
                    P_("act", lambda: act.activation(out=junkc[:, 0:nn], in_=pb_[:, 0:nn], func=AF.Square,
                                                     accum_out=ss4[:, cgi:cgi + 1]), r=[pb_], w=[junkc, ss4])
                P_("dve", lambda: dve.reduce_sum(out=rsc[:, 0:1], in_=ss4[:, 0:ncg], axis=mybir.AxisListType.X), r=[ss4], w=[rsc])
                P_("dve", lambda: dve.tensor_scalar(out=rsc[:, 1:2], in0=rsc[:, 0:1], scalar1=1.0 / DM, scalar2=EPS, op0=ALU.mult,
                                                    op1=ALU.add), r=[rsc], w=[rsc])
                P_("pool", lambda: pool.tensor_tensor(out=rsc[:, 1:2], in0=rsc[:, 1:2], in1=negh[:, 0:1], op=ALU.pow),
                   r=[rsc, negh], w=[rsc])
                P_("dve", lambda: dve.scalar_tensor_tensor(out=ot[:], in0=ot[:], scalar=rsc[:, 1:2], in1=postw[:], op0=ALU.mult,
                                                           op1=ALU.mult), r=[ot, rsc, postw], w=[ot])
                P_("pool", lambda: pool.tensor_tensor(out=ot[:], in0=ot[:], in1=xr[:], op=ALU.add), r=[ot, xr], w=[ot])
                S.dma("sp", yout[(i - 1) * 128:i * 128, :], ot[:], dst[i % 2], reads=[ot])
            for d in dst:
                if d["val"] > 0:
                    S.eng["sp"].wait_ge(d["sem"], d["val"])
            S.barrier()
    return nc


def prep_inputs(cfg, x, meta_tokens, pre_norm_w, w_in, conv_w, a_log, dt_bias, gdn_norm_w, fox_q_norm_w,
                fox_k_norm_w, fox_f_bias, w_out, post_norm_w):
    DM, NT, HG, HF = cfg["DM"], cfg["NT"], cfg["HG"], cfg["HF"]
    KC = DM // 128
    NU = HG + HF
    GWd, FWd = HG * 128, HF * 128
    w = np.asarray(w_in[0], np.float32)
    g_off = [0, GWd, 2 * GWd, 3 * GWd]
    sm_off = 4 * GWd
    f_base = 4 * GWd + 2 * HG
    f_off = [f_base, f_base + FWd, f_base + 2 * FWd, f_base + 3 * FWd]
    ff_off = f_base + 4 * FWd
    wu = np.empty((NU, 4, 128, KC, 128), np.float32)
    for u in range(NU):
        for b in range(4):
            c = (g_off[b] + u * 128) if u < HG else (f_off[b] + (u - HG) * 128)
            wu[u, b] = w[:, c:c + 128].reshape(KC, 128, 128).transpose(1, 0, 2)
    cols = list(range(sm_off, sm_off + 2 * HG)) + list(range(ff_off, ff_off + HF))
    wsm = np.ascontiguousarray(w[:, cols].reshape(KC, 128, len(cols)).transpose(1, 0, 2))
    cw = np.asarray(conv_w[0], np.float32)
    convw = np.empty((128, HG, 3, 4), np.float32)
    for h in range(HG):
        for b in range(3):
            convw[:, h, b, :] = cw[b * GWd + h * 128:b * GWd + (h + 1) * 128, :]
    convw = convw.reshape(128, HG * 12)
    hp = np.concatenate([np.asarray(a_log[0]), np.asarray(dt_bias[0]), np.asarray(fox_f_bias[0])]).astype(np.float32)
    hp = np.ascontiguousarray(np.broadcast_to(hp[None, :], (128, hp.shape[0])))
    ncols = np.stack([np.asarray(gdn_norm_w[0]), np.asarray(fox_q_norm_w[0]), np.asarray(fox_k_norm_w[0])], axis=1).astype(np.float32)
    prew = np.ascontiguousarray(np.broadcast_to(np.asarray(pre_norm_w[0], np.float32)[None, :], (128, DM)))
    postw = np.ascontiguousarray(np.broadcast_to(np.asarray(post_norm_w[0], np.float32)[None, :], (128, DM)))
    wo = np.ascontiguousarray(np.asarray(w_out[0], np.float32).reshape(NU, 128, DM).transpose(1, 0, 2))
    shared = dict(meta=np.ascontiguousarray(np.asarray(meta_tokens, np.float32)), wu=wu, wsm=wsm, convw=convw, hp=hp,
                  ncols=np.ascontiguousarray(ncols), prew=prew, postw=postw, wo=wo)
    x = np.asarray(x, np.float32)
    return [dict(shared, xin=np.ascontiguousarray(x[b])) for b in range(x.shape[0])]


def kernel(**inputs):
    cfg = FULL_CFG
    in_maps = prep_inputs(cfg, **inputs)
    nc = build(cfg)
    res = run_bass_kernel_spmd(nc, in_maps, core_ids=list(range(len(in_maps))))
    return np.stack([np.asarray(r["y"]) for r in res.results], axis=0).astype(np.float32)
```

```python
import numpy as np
from contextlib import ExitStack
import concourse.bass as bass
import concourse.mybir as mybir
from concourse.bass_utils import run_bass_kernel_spmd

F32 = mybir.dt.float32
BF16 = mybir.dt.bfloat16
AF = mybir.ActivationFunctionType
ALU = mybir.AluOpType
EPS = 1e-6
NEG = -30000.0

FULL_CFG = dict(DM=2048, NT=16, GT=4, HG=8, HF=8)


class Dep:
    __slots__ = ("w", "rs")

    def __init__(self):
        self.w = None
        self.rs = []


class Buf:
    def __init__(self, t):
        self.t = t
        self.d = Dep()

    def __getitem__(self, k):
        return self.t[k]


class Sched:
    def __init__(self, nc, ctx):
        self.nc = nc
        self.ctx = ctx
        self.eng = {"pe": nc.tensor, "act": nc.scalar, "dve": nc.vector, "pool": nc.gpsimd, "sp": nc.sync}
        self.sem = {k: ctx.enter_context(nc.semaphore("s_" + k)) for k in self.eng}
        self.cnt = {k: 0 for k in self.eng}
        self.seen = {k: {} for k in self.eng}
        self.dsems = []

    def new_dma_sem(self, name):
        d = {"sem": self.ctx.enter_context(self.nc.semaphore(name)), "val": 0}
        self.dsems.append(d)
        return d

    def _wait(self, e, tok):
        sem, val = tok
        if e == "pe" and sem.name == "s_pe":
            return
        if self.seen[e].get(sem.name, 0) >= val:
            return
        self.eng[e].wait_ge(sem, val)
        self.seen[e][sem.name] = val

    def _waits(self, e, reads, writes):
        best = {}
        for b in reads:
            if b.d.w is not None:
                t = b.d.w
                if t[0].name not in best or best[t[0].name][1] < t[1]:
                    best[t[0].name] = t
        for b in writes:
            for t in ([b.d.w] if b.d.w is not None else []) + b.d.rs:
                if t[0].name not in best or best[t[0].name][1] < t[1]:
                    best[t[0].name] = t
        for t in best.values():
            self._wait(e, t)

    def op(self, e, fn, reads=(), writes=(), inc=True):
        self._waits(e, reads, writes)
        ins = fn()
        tok = (self.sem[e], self.cnt[e] + 1)
        if inc:
            ins.then_inc(self.sem[e], 1)
            self.cnt[e] += 1
        for b in reads:
            b.d.rs.append(tok)
            if len(b.d.rs) > 64:
                b.d.rs = self._compact(b.d.rs)
        for b in writes:
            b.d.w = tok
            b.d.rs = []
        return ins

    @staticmethod
    def _compact(rs):
        best = {}
        for t in rs:
            if t[0].name not in best or best[t[0].name][1] < t[1]:
                best[t[0].name] = t
        return list(best.values())

    def dma(self, e, out, in_, dsem, reads=(), writes=()):
        self._waits(e, reads, writes)
        ins = self.eng[e].dma_start(out=out, in_=in_)
        ins.then_inc(dsem["sem"], 16)
        dsem["val"] += 16
        tok = (dsem["sem"], dsem["val"])
        for b in reads:
            b.d.rs.append(tok)
        for b in writes:
            b.d.w = tok
            b.d.rs = []
        return ins

    def barrier(self):
        toks = [(self.sem[k], self.cnt[k]) for k in self.eng if self.cnt[k] > 0]
        toks += [(d["sem"], d["val"]) for d in self.dsems if d["val"] > 0]
        for e in self.eng:
            for t in toks:
                if t[0].name == "s_" + e:
                    continue
                self._wait(e, t)
        for e in ("act", "dve", "pool"):
            if self.cnt[e] > 0 and self.seen[e].get("s_" + e, 0) < self.cnt[e]:
                self.eng[e].wait_ge(self.sem[e], self.cnt[e])
                self.seen[e]["s_" + e] = self.cnt[e]


def build(cfg):
    DM, NT, GT, HG, HF = cfg["DM"], cfg["NT"], cfg["GT"], cfg["HG"], cfg["HF"]
    KC = DM // 128
    NU = HG + HF
    L = 16 + NT * 128
    NTI = NT + 1
    NG = NT // GT + 1
    NS = 2 * HG + HF
    GW = GT * 128
    HM = max(HG, HF)
    NDT = F32

    def tcol(i):
        return (0, 16) if i == 0 else (16 + (i - 1) * 128, 16 + i * 128)

    def tP(i):
        return 16 if i == 0 else 128

    def gtiles(g):
        return [0] if g == 0 else list(range((g - 1) * GT + 1, g * GT + 1))

    def gcol(g):
        ts = gtiles(g)
        return tcol(ts[0])[0], tcol(ts[-1])[1]

    nc = bass.Bass("TRN2", target_bir_lowering=False)
    xin = nc.dram_tensor("xin", [NT * 128, DM], F32, kind="ExternalInput").ap()
    meta = nc.dram_tensor("meta", [16, DM], F32, kind="ExternalInput").ap()
    wu = nc.dram_tensor("wu", [NU, 4, 128, KC, 128], F32, kind="ExternalInput").ap()
    wsm_d = nc.dram_tensor("wsm", [128, KC, NS], F32, kind="ExternalInput").ap()
    convw_d = nc.dram_tensor("convw", [128, HG * 12], F32, kind="ExternalInput").ap()
    hp_d = nc.dram_tensor("hp", [128, NS], F32, kind="ExternalInput").ap()
    ncols_d = nc.dram_tensor("ncols", [128, 3], F32, kind="ExternalInput").ap()
    prew_d = nc.dram_tensor("prew", [128, DM], F32, kind="ExternalInput").ap()
    postw_d = nc.dram_tensor("postw", [128, DM], F32, kind="ExternalInput").ap()
    wo_d = nc.dram_tensor("wo", [128, NU, DM], F32, kind="ExternalInput").ap()
    yout = nc.dram_tensor("y", [NT * 128, DM], F32, kind="ExternalOutput").ap()
    mscr = nc.dram_tensor("mscr", [NU, 128, L], BF16).ap()
    mdbg = nc.dram_tensor("mdbg", [NU, 128, L], BF16, kind="ExternalOutput").ap() if cfg.get("DBG") else None

    with ExitStack() as ctx:
        ctx.enter_context(nc.Block())
        S = Sched(nc, ctx)
        pe, act, dve, pool = nc.tensor, nc.scalar, nc.vector, nc.gpsimd

        def sb(c, name, shape, dt=F32):
            return Buf(c.enter_context(nc.sbuf_tensor("sb_" + name, shape, dt)))

        def ps(c, name, shape, dt=F32):
            return Buf(c.enter_context(nc.psum_tensor("ps_" + name, shape, dt)))

        xnT = sb(ctx, "xnT", [128, max(KC, NU), L], BF16)
        ones_f = sb(ctx, "ones_f", [128, 128])
        zeros_f = sb(ctx, "zeros_f", [128, 128])
        ident_f = sb(ctx, "ident_f", [128, 128])
        ident_b = sb(ctx, "ident_b", [128, 128], BF16)
        ones_b = sb(ctx, "ones_b", [128, 128], BF16)
        mask_le = sb(ctx, "mask_le", [128, 128])
        bst = sb(ctx, "bst", [128, 128])
        mneg = sb(ctx, "mneg", [128, 128])
        mask_gt = sb(ctx, "mask_gt", [128, 128])
        negh = sb(ctx, "negh", [128, 512])
        convw = sb(ctx, "convw", [128, HG * 12])
        hp = sb(ctx, "hp", [128, NS])
        ncols = sb(ctx, "ncols", [128, 3])
        ncs = sb(ctx, "ncs", [128, 2])
        wsm = sb(ctx, "wsm", [128, KC, NS], BF16)
        sm = sb(ctx, "sm", [128, NTI, NS])
        beta = sb(ctx, "beta", [128, NTI, HG])
        gg = sb(ctx, "gg", [128, NTI, HG])
        nbeta = sb(ctx, "nbeta", [128, NTI, HG])
        bEl = sb(ctx, "bEl", [128, NTI, HG])
        eG = sb(ctx, "eG", [128, NTI, HG])
        neG = sb(ctx, "neG", [128, NTI, HG])
        eGl = sb(ctx, "eGl", [128, NTI, HG])
        dec = sb(ctx, "dec", [128, NTI, HG])
        logf = sb(ctx, "logf", [128, NTI, HF])
        cl = sb(ctx, "cl", [128, NTI, HF])
        rbb = sb(ctx, "rbb", [128, NTI, HF])
        cab = sb(ctx, "cab", [128, NTI, HF])
        afac = sb(ctx, "afac", [128, NTI, HF])
        biasm = sb(ctx, "biasm", [128, HF, NTI, NTI])

        def P_(e, fn, r=(), w=(), inc=True):
            return S.op(e, fn, reads=r, writes=w, inc=inc)

        P_("pool", lambda: pool.memset(ones_f[:], 1.0), w=[ones_f])
        P_("pool", lambda: pool.memset(zeros_f[:], 0.0), w=[zeros_f])
        P_("pool", lambda: pool.memset(ones_b[:], 1.0), w=[ones_b])
        P_("pool", lambda: pool.memset(negh[:], -0.5), w=[negh])
        P_("pool", lambda: pool.affine_select(out=ident_f[:], in_=ones_f[:], pattern=[[-1, 128]], compare_op=ALU.is_equal,
                                             fill=0.0, base=0, channel_multiplier=1), r=[ones_f], w=[ident_f])
        P_("pool", lambda: pool.tensor_copy(out=ident_b[:], in_=ident_f[:]), r=[ident_f], w=[ident_b])
        P_("pool", lambda: pool.affine_select(out=mask_le[:], in_=ones_f[:], pattern=[[1, 128]], compare_op=ALU.is_ge,
                                             fill=0.0, base=0, channel_multiplier=-1), r=[ones_f], w=[mask_le])
        P_("pool", lambda: pool.affine_select(out=bst[:], in_=ones_f[:], pattern=[[-1, 128]], compare_op=ALU.is_gt,
                                             fill=0.0, base=0, channel_multiplier=1), r=[ones_f], w=[bst])
        P_("pool", lambda: pool.affine_select(out=mneg[:], in_=zeros_f[:], pattern=[[1, 128]], compare_op=ALU.is_ge,
                                             fill=NEG, base=0, channel_multiplier=-1), r=[zeros_f], w=[mneg])
        P_("pool", lambda: pool.affine_select(out=mask_gt[:], in_=ones_f[:], pattern=[[1, 128]], compare_op=ALU.is_gt,
                                             fill=0.0, base=0, channel_multiplier=-1), r=[ones_f], w=[mask_gt])

        dpar = S.new_dma_sem("dpar")
        S.dma("sp", convw[:], convw_d, dpar, writes=[convw])
        S.dma("sp", hp[:], hp_d, dpar, writes=[hp])
        S.dma("sp", ncols[:], ncols_d, dpar, writes=[ncols])
        dwsm = S.new_dma_sem("dwsm")
        S.dma("pool", wsm[:], wsm_d, dwsm, writes=[wsm])
        for b in (convw, hp, ncols):
            b.d.w = (dpar["sem"], dpar["val"])
        P_("dve", lambda: dve.tensor_scalar(out=ncs[:, 0:1], in0=ncols[:, 0:1], scalar1=0.5, scalar2=None, op0=ALU.mult),
           r=[ncols], w=[ncs])
        P_("dve", lambda: dve.tensor_scalar(out=ncs[:, 1:2], in0=ncols[:, 1:2], scalar1=float(128 ** -0.5), scalar2=None,
                                            op0=ALU.mult), r=[ncols], w=[ncs])

        with ExitStack() as pa:
            prew = sb(pa, "prew", [128, DM])
            xts = [sb(pa, "xt%d" % i, [128, DM]) for i in range(2)]
            xns = [sb(pa, "xn%d" % i, [128, DM], BF16) for i in range(2)]
            junk = sb(pa, "junkA", [128, DM], BF16)
            ssA = sb(pa, "ssA", [128, NTI])
            rsA = sb(pa, "rsA", [128, NTI])
            ptr = [ps(pa, "ptrA%d" % i, [128, 8, 128], BF16) for i in range(2)]
            psm = ps(pa, "psmA", [128, 512])
            dxs = [S.new_dma_sem("dx%d" % i) for i in range(2)]
            S.dma("sp", prew[:], prew_d, S.new_dma_sem("dprew"), writes=[prew])
            P_("dve", lambda: dve.memset(sm[:], 0.0), w=[sm])
            P_("dve", lambda: dve.memset(psm[:], 0.0), w=[psm])
            nev = 0
            for i in range(NTI):
                P = tP(i)
                c0, c1 = tcol(i)
                xt, xn = xts[i % 2], xns[i % 2]
                src = meta if i == 0 else xin[(i - 1) * 128:i * 128, :]
                S.dma("sp", xt[0:P, :], src, dxs[i % 2], writes=[xt])
                P_("act", lambda: act.activation(out=junk[0:P, :], in_=xt[0:P, :], func=AF.Square,
                                                 accum_out=ssA[0:P, i:i + 1]), r=[xt], w=[junk, ssA])
                P_("dve", lambda: dve.tensor_scalar(out=rsA[0:P, i:i + 1], in0=ssA[0:P, i:i + 1], scalar1=1.0 / DM,
                                                    scalar2=EPS, op0=ALU.mult, op1=ALU.add), r=[ssA], w=[rsA])
                P_("pool", lambda: pool.tensor_tensor(out=rsA[0:P, i:i + 1], in0=rsA[0:P, i:i + 1], in1=negh[0:P, 0:1],
                                                      op=ALU.pow), r=[rsA, negh], w=[rsA])
                P_("dve", lambda: dve.scalar_tensor_tensor(out=xn[0:P, :], in0=xt[0:P, :], scalar=rsA[0:P, i:i + 1],
                                                           in1=prew[0:P, :], op0=ALU.mult, op1=ALU.mult),
                   r=[xt, rsA, prew], w=[xn])
                for k0 in range(0, KC, 8):
                    kn = min(8, KC - k0)
                    pt = ptr[nev % 2]
                    for k in range(kn):
                        P_("pe", lambda: pe.transpose(pt[:, k, 0:P], xn[0:P, (k0 + k) * 128:(k0 + k + 1) * 128],
                                                      ident_b[0:P, 0:P]), r=[xn, ident_b], w=[pt], inc=(k == kn - 1))
                    ev = "act" if nev % 2 == 0 else "dve"
                    if ev == "act":
                        P_("act", lambda: act.copy(out=xnT[:, k0:k0 + kn, c0:c1], in_=pt[:, 0:kn, 0:P]), r=[pt], w=[xnT])
                    else:
                        P_("dve", lambda: dve.tensor_copy(out=xnT[:, k0:k0 + kn, c0:c1], in_=pt[:, 0:kn, 0:P]), r=[pt], w=[xnT])
                    nev += 1
                for k in range(KC):
                    P_("pe", lambda: pe.matmul(psm[0:P, i * NS:(i + 1) * NS], xnT[:, k, c0:c1], wsm[:, k, :],
                                               start=(k == 0), stop=(k == KC - 1)), r=[xnT, wsm], w=[psm], inc=(k == KC - 1))
            P_("dve", lambda: dve.tensor_copy(out=sm[:].rearrange("p t s -> p (t s)"), in_=psm[:, 0:NTI * NS]), r=[psm], w=[sm])

            t1 = sb(pa, "t1s", [128, NTI, HM])
            t2 = sb(pa, "t2s", [128, NTI, HM])
            t3 = sb(pa, "t3s", [128, NTI, HM])
            nea = sb(pa, "nea", [128, HG])
            gc = sb(pa, "gc", [128, NTI, HG])
            glb = sb(pa, "glb", [128, NTI, HG])
            tot = sb(pa, "tot", [128, NTI, HF])
            pcs = ps(pa, "pcs", [128, 512])
            pcf = ps(pa, "pcf", [128, 512])

            def bc(ap, h):
                return ap.unsqueeze(1).to_broadcast([128, NTI, h])

            P_("act", lambda: act.activation(out=t1[:, :, 0:HG], in_=sm[:, :, 0:HG], func=AF.Tanh, scale=0.5), r=[sm], w=[t1])
            P_("dve", lambda: dve.tensor_scalar(out=beta[:], in0=t1[:, :, 0:HG], scalar1=0.5, scalar2=0.5, op0=ALU.mult,
                                                op1=ALU.add), r=[t1], w=[beta])
            P_("act", lambda: act.activation(out=nea[:], in_=hp[:, 0:HG], func=AF.Exp), r=[hp], w=[nea])
            P_("dve", lambda: dve.tensor_scalar(out=nea[:], in0=nea[:], scalar1=-1.0, scalar2=None, op0=ALU.mult), r=[nea], w=[nea])
            P_("dve", lambda: dve.tensor_tensor(out=t1[:, :, 0:HG], in0=sm[:, :, HG:2 * HG], in1=bc(hp[:, HG:2 * HG], HG),
                                                op=ALU.add), r=[sm, hp], w=[t1])
            P_("act", lambda: act.activation(out=t2[:, :, 0:HG], in_=t1[:, :, 0:HG], func=AF.Abs), r=[t1], w=[t2])
            P_("act", lambda: act.activation(out=t2[:, :, 0:HG], in_=t2[:, :, 0:HG], func=AF.Exp, scale=-1.0), r=[t2], w=[t2])
            P_("act", lambda: act.activation(out=t2[:, :, 0:HG], in_=t2[:, :, 0:HG], func=AF.Ln, bias=1.0), r=[t2], w=[t2])
            P_("dve", lambda: dve.scalar_tensor_tensor(out=t3[:, :, 0:HG], in0=t1[:, :, 0:HG], scalar=0.0, in1=t2[:, :, 0:HG],
                                                       op0=ALU.max, op1=ALU.add), r=[t1, t2], w=[t3])
            P_("dve", lambda: dve.tensor_tensor(out=gg[:], in0=t3[:, :, 0:HG], in1=bc(nea[:], HG), op=ALU.mult),
               r=[t3, nea], w=[gg])
            P_("dve", lambda: dve.tensor_tensor(out=t1[:, :, 0:HF], in0=sm[:, :, 2 * HG:NS], in1=bc(hp[:, 2 * HG:NS], HF),
                                                op=ALU.add), r=[sm, hp], w=[t1])
            P_("act", lambda: act.activation(out=t2[:, :, 0:HF], in_=t1[:, :, 0:HF], func=AF.Abs), r=[t1], w=[t2])
            P_("act", lambda: act.activation(out=t2[:, :, 0:HF], in_=t2[:, :, 0:HF], func=AF.Exp, scale=-1.0), r=[t2], w=[t2])
            P_("act", lambda: act.activation(out=t2[:, :, 0:HF], in_=t2[:, :, 0:HF], func=AF.Ln, bias=1.0), r=[t2], w=[t2])
            P_("dve", lambda: dve.tensor_scalar(out=t1[:, :, 0:HF], in0=t1[:, :, 0:HF], scalar1=-1.0, scalar2=None, op0=ALU.mult),
               r=[t1], w=[t1])
            P_("dve", lambda: dve.scalar_tensor_tensor(out=t3[:, :, 0:HF], in0=t1[:, :, 0:HF], scalar=0.0, in1=t2[:, :, 0:HF],
                                                       op0=ALU.max, op1=ALU.add), r=[t1, t2], w=[t3])
            P_("dve", lambda: dve.tensor_scalar(out=logf[:], in0=t3[:, :, 0:HF], scalar1=-1.0, scalar2=None, op0=ALU.mult),
               r=[t3], w=[logf])

            P_("dve", lambda: dve.memset(pcs[:], 0.0), w=[pcs])
            P_("dve", lambda: dve.memset(pcf[:], 0.0), w=[pcf])
            for i in range(NTI):
                P = tP(i)
                o = i * 2 * HG
                o2 = i * 2 * HF
                P_("pe", lambda: pe.matmul(pcs[0:P, o:o + HG], mask_le[0:P, 0:P], gg[0:P, i, :], start=True, stop=True),
                   r=[mask_le, gg], w=[pcs], inc=False)
                P_("pe", lambda: pe.matmul(pcs[:, o + HG:o + 2 * HG], ones_f[0:P, :], gg[0:P, i, :], start=True, stop=True),
                   r=[ones_f, gg], w=[pcs], inc=False)
                P_("pe", lambda: pe.matmul(pcf[0:P, o2:o2 + HF], mask_le[0:P, 0:P], logf[0:P, i, :],
                                           start=True, stop=True), r=[mask_le, logf], w=[pcf], inc=False)
                P_("pe", lambda: pe.matmul(pcf[:, o2 + HF:o2 + 2 * HF], ones_f[0:P, :], logf[0:P, i, :], start=True,
                                           stop=True), r=[ones_f, logf], w=[pcf], inc=True)
            pv = pcs[:, 0:NTI * 2 * HG].rearrange("p (t w) -> p t w", w=2 * HG)
            pv2 = pcf[:, 0:NTI * 2 * HF].rearrange("p (t w) -> p t w", w=2 * HF)
            P_("dve", lambda: dve.tensor_copy(out=gc[:], in_=pv[:, :, 0:HG]), r=[pcs], w=[gc])
            P_("dve", lambda: dve.tensor_copy(out=glb[:], in_=pv[:, :, HG:2 * HG]), r=[pcs], w=[glb])
            P_("dve", lambda: dve.tensor_copy(out=cl[:], in_=pv2[:, :, 0:HF]), r=[pcf], w=[cl])
            P_("dve", lambda: dve.tensor_copy(out=tot[:], in_=pv2[:, :, HF:2 * HF]), r=[pcf], w=[tot])
            P_("act", lambda: act.activation(out=eG[:], in_=gc[:], func=AF.Exp), r=[gc], w=[eG])
            P_("dve", lambda: dve.tensor_scalar(out=neG[:], in0=eG[:], scalar1=-1.0, scalar2=None, op0=ALU.mult), r=[eG], w=[neG])
            P_("dve", lambda: dve.tensor_tensor(out=gc[:], in0=glb[:], in1=gc[:], op=ALU.subtract), r=[glb, gc], w=[gc])
            P_("act", lambda: act.activation(out=eGl[:], in_=gc[:], func=AF.Exp), r=[gc], w=[eGl])
            P_("act", lambda: act.activation(out=dec[:], in_=glb[:], func=AF.Exp), r=[glb], w=[dec])
            P_("dve", lambda: dve.tensor_scalar(out=nbeta[:], in0=beta[:], scalar1=-1.0, scalar2=None, op0=ALU.mult), r=[beta], w=[nbeta])
            P_("dve", lambda: dve.tensor_tensor(out=bEl[:], in0=beta[:], in1=eGl[:], op=ALU.mult), r=[beta, eGl], w=[bEl])
            P_("dve", lambda: dve.memset(rbb[:], 0.0), w=[rbb])
            for i in range(1, NTI):
                P_("dve", lambda: dve.tensor_tensor(out=rbb[:, i, :], in0=rbb[:, i - 1, :], in1=tot[:, i - 1, :], op=ALU.add),
                   r=[rbb, tot], w=[rbb])
            P_("dve", lambda: dve.tensor_tensor(out=cab[:], in0=rbb[:], in1=cl[:], op=ALU.add), r=[rbb, cl], w=[cab])
            P_("act", lambda: act.activation(out=afac[:], in_=cl[:], func=AF.Exp), r=[cl], w=[afac])
            for h in range(HF):
                P_("dve", lambda: dve.tensor_tensor(out=biasm[:, h, :, :],
                                                    in0=rbb[:, :, h].unsqueeze(2).to_broadcast([128, NTI, NTI]),
                                                    in1=cab[:, :, h].unsqueeze(1).to_broadcast([128, NTI, NTI]),
                                                    op=ALU.subtract), r=[rbb, cab], w=[biasm])
            S.barrier()

        with ExitStack() as pb:
            wsl = [[sb(pb, "w%d_%d" % (s, b), [128, KC, 128], BF16) for b in range(4)] for s in range(2)]
            dws = [[S.new_dma_sem("dw%d_%d" % (s, b)) for b in range(4)] for s in range(2)]
            mts = [sb(pb, "mts%d" % i, [128, L], BF16) for i in range(2)]
            dms = [S.new_dma_sem("dm%d" % i) for i in range(2)]
            pbank = [ps(pb, "pbank%d" % i, [128, 512]) for i in range(2)]
            nproj = [0]

            def load_w(u):
                for b in range(4):
                    S.dma("pool", wsl[u % 2][b][:], wu[u, b], dws[u % 2][b], writes=[wsl[u % 2][b]])

            def project(u, b, g):
                c0, c1 = gcol(g)
                n = c1 - c0
                pbk = pbank[nproj[0] % 2]
                nproj[0] += 1
                w = wsl[u % 2][b]
                for k in range(KC):
                    P_("pe", lambda: pe.matmul(pbk[:, 0:n], w[:, k, :], xnT[:, k, c0:c1], start=(k == 0), stop=(k == KC - 1)),
                       r=[w, xnT], w=[pbk], inc=(k == KC - 1))
                return pbk, n

            load_w(0)

            def merge(front, back, ratio):
                fa, ba = front is not None, back is not None
                while fa or ba:
                    if ba:
                        try:
                            next(back)
                        except StopIteration:
                            ba = False
                    for _ in range(ratio if ba else 1):
                        if fa:
                            try:
                                next(front)
                            except StopIteration:
                                fa = False

            with ExitStack() as pg:
                raw = [sb(pg, "raw%d" % b, [128, 3 + GW]) for b in range(3)]
                ycv = [sb(pg, "ycv%d" % b, [128, GW]) for b in range(3)]
                tnh = sb(pg, "tnh", [128, GW])
                sqs = [sb(pg, "sq%d" % i, [128, GW], BF16) for i in range(2)]
                rs8 = sb(pg, "rs8", [128, 2 * GT])
                dg = [sb(pg, "dg%d" % i, [128, 128]) for i in range(2)]
                qT = [sb(pg, "qT%d" % i, [128, GW], BF16) for i in range(2)]
                kT = [sb(pg, "kT%d" % i, [128, GW], BF16) for i in range(2)]
                vT = sb(pg, "vT", [128, GW], BF16)
                szT = [sb(pg, "szT%d" % i, [128, GW], BF16) for i in range(2)]
                ktok = [sb(pg, "ktok%d" % i, [128, GT, 128], BF16) for i in range(2)]
                vtok = [sb(pg, "vtok%d" % i, [128, GT, 128], BF16) for i in range(2)]
                NSL = 2 * GT
                Ysl = [sb(pg, "Y%d" % i, [128, 128], BF16) for i in range(NSL)]
                Asl = [sb(pg, "A%d" % i, [128, 128], BF16) for i in range(NSL)]
                Ag = [sb(pg, "Ag%d" % i, [128, 128]) for i in range(2)]
                Dge = [sb(pg, "Dge%d" % i, [128, 128]) for i in range(2)]
                Dgt = [sb(pg, "Dgt%d" % i, [128, 128]) for i in range(2)]
                Nn = [[sb(pg, "Nn%d_%d" % (s_, i), [128, 2, 128]) for i in range(2)] for s_ in range(2)]
                Pm = [[sb(pg, "Pm%d_%d" % (s_, i), [128, 128]) for i in range(2)] for s_ in range(2)]
                Rb = sb(pg, "Rb", [128, 128], BF16)
                vnew = sb(pg, "vnew", [128, 128], BF16)
                vnd = sb(pg, "vnd", [128, 128], BF16)
                t1o = sb(pg, "t1o", [128, 128])
                oo = sb(pg, "oo", [128, 128])
                onb = sb(pg, "onb", [128, 128], BF16)
                junkg = sb(pg, "junkg", [128, 128], BF16)
                sso = sb(pg, "sso", [128, 2])
                Sf = sb(pg, "Sf", [128, 128])
                Sb = sb(pg, "Sb", [128, 128], BF16)
                pT = ps(pg, "pT", [128, 8, 128], BF16)
                pU = ps(pg, "pU", [128, 512])
                pV = ps(pg, "pV", [128, 4, 128])
                pW = [ps(pg, "pW%d" % i, [128, 3, 128]) for i in range(2)]
                pC = ps(pg, "pC", [128, 4, 128])

                def gdn_front(h, g, kf):
                    u = h
                    c0, c1 = gcol(g)
                    n = c1 - c0
                    par = kf % 2
                    tiles = gtiles(g)
                    if g == 0:
                        if u + 1 < NU:
                            load_w(u + 1)
                        for b in range(3):
                            P_("dve", lambda: dve.memset(raw[b][:, 0:3], 0.0), w=[raw[b]])
                    for b in range(3):
                        pbk, _ = project(u, b, g)
                        P_("act", lambda: act.copy(out=raw[b][:, 3:3 + n], in_=pbk[:, 0:n]), r=[pbk], w=[raw[b]])
                        yield
                    pbk, _ = project(u, 3, g)
                    P_("act", lambda: act.activation(out=tnh[:, 0:n], in_=pbk[:, 0:n], func=AF.Tanh, scale=0.5), r=[pbk], w=[tnh])
                    P_("dve", lambda: dve.scalar_tensor_tensor(out=szT[par][:, 0:n], in0=tnh[:, 0:n], scalar=1.0, in1=pbk[:, 0:n],
                                                               op0=ALU.add, op1=ALU.mult), r=[tnh, pbk], w=[szT[par]])
                    yield
                    for b in range(3):
                        cw = lambda j: convw[:, h * 12 + b * 4 + j:h * 12 + b * 4 + j + 1]
                        y = ycv[b]
                        P_("dve", lambda: dve.tensor_scalar(out=y[:, 0:n], in0=raw[b][:, 3:3 + n], scalar1=cw(3), scalar2=None,
                                                            op0=ALU.mult), r=[raw[b], convw], w=[y])
                        for j in (2, 1, 0):
                            P_("dve", lambda: dve.scalar_tensor_tensor(out=y[:, 0:n], in0=raw[b][:, j:j + n], scalar=cw(j),
                                                                       in1=y[:, 0:n], op0=ALU.mult, op1=ALU.add),
                               r=[raw[b], convw, y], w=[y])
                        yield
                        P_("pool", lambda: pool.tensor_copy(out=raw[b][:, 0:3], in_=raw[b][:, n:n + 3]), r=[raw[b]], w=[raw[b]])
                        P_("act", lambda: act.activation(out=tnh[:, 0:n], in_=y[:, 0:n], func=AF.Tanh, scale=0.5), r=[y], w=[tnh])
                        if b < 2:
                            P_("dve", lambda: dve.scalar_tensor_tensor(out=y[:, 0:n], in0=tnh[:, 0:n], scalar=1.0, in1=y[:, 0:n],
                                                                       op0=ALU.add, op1=ALU.mult), r=[tnh, y], w=[y])
                            P_("pool", lambda: pool.tensor_tensor(out=sqs[b][:, 0:n], in0=y[:, 0:n], in1=y[:, 0:n], op=ALU.mult),
                               r=[y], w=[sqs[b]])
                            for ti, i in enumerate(tiles):
                                P = tP(i)
                                lc = tcol(i)[0] - c0
                                P_("pe", lambda: pe.matmul(pV[0:P, 3, b * GT + ti:b * GT + ti + 1], sqs[b][:, lc:lc + P],
                                                           ones_b[:, 0:1], start=True, stop=True), r=[sqs[b], ones_b], w=[pV],
                                   inc=(ti == len(tiles) - 1))
                        else:
                            P_("dve", lambda: dve.scalar_tensor_tensor(out=vT[:, 0:n], in0=tnh[:, 0:n], scalar=1.0, in1=y[:, 0:n],
                                                                       op0=ALU.add, op1=ALU.mult), r=[tnh, y], w=[vT])
                        yield
                    PG = tP(tiles[0])
                    ncol8 = 2 * GT
                    P_("dve", lambda: dve.tensor_scalar(out=rs8[0:PG, 0:ncol8], in0=pV[0:PG, 3, 0:ncol8], scalar1=1.0,
                                                        scalar2=4.0 * EPS, op0=ALU.mult, op1=ALU.add), r=[pV], w=[rs8])
                    P_("pool", lambda: pool.tensor_tensor(out=rs8[0:PG, 0:ncol8], in0=rs8[0:PG, 0:ncol8], in1=negh[0:PG, 0:ncol8],
                                                          op=ALU.pow), r=[rs8, negh], w=[rs8])
                    yield
                    for b in range(2):
                        y = ycv[b]
                        for ti, i in enumerate(tiles):
                            P = tP(i)
                            lc = tcol(i)[0] - c0
                            dgb = dg[(b * GT + ti) % 2]
                            P_("dve", lambda: dve.tensor_scalar(out=dgb[0:P, 0:P], in0=ident_f[0:P, 0:P],
                                                                scalar1=rs8[0:P, b * GT + ti:b * GT + ti + 1], scalar2=None,
                                                                op0=ALU.mult), r=[ident_f, rs8], w=[dgb])
                            P_("pe", lambda: pe.matmul(pU[:, lc:lc + P], ones_f[0:P, :], dgb[0:P, 0:P], start=True, stop=True),
                               r=[ones_f, dgb], w=[pU])
                        dst = qT[par] if b == 0 else kT[par]
                        sc = float(128 ** -0.5) if b == 0 else 1.0
                        P_("dve", lambda: dve.scalar_tensor_tensor(out=dst[:, 0:n], in0=y[:, 0:n], scalar=sc, in1=pU[:, 0:n],
                                                                   op0=ALU.mult, op1=ALU.mult), r=[y, pU], w=[dst])
                        yield
                    for t0 in range(0, len(tiles), 2):
                        pr = tiles[t0:t0 + 2]
                        for q_, i in enumerate(pr):
                            P = tP(i)
                            lc = tcol(i)[0] - c0
                            P_("pe", lambda: pe.transpose(pT[0:P, 2 * q_, :], kT[par][:, lc:lc + P], ident_b[:]),
                               r=[kT[par], ident_b], w=[pT], inc=False)
                            P_("pe", lambda: pe.transpose(pT[0:P, 2 * q_ + 1, :], vT[:, lc:lc + P], ident_b[:]),
                               r=[vT, ident_b], w=[pT], inc=True)
                        for q_, i in enumerate(pr):
                            P = tP(i)
                            ti = t0 + q_
                            P_("act", lambda: act.copy(out=ktok[par][0:P, ti, :], in_=pT[0:P, 2 * q_, :]), r=[pT], w=[ktok[par]])
                            P_("act", lambda: act.activation(out=vtok[par][0:P, ti, :], in_=pT[0:P, 2 * q_ + 1, :], func=AF.Copy,
                                                             scale=0.5), r=[pT], w=[vtok[par]])
                        yield
                    for t0 in range(0, len(tiles), 2):
                        pr = tiles[t0:t0 + 2]
                        for q_, i in enumerate(pr):
                            P = tP(i)
                            ti = t0 + q_
                            lc = tcol(i)[0] - c0
                            sl = par * GT + ti
                            N0 = Nn[q_][0]
                            P_("dve", lambda: dve.tensor_scalar(out=Ag[q_][0:P, 0:P], in0=mask_le[0:P, 0:P],
                                                                scalar1=gg[0:P, i, h:h + 1], scalar2=None, op0=ALU.mult),
                               r=[mask_le, gg], w=[Ag[q_]])
                            P_("pe", lambda: pe.matmul(pV[0:P, 0, 0:P], bst[0:P, 0:P], Ag[q_][0:P, 0:P], start=True, stop=False),
                               r=[bst, Ag[q_]], w=[pV], inc=False)
                            P_("pe", lambda: pe.matmul(pV[0:P, 0, 0:P], ident_f[0:P, 0:P], mneg[0:P, 0:P], start=False, stop=True),
                               r=[ident_f, mneg], w=[pV], inc=False)
                            P_("pe", lambda: pe.matmul(pV[0:P, 1, 0:P], kT[par][:, lc:lc + P], kT[par][:, lc:lc + P], start=True,
                                                       stop=True), r=[kT[par]], w=[pV], inc=False)
                            P_("pe", lambda: pe.matmul(pV[0:P, 2, 0:P], kT[par][:, lc:lc + P], qT[par][:, lc:lc + P], start=True,
                                                       stop=True), r=[kT[par], qT[par]], w=[pV], inc=True)
                            P_("act", lambda: act.activation(out=Dge[q_][0:P, 0:P], in_=pV[0:P, 0, 0:P], func=AF.Exp),
                               r=[pV], w=[Dge[q_]])
                            P_("dve", lambda: dve.scalar_tensor_tensor(out=Dgt[q_][0:P, 0:P], in0=pV[0:P, 1, 0:P],
                                                                       scalar=nbeta[0:P, i, h:h + 1], in1=Dge[q_][0:P, 0:P],
                                                                       op0=ALU.mult, op1=ALU.mult),
                               r=[pV, nbeta, Dge[q_]], w=[Dgt[q_]])
                            P_("dve", lambda: dve.tensor_tensor(out=Asl[sl][0:P, 0:P], in0=pV[0:P, 2, 0:P], in1=Dge[q_][0:P, 0:P],
                                                                op=ALU.mult), r=[pV, Dge[q_]], w=[Asl[sl]])
                            P_("dve", lambda: dve.tensor_tensor(out=N0[0:P, 1, 0:P], in0=Dgt[q_][0:P, 0:P], in1=mask_gt[0:P, 0:P],
                                                                op=ALU.mult), r=[Dgt[q_], mask_gt], w=[N0])
                            P_("pe", lambda: pe.transpose(pW[q_][0:P, 2, 0:P], N0[0:P, 1, 0:P], ident_f[0:P, 0:P]),
                               r=[N0, ident_f], w=[pW[q_]])
                            P_("act", lambda: act.copy(out=N0[0:P, 0, 0:P], in_=pW[q_][0:P, 2, 0:P]), r=[pW[q_]], w=[N0])
                            P_("dve", lambda: dve.tensor_tensor(out=Pm[q_][0][0:P, 0:P], in0=N0[0:P, 1, 0:P], in1=ident_f[0:P, 0:P],
                                                                op=ALU.add), r=[N0, ident_f], w=[Pm[q_][0]])
                            yield
                        nlev = 6 if tP(pr[0]) == 128 else 3
                        for lv in range(1, nlev + 1):
                            last = lv == nlev
                            cur = (lv - 1) % 2
                            for q_, i in enumerate(pr):
                                P = tP(i)
                                Nc, Nx = Nn[q_][cur], Nn[q_][1 - cur]
                                pw = pW[q_]
                                P_("pe", lambda: pe.matmul(pw[0:P, 0, 0:P], Nc[0:P, 1, 0:P], Nc[0:P, 0, 0:P], start=True, stop=True),
                                   r=[Nc], w=[pw], inc=last)
                                if not last:
                                    P_("pe", lambda: pe.matmul(pw[0:P, 1, 0:P], Nc[0:P, 0, 0:P], Nc[0:P, 1, 0:P], start=True,
                                                               stop=True), r=[Nc], w=[pw], inc=True)
                            for q_, i in enumerate(pr):
                                P = tP(i)
                                Nx = Nn[q_][1 - cur]
                                pw = pW[q_]
                                if not last:
                                    P_("act", lambda: act.copy(out=Nx[0:P, :, 0:P], in_=pw[0:P, 0:2, 0:P]), r=[pw], w=[Nx])
                                else:
                                    P_("act", lambda: act.copy(out=Nx[0:P, 0, 0:P], in_=pw[0:P, 0, 0:P]), r=[pw], w=[Nx])
                            for q_, i in enumerate(pr):
                                P = tP(i)
                                Nx = Nn[q_][1 - cur]
                                Pc = Pm[q_][cur]
                                P_("pe", lambda: pe.matmul(pW[q_][0:P, 2, 0:P], Nx[0:P, 0, 0:P], Pc[0:P, 0:P], start=True, stop=True),
                                   r=[Nx, Pc], w=[pW[q_]])
                            for q_, i in enumerate(pr):
                                P = tP(i)
                                sl = par * GT + t0 + q_
                                Pc = Pm[q_][cur]
                                Pn = Ysl[sl] if last else Pm[q_][1 - cur]
                                P_("dve", lambda: dve.tensor_tensor(out=Pn[0:P, 0:P], in0=pW[q_][0:P, 2, 0:P], in1=Pc[0:P, 0:P],
                                                                    op=ALU.add), r=[pW[q_], Pc], w=[Pn])
                            yield

                def gdn_chain(h, g, kf):
                    c0, c1 = gcol(g)
                    par = kf % 2
                    tiles = gtiles(g)
                    mt = mts[h % 2]
                    if g == 0:
                        P_("dve", lambda: dve.memset(Sf[:], 0.0), w=[Sf])
                        P_("dve", lambda: dve.memset(Sb[:], 0.0), w=[Sb])
                    for ti, i in enumerate(tiles):
                        P = tP(i)
                        lc = tcol(i)[0] - c0
                        tc0, tc1 = tcol(i)
                        sl = par * GT + ti
                        pc = pC
                        kTt = kT[par][:, lc:lc + P]
                        qTt = qT[par][:, lc:lc + P]
                        P_("pe", lambda: pe.matmul(pc[0:P, 0, :], kTt, Sb[:], start=True, stop=True), r=[kT[par], Sb], w=[pc],
                           inc=False)
                        P_("pe", lambda: pe.matmul(pc[0:P, 2, :], qTt, Sb[:], start=True, stop=True), r=[qT[par], Sb], w=[pc])
                        P_("dve", lambda: dve.scalar_tensor_tensor(out=Rb[0:P, :], in0=pc[0:P, 0, :], scalar=neG[0:P, i, h:h + 1],
                                                                   in1=vtok[par][0:P, ti, :], op0=ALU.mult, op1=ALU.add),
                           r=[pc, neG, vtok[par]], w=[Rb])
                        P_("pe", lambda: pe.matmul(pc[0:P, 1, :], Ysl[sl][0:P, 0:P], Rb[0:P, :], start=True, stop=True),
                           r=[Ysl[sl], Rb], w=[pc])
                        yield
                        P_("act", lambda: act.activation(out=vnd[0:P, :], in_=pc[0:P, 1, :], func=AF.Copy,
                                                         scale=bEl[0:P, i, h:h + 1]), r=[pc, bEl], w=[vnd])
                        P_("act", lambda: act.activation(out=vnew[0:P, :], in_=pc[0:P, 1, :], func=AF.Copy,
                                                         scale=beta[0:P, i, h:h + 1]), r=[pc, beta], w=[vnew])
                        P_("act", lambda: act.activation(out=t1o[0:P, :], in_=pc[0:P, 2, :], func=AF.Copy,
                                                         scale=eG[0:P, i, h:h + 1]), r=[pc, eG], w=[t1o])
                        P_("pe", lambda: pe.matmul(pc[:, 0, :], ktok[par][0:P, ti, :], vnd[0:P, :], start=True, stop=True),
                           r=[ktok[par], vnd], w=[pc], inc=False)
                        P_("pe", lambda: pe.matmul(pc[0:P, 3, :], Asl[sl][0:P, 0:P], vnew[0:P, :], start=True, stop=True),
                           r=[Asl[sl], vnew], w=[pc])
                        P_("dve", lambda: dve.scalar_tensor_tensor(out=Sf[:], in0=Sf[:], scalar=dec[:, i, h:h + 1], in1=pc[:, 0, :],
                                                                   op0=ALU.mult, op1=ALU.add), r=[Sf, dec, pc], w=[Sf])
                        P_("act", lambda: act.copy(out=Sb[:], in_=Sf[:]), r=[Sf], w=[Sb])
                        yield
                        P_("dve", lambda: dve.tensor_tensor(out=oo[0:P, :], in0=pc[0:P, 3, :], in1=t1o[0:P, :], op=ALU.add),
                           r=[pc, t1o], w=[oo])
                        P_("act", lambda: act.activation(out=junkg[0:P, :], in_=oo[0:P, :], func=AF.Square,
                                                         accum_out=sso[0:P, 0:1]), r=[oo], w=[junkg, sso])
                        P_("dve", lambda: dve.tensor_scalar(out=sso[0:P, 1:2], in0=sso[0:P, 0:1], scalar1=1.0 / 128, scalar2=EPS,
                                                            op0=ALU.mult, op1=ALU.add), r=[sso], w=[sso])
                        P_("pool", lambda: pool.tensor_tensor(out=sso[0:P, 1:2], in0=sso[0:P, 1:2], in1=negh[0:P, 0:1], op=ALU.pow),
                           r=[sso, negh], w=[sso])
                        P_("act", lambda: act.activation(out=onb[0:P, :], in_=oo[0:P, :], func=AF.Copy, scale=sso[0:P, 1:2]),
                           r=[oo, sso], w=[onb])
                        osl = 4 + (ti % 4)
                        P_("pe", lambda: pe.transpose(pT[:, osl, 0:P], onb[0:P, :], ident_b[0:P, 0:P]), r=[onb, ident_b], w=[pT])
                        P_("dve", lambda: dve.scalar_tensor_tensor(out=mt[:, tc0:tc1], in0=pT[:, osl, 0:P], scalar=ncs[:, 0:1],
                                                                   in1=szT[par][:, lc:lc + P], op0=ALU.mult, op1=ALU.mult),
                           r=[pT, ncs, szT[par]], w=[mt])
                        yield
                    if g == NG - 1:
                        S.dma("sp", mscr[h], mt[:], dms[h % 2], reads=[mt])
                        if mdbg is not None:
                            S.dma("sp", mdbg[h], mt[:], dms[h % 2], reads=[mt])

                P_("dve", lambda: dve.memset(pV[:, 3, :], 1.0), w=[pV])
                flat = [(h, g) for h in range(HG) for g in range(NG)]
                for _ in gdn_front(flat[0][0], flat[0][1], 0):
                    pass
                for kf in range(len(flat)):
                    fr = gdn_front(flat[kf + 1][0], flat[kf + 1][1], kf + 1) if kf + 1 < len(flat) else None
                    merge(fr, gdn_chain(flat[kf][0], flat[kf][1], kf), 3)
                S.barrier()

            with ExitStack() as pf:
                qTf = [sb(pf, "qTf%d" % i, [128, GW], BF16) for i in range(2)]
                kTf = [sb(pf, "kTf%d" % i, [128, L], BF16) for i in range(2)]
                vTf = sb(pf, "vTf", [128, GW], BF16)
                vext = [sb(pf, "vext%d" % i, [128, NTI, 132], BF16) for i in range(2)]
                sgT = [sb(pf, "sgT%d" % i, [128, GW], BF16) for i in range(2)]
                rawq = sb(pf, "rawq", [128, GW])
                rawk = sb(pf, "rawk", [128, GW])
                sqf = sb(pf, "sqf", [128, GW], BF16)
                rs8f = sb(pf, "rs8f", [128, 2 * GT])
                dgf = [sb(pf, "dgf%d" % i, [128, 128]) for i in range(2)]
                tnf = sb(pf, "tnf", [128, GW])
                Af = [sb(pf, "Af%d" % i, [128, 128]) for i in range(2)]
                pTs = [sb(pf, "pTs%d" % i, [128, 128], BF16) for i in range(4)]
                tof = sb(pf, "tof", [128, 132])
                of = sb(pf, "of", [128, 132])
                rinv = sb(pf, "rinv", [128, 1])
                onf = sb(pf, "onf", [128, 128], BF16)
                pUf = ps(pf, "pUf", [128, 512])
                pTf = ps(pf, "pTf", [128, GT + 1, 128], BF16)
                pSSf = ps(pf, "pSSf", [128, 2 * GT])
                pS = [ps(pf, "pS%d" % i, [128, 128]) for i in range(2)]
                pO = ps(pf, "pO", [128, 2, 132])

                def fox_front(h, g, kf):
                    u = HG + h
                    c0, c1 = gcol(g)
                    n = c1 - c0
                    par = kf % 2
                    hp_ = h % 2
                    tiles = gtiles(g)
                    if g == 0 and u + 1 < NU:
                        load_w(u + 1)
                    for b in range(2):
                        pbk, _ = project(u, b, g)
                        rw = rawq if b == 0 else rawk
                        P_("act", lambda: act.copy(out=rw[:, 0:n], in_=pbk[:, 0:n]), r=[pbk], w=[rw])
                        P_("pool", lambda: pool.tensor_tensor(out=sqf[:, 0:n], in0=rw[:, 0:n], in1=rw[:, 0:n], op=ALU.mult),
                           r=[rw], w=[sqf])
                        for ti, i in enumerate(tiles):
                            P = tP(i)
                            lc = tcol(i)[0] - c0
                            P_("pe", lambda: pe.matmul(pSSf[0:P, b * GT + ti:b * GT + ti + 1], sqf[:, lc:lc + P], ones_b[:, 0:1],
                                                       start=True, stop=True), r=[sqf, ones_b], w=[pSSf],
                               inc=(ti == len(tiles) - 1))
                        yield
                    pbk, _ = project(u, 2, g)
                    P_("act", lambda: act.copy(out=vTf[:, 0:n], in_=pbk[:, 0:n]), r=[pbk], w=[vTf])
                    yield
                    pbk, _ = project(u, 3, g)
                    P_("act", lambda: act.activation(out=tnf[:, 0:n], in_=pbk[:, 0:n], func=AF.Tanh, scale=0.5), r=[pbk], w=[tnf])
                    P_("dve", lambda: dve.scalar_tensor_tensor(out=sgT[par][:, 0:n], in0=tnf[:, 0:n], scalar=1.0, in1=pbk[:, 0:n],
                                                               op0=ALU.add, op1=ALU.mult), r=[tnf, pbk], w=[sgT[par]])
                    yield
                    PG = tP(tiles[0])
                    ncol8 = 2 * GT
                    P_("dve", lambda: dve.tensor_scalar(out=rs8f[0:PG, 0:ncol8], in0=pSSf[0:PG, 0:ncol8], scalar1=1.0 / 128,
                                                        scalar2=EPS, op0=ALU.mult, op1=ALU.add), r=[pSSf], w=[rs8f])
                    P_("pool", lambda: pool.tensor_tensor(out=rs8f[0:PG, 0:ncol8], in0=rs8f[0:PG, 0:ncol8],
                                                          in1=negh[0:PG, 0:ncol8], op=ALU.pow), r=[rs8f, negh], w=[rs8f])
                    for b in range(2):
                        rw = rawq if b == 0 else rawk
                        for ti, i in enumerate(tiles):
                            P = tP(i)
                            lc = tcol(i)[0] - c0
                            dgb = dgf[(b * GT + ti) % 2]
                            P_("dve", lambda: dve.tensor_scalar(out=dgb[0:P, 0:P], in0=ident_f[0:P, 0:P],
                                                                scalar1=rs8f[0:P, b * GT + ti:b * GT + ti + 1], scalar2=None,
                                                                op0=ALU.mult), r=[ident_f, rs8f], w=[dgb])
                            P_("pe", lambda: pe.matmul(pUf[:, lc:lc + P], ones_f[0:P, :], dgb[0:P, 0:P], start=True, stop=True),
                               r=[ones_f, dgb], w=[pUf])
                        if b == 0:
                            P_("dve", lambda: dve.scalar_tensor_tensor(out=qTf[par][:, 0:n], in0=rw[:, 0:n], scalar=ncs[:, 1:2],
                                                                       in1=pUf[:, 0:n], op0=ALU.mult, op1=ALU.mult),
                               r=[rw, ncs, pUf], w=[qTf[par]])
                        else:
                            P_("dve", lambda: dve.scalar_tensor_tensor(out=kTf[hp_][:, c0:c1], in0=rw[:, 0:n], scalar=ncols[:, 2:3],
                                                                       in1=pUf[:, 0:n], op0=ALU.mult, op1=ALU.mult),
                               r=[rw, ncols, pUf], w=[kTf[hp_]])
                        yield
                    for ti, i in enumerate(tiles):
                        P = tP(i)
                        lc = tcol(i)[0] - c0
                        P_("pe", lambda: pe.transpose(pTf[0:P, ti, :], vTf[:, lc:lc + P], ident_b[:]), r=[vTf, ident_b], w=[pTf],
                           inc=(ti == len(tiles) - 1))
                    for ti, i in enumerate(tiles):
                        P = tP(i)
                        P_("act", lambda: act.copy(out=vext[hp_][0:P, i, 0:128], in_=pTf[0:P, ti, :]), r=[pTf], w=[vext[hp_]])
                    yield

                def fox_attn(h, g, kf):
                    c0, c1 = gcol(g)
                    par = kf % 2
                    hp_ = h % 2
                    tiles = gtiles(g)
                    mt = mts[(HG + h) % 2]
                    kTh, veh = kTf[hp_], vext[hp_]
                    nS = [0]
                    for ti, i in enumerate(tiles):
                        P = tP(i)
                        lc = tcol(i)[0] - c0
                        tc0, tc1 = tcol(i)
                        qt = qTf[par][:, lc:lc + P]
                        blocks = list(range(i)) + ["d"]
                        slot = {}

                        def emit_s(bi):
                            j = blocks[bi]
                            psj = pS[nS[0] % 2]
                            pts = pTs[nS[0] % 4]
                            nS[0] += 1
                            slot[bi] = (psj, pts)
                            if j == "d":
                                afb = Af[ti % 2]
                                P_("dve", lambda: dve.tensor_scalar(out=afb[0:P, 0:P], in0=mask_le[0:P, 0:P],
                                                                    scalar1=logf[0:P, i, h:h + 1], scalar2=None, op0=ALU.mult),
                                   r=[mask_le, logf], w=[afb])
                                P_("pe", lambda: pe.matmul(psj[0:P, 0:P], kTh[:, tc0:tc1], qt, start=True, stop=False),
                                   r=[kTh, qTf[par]], w=[psj], inc=False)
                                P_("pe", lambda: pe.matmul(psj[0:P, 0:P], bst[0:P, 0:P], afb[0:P, 0:P], start=False, stop=False),
                                   r=[bst, afb], w=[psj], inc=False)
                                P_("pe", lambda: pe.matmul(psj[0:P, 0:P], ident_f[0:P, 0:P], mneg[0:P, 0:P], start=False, stop=True),
                                   r=[ident_f, mneg], w=[psj])
                            else:
                                Pj = tP(j)
                                jc0, jc1 = tcol(j)
                                P_("pe", lambda: pe.matmul(psj[0:Pj, 0:P], kTh[:, jc0:jc1], qt, start=True, stop=True),
                                   r=[kTh, qTf[par]], w=[psj])

                        def emit_pv(bi):
                            j = blocks[bi]
                            psj, pts = slot[bi]
                            if j == "d":
                                P_("act", lambda: act.activation(out=pts[0:P, 0:P], in_=psj[0:P, 0:P], func=AF.Exp), r=[psj], w=[pts])
                                P_("pe", lambda: pe.matmul(pO[0:P, 1, 0:129], pts[0:P, 0:P], veh[0:P, i, 0:129], start=True,
                                                           stop=True), r=[pts, veh], w=[pO])
                            else:
                                Pj = tP(j)
                                P_("act", lambda: act.activation(out=pts[0:Pj, 0:P], in_=psj[0:Pj, 0:P], func=AF.Exp,
                                                                 bias=biasm[0:Pj, h, i, j:j + 1]), r=[psj, biasm], w=[pts])
                                P_("pe", lambda: pe.matmul(pO[0:P, 0, 0:129], pts[0:Pj, 0:P], veh[0:Pj, j, 0:129], start=(j == 0),
                                                           stop=(j == i - 1)), r=[pts, veh], w=[pO], inc=(j == i - 1))

                        nb = len(blocks)
                        emit_s(0)
                        if nb > 1:
                            emit_s(1)
                        for bi in range(nb):
                            emit_pv(bi)
                            if bi + 2 < nb:
                                emit_s(bi + 2)
                            if bi % 3 == 2:
                                yield
                        if i > 0:
                            P_("act", lambda: act.activation(out=tof[0:P, 0:129], in_=pO[0:P, 0, 0:129], func=AF.Copy,
                                                             scale=afac[0:P, i, h:h + 1]), r=[pO, afac], w=[tof])
                            P_("dve", lambda: dve.tensor_tensor(out=of[0:P, 0:129], in0=pO[0:P, 1, 0:129], in1=tof[0:P, 0:129],
                                                                op=ALU.add), r=[pO, tof], w=[of])
                        else:
                            P_("dve", lambda: dve.tensor_copy(out=of[0:P, 0:129], in_=pO[0:P, 1, 0:129]), r=[pO], w=[of])
                        P_("dve", lambda: dve.reciprocal(out=rinv[0:P, :], in_=of[0:P, 128:129]), r=[of], w=[rinv])
                        P_("act", lambda: act.activation(out=onf[0:P, :], in_=of[0:P, 0:128], func=AF.Copy, scale=rinv[0:P, 0:1]),
                           r=[of, rinv], w=[onf])
                        P_("pe", lambda: pe.transpose(pTf[:, GT, 0:P], onf[0:P, :], ident_b[0:P, 0:P]), r=[onf, ident_b], w=[pTf])
                        P_("dve", lambda: dve.scalar_tensor_tensor(out=mt[:, tc0:tc1], in0=pTf[:, GT, 0:P], scalar=0.5,
                                                                   in1=sgT[par][:, lc:lc + P], op0=ALU.mult, op1=ALU.mult),
                           r=[pTf, sgT[par]], w=[mt])
                        yield
                    if g == NG - 1:
                        u = HG + h
                        S.dma("sp", mscr[u], mt[:], dms[u % 2], reads=[mt])
                        if mdbg is not None:
                            S.dma("sp", mdbg[u], mt[:], dms[u % 2], reads=[mt])

                for vx in vext:
                    P_("pool", lambda: pool.memset(vx[:], 1.0), w=[vx])
                P_("dve", lambda: dve.memset(pSSf[:], 1.0), w=[pSSf])
                flat = [(h, g) for h in range(HF) for g in range(NG)]
                for _ in fox_front(flat[0][0], flat[0][1], 0):
                    pass
                for kf in range(len(flat)):
                    fr = fox_front(flat[kf + 1][0], flat[kf + 1][1], kf + 1) if kf + 1 < len(flat) else None
                    merge(fr, fox_attn(flat[kf][0], flat[kf][1], kf), 1)
                S.barrier()

        with ExitStack() as pc_:
            wo = sb(pc_, "wo", [128, NU, DM], BF16)
            postw = sb(pc_, "postw", [128, DM])
            ots = [sb(pc_, "ot%d" % i, [128, DM]) for i in range(2)]
            xrs = [sb(pc_, "xr%d" % i, [128, DM]) for i in range(2)]
            junkc = sb(pc_, "junkc", [128, 512], BF16)
            ss4 = sb(pc_, "ss4", [128, 8])
            rsc = sb(pc_, "rsc", [128, 2])
            pob = [ps(pc_, "pob%d" % i, [128, 512]) for i in range(2)]
            dwo = S.new_dma_sem("dwo")
            dmt = S.new_dma_sem("dmt")
            dxr = [S.new_dma_sem("dxr%d" % i) for i in range(2)]
            dst = [S.new_dma_sem("dst%d" % i) for i in range(2)]
            mT = xnT
            for u in range(NU):
                S.dma("sp", mT[:, u, :], mscr[u], dmt, writes=[mT])
            mT.d.w = (dmt["sem"], dmt["val"])
            S.dma("sp", postw[:], postw_d, S.new_dma_sem("dpostw"), writes=[postw])
            NQ = max(1, DM // 512)
            for q in range(NU):
                S.dma("pool", wo[:, q, :], wo_d[:, q, :], dwo, writes=[wo])
            wo.d.w = (dwo["sem"], dwo["val"])
            ncg = (DM + 511) // 512
            nmm = 0
            for i in range(1, NTI):
                tc0, tc1 = tcol(i)
                ot, xr = ots[i % 2], xrs[i % 2]
                S.dma("sp", xr[:], xin[(i - 1) * 128:i * 128, :], dxr[i % 2], writes=[xr])
                for cgi in range(ncg):
                    n0 = cgi * 512
                    nn = min(512, DM - n0)
                    pb_ = pob[nmm % 2]
                    nmm += 1
                    for k in range(NU):
                        P_("pe", lambda: pe.matmul(pb_[:, 0:nn], mT[:, k, tc0:tc1], wo[:, k, n0:n0 + nn], start=(k == 0),
                                                   stop=(k == NU - 1)), r=[mT, wo], w=[pb_], inc=(k == NU - 1))
                    P_("act", lambda: act.copy(out=ot[:, n0:n0 + nn], in_=pb_[:, 0:nn]), r=[pb_], w=[ot])
                    P_("act", lambda: act.activation(out=junkc[:, 0:nn], in_=pb_[:, 0:nn], func=AF.Square,
                                                     accum_out=ss4[:, cgi:cgi + 1]), r=[pb_], w=[junkc, ss4])
                P_("dve", lambda: dve.reduce_sum(out=rsc[:, 0:1], in_=ss4[:, 0:ncg], axis=mybir.AxisListType.X), r=[ss4], w=[rsc])
                P_("dve", lambda: dve.tensor_scalar(out=rsc[:, 1:2], in0=rsc[:, 0:1], scalar1=1.0 / DM, scalar2=EPS, op0=ALU.mult,
                                                    op1=ALU.add), r=[rsc], w=[rsc])
                P_("pool", lambda: pool.tensor_tensor(out=rsc[:, 1:2], in0=rsc[:, 1:2], in1=negh[:, 0:1], op=ALU.pow),
                   r=[rsc, negh], w=[rsc])
                P_("dve", lambda: dve.scalar_tensor_tensor(out=ot[:], in0=ot[:], scalar=rsc[:, 1:2], in1=postw[:], op0=ALU.mult,
                                                           op1=ALU.mult), r=[ot, rsc, postw], w=[ot])
                P_("pool", lambda: pool.tensor_tensor(out=ot[:], in0=ot[:], in1=xr[:], op=ALU.add), r=[ot, xr], w=[ot])
                S.dma("sp", yout[(i - 1) * 128:i * 128, :], ot[:], dst[i % 2], reads=[ot])
            for d in dst:
                if d["val"] > 0:
                    S.eng["sp"].wait_ge(d["sem"], d["val"])
            S.barrier()
    return nc


def prep_inputs(cfg, x, meta_tokens, pre_norm_w, w_in, conv_w, a_log, dt_bias, gdn_norm_w, fox_q_norm_w,
                fox_k_norm_w, fox_f_bias, w_out, post_norm_w):
    DM, NT, HG, HF = cfg["DM"], cfg["NT"], cfg["HG"], cfg["HF"]
    KC = DM // 128
    NU = HG + HF
    GWd, FWd = HG * 128, HF * 128
    w = np.asarray(w_in[0], np.float32)
    g_off = [0, GWd, 2 * GWd, 3 * GWd]
    sm_off = 4 * GWd
    f_base = 4 * GWd + 2 * HG
    f_off = [f_base, f_base + FWd, f_base + 2 * FWd, f_base + 3 * FWd]
    ff_off = f_base + 4 * FWd
    wu = np.empty((NU, 4, 128, KC, 128), np.float32)
    for u in range(NU):
        for b in range(4):
            c = (g_off[b] + u * 128) if u < HG else (f_off[b] + (u - HG) * 128)
            wu[u, b] = w[:, c:c + 128].reshape(KC, 128, 128).transpose(1, 0, 2)
    cols = list(range(sm_off, sm_off + 2 * HG)) + list(range(ff_off, ff_off + HF))
    wsm = np.ascontiguousarray(w[:, cols].reshape(KC, 128, len(cols)).transpose(1, 0, 2))
    cw = np.asarray(conv_w[0], np.float32)
    convw = np.empty((128, HG, 3, 4), np.float32)
    for h in range(HG):
        for b in range(3):
            convw[:, h, b, :] = cw[b * GWd + h * 128:b * GWd + (h + 1) * 128, :]
    convw = convw.reshape(128, HG * 12)
    hp = np.concatenate([np.asarray(a_log[0]), np.asarray(dt_bias[0]), np.asarray(fox_f_bias[0])]).astype(np.float32)
    hp = np.ascontiguousarray(np.broadcast_to(hp[None, :], (128, hp.shape[0])))
    ncols = np.stack([np.asarray(gdn_norm_w[0]), np.asarray(fox_q_norm_w[0]), np.asarray(fox_k_norm_w[0])], axis=1).astype(np.float32)
    prew = np.ascontiguousarray(np.broadcast_to(np.asarray(pre_norm_w[0], np.float32)[None, :], (128, DM)))
    postw = np.ascontiguousarray(np.broadcast_to(np.asarray(post_norm_w[0], np.float32)[None, :], (128, DM)))
    wo = np.ascontiguousarray(np.asarray(w_out[0], np.float32).reshape(NU, 128, DM).transpose(1, 0, 2))
    shared = dict(meta=np.ascontiguousarray(np.asarray(meta_tokens, np.float32)), wu=wu, wsm=wsm, convw=convw, hp=hp,
                  ncols=np.ascontiguousarray(ncols), prew=prew, postw=postw, wo=wo)
    x = np.asarray(x, np.float32)
    return [dict(shared, xin=np.ascontiguousarray(x[b])) for b in range(x.shape[0])]


def kernel(**inputs):
    cfg = FULL_CFG
    in_maps = prep_inputs(cfg, **inputs)
    nc = build(cfg)
    res = run_bass_kernel_spmd(nc, in_maps, core_ids=list(range(len(in_maps))))
    return np.stack([np.asarray(r["y"]) for r in res.results], axis=0).astype(np.float32)
```

```python
import numpy as np
from contextlib import ExitStack
import concourse.bass as bass
import concourse.mybir as mybir
from concourse.bass_utils import run_bass_kernel_spmd

F32 = mybir.dt.float32
BF16 = mybir.dt.bfloat16
AF = mybir.ActivationFunctionType
ALU = mybir.AluOpType
EPS = 1e-6
NEG = -30000.0

FULL_CFG = dict(DM=2048, NT=16, GT=4, HG=8, HF=8)


class Dep:
    __slots__ = ("w", "rs")

    def __init__(self):
        self.w = None
        self.rs = []


class Buf:
    def __init__(self, t):
        self.t = t
        self.d = Dep()

    def __getitem__(self, k):
        return self.t[k]


class Sched:
    def __init__(self, nc, ctx):
        self.nc = nc
        self.ctx = ctx
        self.eng = {"pe": nc.tensor, "act": nc.scalar, "dve": nc.vector, "pool": nc.gpsimd, "sp": nc.sync}
        self.sem = {k: ctx.enter_context(nc.semaphore("s_" + k)) for k in self.eng}
        self.cnt = {k: 0 for k in self.eng}
        self.seen = {k: {} for k in self.eng}
        self.dsems = []

    def new_dma_sem(self, name):
        d = {"sem": self.ctx.enter_context(self.nc.semaphore(name)), "val": 0}
        self.dsems.append(d)
        return d

    def _wait(self, e, tok):
        sem, val = tok
        if e == "pe" and sem.name == "s_pe":
            return
        if self.seen[e].get(sem.name, 0) >= val:
            return
        self.eng[e].wait_ge(sem, val)
        self.seen[e][sem.name] = val

    def _waits(self, e, reads, writes):
        best = {}
        for b in reads:
            if b.d.w is not None:
                t = b.d.w
                if t[0].name not in best or best[t[0].name][1] < t[1]:
                    best[t[0].name] = t
        for b in writes:
            for t in ([b.d.w] if b.d.w is not None else []) + b.d.rs:
                if t[0].name not in best or best[t[0].name][1] < t[1]:
                    best[t[0].name] = t
        for t in best.values():
            self._wait(e, t)

    def op(self, e, fn, reads=(), writes=(), inc=True):
        self._waits(e, reads, writes)
        ins = fn()
        tok = (self.sem[e], self.cnt[e] + 1)
        if inc:
            ins.then_inc(self.sem[e], 1)
            self.cnt[e] += 1
        for b in reads:
            b.d.rs.append(tok)
            if len(b.d.rs) > 64:
                b.d.rs = self._compact(b.d.rs)
        for b in writes:
            b.d.w = tok
            b.d.rs = []
        return ins

    @staticmethod
    def _compact(rs):
        best = {}
        for t in rs:
            if t[0].name not in best or best[t[0].name][1] < t[1]:
                best[t[0].name] = t
        return list(best.values())

    def dma(self, e, out, in_, dsem, reads=(), writes=()):
        self._waits(e, reads, writes)
        ins = self.eng[e].dma_start(out=out, in_=in_)
        ins.then_inc(dsem["sem"], 16)
        dsem["val"] += 16
        tok = (dsem["sem"], dsem["val"])
        for b in reads:
            b.d.rs.append(tok)
        for b in writes:
            b.d.w = tok
            b.d.rs = []
        return ins

    def barrier(self):
        toks = [(self.sem[k], self.cnt[k]) for k in self.eng if self.cnt[k] > 0]
        toks += [(d["sem"], d["val"]) for d in self.dsems if d["val"] > 0]
        for e in self.eng:
            for t in toks:
                if t[0].name == "s_" + e:
                    continue
                self._wait(e, t)
        for e in ("act", "dve", "pool"):
            if self.cnt[e] > 0 and self.seen[e].get("s_" + e, 0) < self.cnt[e]:
                self.eng[e].wait_ge(self.sem[e], self.cnt[e])
                self.seen[e]["s_" + e] = self.cnt[e]


def build(cfg):
    DM, NT, GT, HG, HF = cfg["DM"], cfg["NT"], cfg["GT"], cfg["HG"], cfg["HF"]
    KC = DM // 128
    NU = HG + HF
    L = 16 + NT * 128
    NTI = NT + 1
    NG = NT // GT + 1
    NS = 2 * HG + HF
    GW = GT * 128
    HM = max(HG, HF)
    NDT = F32

    def tcol(i):
        return (0, 16) if i == 0 else (16 + (i - 1) * 128, 16 + i * 128)

    def tP(i):
        return 16 if i == 0 else 128

    def gtiles(g):
        return [0] if g == 0 else list(range((g - 1) * GT + 1, g * GT + 1))

    def gcol(g):
        ts = gtiles(g)
        return tcol(ts[0])[0], tcol(ts[-1])[1]

    nc = bass.Bass("TRN2", target_bir_lowering=False)
    xin = nc.dram_tensor("xin", [NT * 128, DM], F32, kind="ExternalInput").ap()
    meta = nc.dram_tensor("meta", [16, DM], F32, kind="ExternalInput").ap()
    wu = nc.dram_tensor("wu", [NU, 4, 128, KC, 128], F32, kind="ExternalInput").ap()
    wsm_d = nc.dram_tensor("wsm", [128, KC, NS], F32, kind="ExternalInput").ap()
    convw_d = nc.dram_tensor("convw", [128, HG * 12], F32, kind="ExternalInput").ap()
    hp_d = nc.dram_tensor("hp", [128, NS], F32, kind="ExternalInput").ap()
    ncols_d = nc.dram_tensor("ncols", [128, 3], F32, kind="ExternalInput").ap()
    prew_d = nc.dram_tensor("prew", [128, DM], F32, kind="ExternalInput").ap()
    postw_d = nc.dram_tensor("postw", [128, DM], F32, kind="ExternalInput").ap()
    wo_d = nc.dram_tensor("wo", [128, NU, DM], F32, kind="ExternalInput").ap()
    yout = nc.dram_tensor("y", [NT * 128, DM], F32, kind="ExternalOutput").ap()
    mscr = nc.dram_tensor("mscr", [NU, 128, L], BF16).ap()
    mdbg = nc.dram_tensor("mdbg", [NU, 128, L], BF16, kind="ExternalOutput").ap() if cfg.get("DBG") else None

    with ExitStack() as ctx:
        ctx.enter_context(nc.Block())
        S = Sched(nc, ctx)
        pe, act, dve, pool = nc.tensor, nc.scalar, nc.vector, nc.gpsimd

        def sb(c, name, shape, dt=F32):
            return Buf(c.enter_context(nc.sbuf_tensor("sb_" + name, shape, dt)))

        def ps(c, name, shape, dt=F32):
            return Buf(c.enter_context(nc.psum_tensor("ps_" + name, shape, dt)))

        xnT = sb(ctx, "xnT", [128, max(KC, NU), L], BF16)
        ones_f = sb(ctx, "ones_f", [128, 128])
        zeros_f = sb(ctx, "zeros_f", [128, 128])
        ident_f = sb(ctx, "ident_f", [128, 128])
        ident_b = sb(ctx, "ident_b", [128, 128], BF16)
        ones_b = sb(ctx, "ones_b", [128, 128], BF16)
        ident2 = sb(ctx, "ident2", [128, 2, 128], BF16)
        mneg_b = sb(ctx, "mneg_b", [128, 128], BF16)
        mask_le = sb(ctx, "mask_le", [128, 128])
        bst = sb(ctx, "bst", [128, 128])
        mneg = sb(ctx, "mneg", [128, 128])
        mask_gt = sb(ctx, "mask_gt", [128, 128])
        negh = sb(ctx, "negh", [128, 512])
        convw = sb(ctx, "convw", [128, HG * 12])
        hp = sb(ctx, "hp", [128, NS])
        ncols = sb(ctx, "ncols", [128, 3])
        ncs = sb(ctx, "ncs", [128, 2])
        wsm = sb(ctx, "wsm", [128, KC, NS], BF16)
        sm = sb(ctx, "sm", [128, NTI, NS])
        beta = sb(ctx, "beta", [128, NTI, HG])
        gg = sb(ctx, "gg", [128, NTI, HG])
        nbeta = sb(ctx, "nbeta", [128, NTI, HG])
        bEl = sb(ctx, "bEl", [128, NTI, HG])
        eG = sb(ctx, "eG", [128, NTI, HG])
        neG = sb(ctx, "neG", [128, NTI, HG])
        eGl = sb(ctx, "eGl", [128, NTI, HG])
        dec = sb(ctx, "dec", [128, NTI, HG])
        logf = sb(ctx, "logf", [128, NTI, HF])
        cl = sb(ctx, "cl", [128, NTI, HF])
        rbb = sb(ctx, "rbb", [128, NTI, HF])
        cab = sb(ctx, "cab", [128, NTI, HF])
        afac = sb(ctx, "afac", [128, NTI, HF])
        biasm = sb(ctx, "biasm", [128, HF, NTI, NTI])

        def P_(e, fn, r=(), w=(), inc=True):
            return S.op(e, fn, reads=r, writes=w, inc=inc)

        P_("pool", lambda: pool.memset(ones_f[:], 1.0), w=[ones_f])
        P_("pool", lambda: pool.memset(zeros_f[:], 0.0), w=[zeros_f])
        P_("pool", lambda: pool.memset(ones_b[:], 1.0), w=[ones_b])
        P_("pool", lambda: pool.memset(negh[:], -0.5), w=[negh])
        P_("pool", lambda: pool.affine_select(out=ident_f[:], in_=ones_f[:], pattern=[[-1, 128]], compare_op=ALU.is_equal,
                                             fill=0.0, base=0, channel_multiplier=1), r=[ones_f], w=[ident_f])
        P_("pool", lambda: pool.tensor_copy(out=ident_b[:], in_=ident_f[:]), r=[ident_f], w=[ident_b])
        P_("pool", lambda: pool.affine_select(out=mask_le[:], in_=ones_f[:], pattern=[[1, 128]], compare_op=ALU.is_ge,
                                             fill=0.0, base=0, channel_multiplier=-1), r=[ones_f], w=[mask_le])
        P_("pool", lambda: pool.affine_select(out=bst[:], in_=ones_f[:], pattern=[[-1, 128]], compare_op=ALU.is_gt,
                                             fill=0.0, base=0, channel_multiplier=1), r=[ones_f], w=[bst])
        P_("pool", lambda: pool.affine_select(out=mneg[:], in_=zeros_f[:], pattern=[[1, 128]], compare_op=ALU.is_ge,
                                             fill=NEG, base=0, channel_multiplier=-1), r=[zeros_f], w=[mneg])
        P_("pool", lambda: pool.affine_select(out=mask_gt[:], in_=ones_f[:], pattern=[[1, 128]], compare_op=ALU.is_gt,
                                             fill=0.0, base=0, channel_multiplier=-1), r=[ones_f], w=[mask_gt])

        P_("pool", lambda: pool.tensor_copy(out=ident2[:, 0, :], in_=ident_f[:]), r=[ident_f], w=[ident2])
        P_("pool", lambda: pool.tensor_copy(out=ident2[:, 1, :], in_=ident_f[:]), r=[ident_f], w=[ident2])
        P_("pool", lambda: pool.tensor_copy(out=mneg_b[:], in_=mneg[:]), r=[mneg], w=[mneg_b])

        dpar = S.new_dma_sem("dpar")
        S.dma("sp", convw[:], convw_d, dpar, writes=[convw])
        S.dma("sp", hp[:], hp_d, dpar, writes=[hp])
        S.dma("sp", ncols[:], ncols_d, dpar, writes=[ncols])
        dwsm = S.new_dma_sem("dwsm")
        S.dma("pool", wsm[:], wsm_d, dwsm, writes=[wsm])
        for b in (convw, hp, ncols):
            b.d.w = (dpar["sem"], dpar["val"])
        P_("dve", lambda: dve.tensor_scalar(out=ncs[:, 0:1], in0=ncols[:, 0:1], scalar1=0.5, scalar2=None, op0=ALU.mult),
           r=[ncols], w=[ncs])
        P_("dve", lambda: dve.tensor_scalar(out=ncs[:, 1:2], in0=ncols[:, 1:2], scalar1=float(128 ** -0.5), scalar2=None,
                                            op0=ALU.mult), r=[ncols], w=[ncs])

        with ExitStack() as pa:
            prew = sb(pa, "prew", [128, DM])
            xts = [sb(pa, "xt%d" % i, [128, DM]) for i in range(2)]
            xns = [sb(pa, "xn%d" % i, [128, DM], BF16) for i in range(2)]
            junk = sb(pa, "junkA", [128, DM], BF16)
            ssA = sb(pa, "ssA", [128, NTI])
            rsA = sb(pa, "rsA", [128, NTI])
            ptr = [ps(pa, "ptrA%d" % i, [128, 8, 128], BF16) for i in range(2)]
            psm = ps(pa, "psmA", [128, 512])
            dxs = [S.new_dma_sem("dx%d" % i) for i in range(2)]
            S.dma("sp", prew[:], prew_d, S.new_dma_sem("dprew"), writes=[prew])
            P_("dve", lambda: dve.memset(sm[:], 0.0), w=[sm])
            P_("dve", lambda: dve.memset(psm[:], 0.0), w=[psm])
            nev = 0
            for i in range(NTI):
                P = tP(i)
                c0, c1 = tcol(i)
                xt, xn = xts[i % 2], xns[i % 2]
                src = meta if i == 0 else xin[(i - 1) * 128:i * 128, :]
                S.dma("sp", xt[0:P, :], src, dxs[i % 2], writes=[xt])
                P_("act", lambda: act.activation(out=junk[0:P, :], in_=xt[0:P, :], func=AF.Square,
                                                 accum_out=ssA[0:P, i:i + 1]), r=[xt], w=[junk, ssA])
                P_("dve", lambda: dve.tensor_scalar(out=rsA[0:P, i:i + 1], in0=ssA[0:P, i:i + 1], scalar1=1.0 / DM,
                                                    scalar2=EPS, op0=ALU.mult, op1=ALU.add), r=[ssA], w=[rsA])
                P_("pool", lambda: pool.tensor_tensor(out=rsA[0:P, i:i + 1], in0=rsA[0:P, i:i + 1], in1=negh[0:P, 0:1],
                                                      op=ALU.pow), r=[rsA, negh], w=[rsA])
                P_("dve", lambda: dve.scalar_tensor_tensor(out=xn[0:P, :], in0=xt[0:P, :], scalar=rsA[0:P, i:i + 1],
                                                           in1=prew[0:P, :], op0=ALU.mult, op1=ALU.mult),
                   r=[xt, rsA, prew], w=[xn])
                for k0 in range(0, KC, 8):
                    kn = min(8, KC - k0)
                    pt = ptr[nev % 2]
                    for k in range(kn):
                        P_("pe", lambda: pe.transpose(pt[:, k, 0:P], xn[0:P, (k0 + k) * 128:(k0 + k + 1) * 128],
                                                      ident_b[0:P, 0:P]), r=[xn, ident_b], w=[pt], inc=(k == kn - 1))
                    ev = "act" if nev % 2 == 0 else "dve"
                    if ev == "act":
                        P_("act", lambda: act.copy(out=xnT[:, k0:k0 + kn, c0:c1], in_=pt[:, 0:kn, 0:P]), r=[pt], w=[xnT])
                    else:
                        P_("dve", lambda: dve.tensor_copy(out=xnT[:, k0:k0 + kn, c0:c1], in_=pt[:, 0:kn, 0:P]), r=[pt], w=[xnT])
                    nev += 1
                for k in range(KC):
                    P_("pe", lambda: pe.matmul(psm[0:P, i * NS:(i + 1) * NS], xnT[:, k, c0:c1], wsm[:, k, :],
                                               start=(k == 0), stop=(k == KC - 1)), r=[xnT, wsm], w=[psm], inc=(k == KC - 1))
            P_("dve", lambda: dve.tensor_copy(out=sm[:].rearrange("p t s -> p (t s)"), in_=psm[:, 0:NTI * NS]), r=[psm], w=[sm])

            t1 = sb(pa, "t1s", [128, NTI, HM])
            t2 = sb(pa, "t2s", [128, NTI, HM])
            t3 = sb(pa, "t3s", [128, NTI, HM])
            nea = sb(pa, "nea", [128, HG])
            gc = sb(pa, "gc", [128, NTI, HG])
            glb = sb(pa, "glb", [128, NTI, HG])
            tot = sb(pa, "tot", [128, NTI, HF])
            pcs = ps(pa, "pcs", [128, 512])
            pcf = ps(pa, "pcf", [128, 512])

            def bc(ap, h):
                return ap.unsqueeze(1).to_broadcast([128, NTI, h])

            P_("act", lambda: act.activation(out=t1[:, :, 0:HG], in_=sm[:, :, 0:HG], func=AF.Tanh, scale=0.5), r=[sm], w=[t1])
            P_("dve", lambda: dve.tensor_scalar(out=beta[:], in0=t1[:, :, 0:HG], scalar1=0.5, scalar2=0.5, op0=ALU.mult,
                                                op1=ALU.add), r=[t1], w=[beta])
            P_("act", lambda: act.activation(out=nea[:], in_=hp[:, 0:HG], func=AF.Exp), r=[hp], w=[nea])
            P_("dve", lambda: dve.tensor_scalar(out=nea[:], in0=nea[:], scalar1=-1.0, scalar2=None, op0=ALU.mult), r=[nea], w=[nea])
            P_("dve", lambda: dve.tensor_tensor(out=t1[:, :, 0:HG], in0=sm[:, :, HG:2 * HG], in1=bc(hp[:, HG:2 * HG], HG),
                                                op=ALU.add), r=[sm, hp], w=[t1])
            P_("act", lambda: act.activation(out=t2[:, :, 0:HG], in_=t1[:, :, 0:HG], func=AF.Abs), r=[t1], w=[t2])
            P_("act", lambda: act.activation(out=t2[:, :, 0:HG], in_=t2[:, :, 0:HG], func=AF.Exp, scale=-1.0), r=[t2], w=[t2])
            P_("act", lambda: act.activation(out=t2[:, :, 0:HG], in_=t2[:, :, 0:HG], func=AF.Ln, bias=1.0), r=[t2], w=[t2])
            P_("dve", lambda: dve.scalar_tensor_tensor(out=t3[:, :, 0:HG], in0=t1[:, :, 0:HG], scalar=0.0, in1=t2[:, :, 0:HG],
                                                       op0=ALU.max, op1=ALU.add), r=[t1, t2], w=[t3])
            P_("dve", lambda: dve.tensor_tensor(out=gg[:], in0=t3[:, :, 0:HG], in1=bc(nea[:], HG), op=ALU.mult),
               r=[t3, nea], w=[gg])
            P_("dve", lambda: dve.tensor_tensor(out=t1[:, :, 0:HF], in0=sm[:, :, 2 * HG:NS], in1=bc(hp[:, 2 * HG:NS], HF),
                                                op=ALU.add), r=[sm, hp], w=[t1])
            P_("act", lambda: act.activation(out=t2[:, :, 0:HF], in_=t1[:, :, 0:HF], func=AF.Abs), r=[t1], w=[t2])
            P_("act", lambda: act.activation(out=t2[:, :, 0:HF], in_=t2[:, :, 0:HF], func=AF.Exp, scale=-1.0), r=[t2], w=[t2])
            P_("act", lambda: act.activation(out=t2[:, :, 0:HF], in_=t2[:, :, 0:HF], func=AF.Ln, bias=1.0), r=[t2], w=[t2])
            P_("dve", lambda: dve.tensor_scalar(out=t1[:, :, 0:HF], in0=t1[:, :, 0:HF], scalar1=-1.0, scalar2=None, op0=ALU.mult),
               r=[t1], w=[t1])
            P_("dve", lambda: dve.scalar_tensor_tensor(out=t3[:, :, 0:HF], in0=t1[:, :, 0:HF], scalar=0.0, in1=t2[:, :, 0:HF],
                                                       op0=ALU.max, op1=ALU.add), r=[t1, t2], w=[t3])
            P_("dve", lambda: dve.tensor_scalar(out=logf[:], in0=t3[:, :, 0:HF], scalar1=-1.0, scalar2=None, op0=ALU.mult),
               r=[t3], w=[logf])

            P_("dve", lambda: dve.memset(pcs[:], 0.0), w=[pcs])
            P_("dve", lambda: dve.memset(pcf[:], 0.0), w=[pcf])
            for i in range(NTI):
                P = tP(i)
                o = i * 2 * HG
                o2 = i * 2 * HF
                P_("pe", lambda: pe.matmul(pcs[0:P, o:o + HG], mask_le[0:P, 0:P], gg[0:P, i, :], start=True, stop=True),
                   r=[mask_le, gg], w=[pcs], inc=False)
                P_("pe", lambda: pe.matmul(pcs[:, o + HG:o + 2 * HG], ones_f[0:P, :], gg[0:P, i, :], start=True, stop=True),
                   r=[ones_f, gg], w=[pcs], inc=False)
                P_("pe", lambda: pe.matmul(pcf[0:P, o2:o2 + HF], mask_le[0:P, 0:P], logf[0:P, i, :],
                                           start=True, stop=True), r=[mask_le, logf], w=[pcf], inc=False)
                P_("pe", lambda: pe.matmul(pcf[:, o2 + HF:o2 + 2 * HF], ones_f[0:P, :], logf[0:P, i, :], start=True,
                                           stop=True), r=[ones_f, logf], w=[pcf], inc=True)
            pv = pcs[:, 0:NTI * 2 * HG].rearrange("p (t w) -> p t w", w=2 * HG)
            pv2 = pcf[:, 0:NTI * 2 * HF].rearrange("p (t w) -> p t w", w=2 * HF)
            P_("dve", lambda: dve.tensor_copy(out=gc[:], in_=pv[:, :, 0:HG]), r=[pcs], w=[gc])
            P_("dve", lambda: dve.tensor_copy(out=glb[:], in_=pv[:, :, HG:2 * HG]), r=[pcs], w=[glb])
            P_("dve", lambda: dve.tensor_copy(out=cl[:], in_=pv2[:, :, 0:HF]), r=[pcf], w=[cl])
            P_("dve", lambda: dve.tensor_copy(out=tot[:], in_=pv2[:, :, HF:2 * HF]), r=[pcf], w=[tot])
            P_("act", lambda: act.activation(out=eG[:], in_=gc[:], func=AF.Exp), r=[gc], w=[eG])
            P_("dve", lambda: dve.tensor_scalar(out=neG[:], in0=eG[:], scalar1=-1.0, scalar2=None, op0=ALU.mult), r=[eG], w=[neG])
            P_("dve", lambda: dve.tensor_tensor(out=gc[:], in0=glb[:], in1=gc[:], op=ALU.subtract), r=[glb, gc], w=[gc])
            P_("act", lambda: act.activation(out=eGl[:], in_=gc[:], func=AF.Exp), r=[gc], w=[eGl])
            P_("act", lambda: act.activation(out=dec[:], in_=glb[:], func=AF.Exp), r=[glb], w=[dec])
            P_("dve", lambda: dve.tensor_scalar(out=nbeta[:], in0=beta[:], scalar1=-1.0, scalar2=None, op0=ALU.mult), r=[beta], w=[nbeta])
            P_("dve", lambda: dve.tensor_tensor(out=bEl[:], in0=beta[:], in1=eGl[:], op=ALU.mult), r=[beta, eGl], w=[bEl])
            P_("dve", lambda: dve.memset(rbb[:], 0.0), w=[rbb])
            for i in range(1, NTI):
                P_("dve", lambda: dve.tensor_tensor(out=rbb[:, i, :], in0=rbb[:, i - 1, :], in1=tot[:, i - 1, :], op=ALU.add),
                   r=[rbb, tot], w=[rbb])
            P_("dve", lambda: dve.tensor_tensor(out=cab[:], in0=rbb[:], in1=cl[:], op=ALU.add), r=[rbb, cl], w=[cab])
            P_("act", lambda: act.activation(out=afac[:], in_=cl[:], func=AF.Exp), r=[cl], w=[afac])
            for h in range(HF):
                P_("dve", lambda: dve.tensor_tensor(out=biasm[:, h, :, :],
                                                    in0=rbb[:, :, h].unsqueeze(2).to_broadcast([128, NTI, NTI]),
                                                    in1=cab[:, :, h].unsqueeze(1).to_broadcast([128, NTI, NTI]),
                                                    op=ALU.subtract), r=[rbb, cab], w=[biasm])
            S.barrier()

        with ExitStack() as pb:
            wsl = [[sb(pb, "w%d_%d" % (s, b), [128, KC, 128], BF16) for b in range(4)] for s in range(2)]
            dws = [[S.new_dma_sem("dw%d_%d" % (s, b)) for b in range(4)] for s in range(2)]
            mts = [sb(pb, "mts%d" % i, [128, L], BF16) for i in range(2)]
            dms = [S.new_dma_sem("dm%d" % i) for i in range(2)]
            pbank = [ps(pb, "pbank%d" % i, [128, 512]) for i in range(2)]
            nproj = [0]

            def load_w(u):
                for b in range(4):
                    S.dma("pool", wsl[u % 2][b][:], wu[u, b], dws[u % 2][b], writes=[wsl[u % 2][b]])

            def project(u, b, g):
                c0, c1 = gcol(g)
                n = c1 - c0
                pbk = pbank[nproj[0] % 2]
                nproj[0] += 1
                w = wsl[u % 2][b]
                for k in range(KC):
                    P_("pe", lambda: pe.matmul(pbk[:, 0:n], w[:, k, :], xnT[:, k, c0:c1], start=(k == 0), stop=(k == KC - 1)),
                       r=[w, xnT], w=[pbk], inc=(k == KC - 1))
                return pbk, n

            load_w(0)

            def merge(front, back, ratio):
                fa, ba = front is not None, back is not None
                while fa or ba:
                    if ba:
                        try:
                            next(back)
                        except StopIteration:
                            ba = False
                    for _ in range(ratio if ba else 1):
                        if fa:
                            try:
                                next(front)
                            except StopIteration:
                                fa = False

            with ExitStack() as pg:
                raw = [sb(pg, "raw%d" % b, [128, 3 + GW]) for b in range(3)]
                ycv = [sb(pg, "ycv%d" % b, [128, GW]) for b in range(3)]
                tnh = sb(pg, "tnh", [128, GW])
                sqs = [sb(pg, "sq%d" % i, [128, GW], BF16) for i in range(2)]
                rs8 = sb(pg, "rs8", [128, 2 * GT])
                dg = [sb(pg, "dg%d" % i, [128, 128], BF16) for i in range(2)]
                qT = [sb(pg, "qT%d" % i, [128, GW], BF16) for i in range(2)]
                kT = [sb(pg, "kT%d" % i, [128, GW], BF16) for i in range(2)]
                vT = sb(pg, "vT", [128, GW], BF16)
                szT = [sb(pg, "szT%d" % i, [128, GW], BF16) for i in range(2)]
                ktok = [sb(pg, "ktok%d" % i, [128, GT, 128], BF16) for i in range(2)]
                vtok = [sb(pg, "vtok%d" % i, [128, GT, 128], BF16) for i in range(2)]
                NSL = 2 * GT
                Ysl = [sb(pg, "Y%d" % i, [128, 128], BF16) for i in range(NSL)]
                Asl = [sb(pg, "A%d" % i, [128, 128], BF16) for i in range(NSL)]
                Ag = [sb(pg, "Ag%d" % i, [128, 128]) for i in range(2)]
                Dge = [sb(pg, "Dge%d" % i, [128, 128]) for i in range(2)]
                Dgt = [sb(pg, "Dgt%d" % i, [128, 128]) for i in range(2)]
                NP = [[sb(pg, "NP%d_%d" % (s_, i), [128, 2, 128], BF16) for i in range(2)] for s_ in range(2)]
                PP = [[sb(pg, "PP%d_%d" % (s_, i), [128, 2, 128], BF16) for i in range(2)] for s_ in range(2)]
                NT0 = [sb(pg, "NT0_%d" % s_, [128, 128], BF16) for s_ in range(2)]
                Rn = [sb(pg, "Rn%d" % s_, [128, 128], BF16) for s_ in range(2)]
                Rb = sb(pg, "Rb", [128, 128], BF16)
                vnew = sb(pg, "vnew", [128, 128], BF16)
                vnd = sb(pg, "vnd", [128, 128], BF16)
                t1o = sb(pg, "t1o", [128, 128])
                oo = sb(pg, "oo", [128, 128])
                onb = sb(pg, "onb", [128, 128], BF16)
                junkg = sb(pg, "junkg", [128, 128], BF16)
                sso = sb(pg, "sso", [128, 2])
                Sf = sb(pg, "Sf", [128, 128])
                Sb = sb(pg, "Sb", [128, 128], BF16)
                pT = ps(pg, "pT", [128, 8, 128], BF16)
                pU = ps(pg, "pU", [128, 512])
                pV = ps(pg, "pV", [128, 4, 128])
                pW = [ps(pg, "pW%d" % i, [128, 4, 128]) for i in range(2)]
                pC = ps(pg, "pC", [128, 4, 128])

                def gdn_front(h, g, kf):
                    u = h
                    c0, c1 = gcol(g)
                    n = c1 - c0
                    par = kf % 2
                    tiles = gtiles(g)
                    if g == 0:
                        if u + 1 < NU:
                            load_w(u + 1)
                        for b in range(3):
                            P_("dve", lambda: dve.memset(raw[b][:, 0:3], 0.0), w=[raw[b]])
                    for b in range(3):
                        pbk, _ = project(u, b, g)
                        P_("act", lambda: act.copy(out=raw[b][:, 3:3 + n], in_=pbk[:, 0:n]), r=[pbk], w=[raw[b]])
                        yield
                    pbk, _ = project(u, 3, g)
                    P_("act", lambda: act.activation(out=tnh[:, 0:n], in_=pbk[:, 0:n], func=AF.Tanh, scale=0.5), r=[pbk], w=[tnh])
                    P_("dve", lambda: dve.scalar_tensor_tensor(out=szT[par][:, 0:n], in0=tnh[:, 0:n], scalar=1.0, in1=pbk[:, 0:n],
                                                               op0=ALU.add, op1=ALU.mult), r=[tnh, pbk], w=[szT[par]])
                    yield
                    for b in range(3):
                        cw = lambda j: convw[:, h * 12 + b * 4 + j:h * 12 + b * 4 + j + 1]
                        y = ycv[b]
                        P_("dve", lambda: dve.tensor_scalar(out=y[:, 0:n], in0=raw[b][:, 3:3 + n], scalar1=cw(3), scalar2=None,
                                                            op0=ALU.mult), r=[raw[b], convw], w=[y])
                        for j in (2, 1, 0):
                            P_("dve", lambda: dve.scalar_tensor_tensor(out=y[:, 0:n], in0=raw[b][:, j:j + n], scalar=cw(j),
                                                                       in1=y[:, 0:n], op0=ALU.mult, op1=ALU.add),
                               r=[raw[b], convw, y], w=[y])
                        yield
                        P_("pool", lambda: pool.tensor_copy(out=raw[b][:, 0:3], in_=raw[b][:, n:n + 3]), r=[raw[b]], w=[raw[b]])
                        P_("act", lambda: act.activation(out=tnh[:, 0:n], in_=y[:, 0:n], func=AF.Tanh, scale=0.5), r=[y], w=[tnh])
                        if b < 2:
                            P_("dve", lambda: dve.scalar_tensor_tensor(out=y[:, 0:n], in0=tnh[:, 0:n], scalar=1.0, in1=y[:, 0:n],
                                                                       op0=ALU.add, op1=ALU.mult), r=[tnh, y], w=[y])
                            P_("pool", lambda: pool.tensor_tensor(out=sqs[b][:, 0:n], in0=y[:, 0:n], in1=y[:, 0:n], op=ALU.mult),
                               r=[y], w=[sqs[b]])
                            for ti, i in enumerate(tiles):
                                P = tP(i)
                                lc = tcol(i)[0] - c0
                                P_("pe", lambda: pe.matmul(pV[0:P, 3, b * GT + ti:b * GT + ti + 1], sqs[b][:, lc:lc + P],
                                                           ones_b[:, 0:1], start=True, stop=True), r=[sqs[b], ones_b], w=[pV],
                                   inc=(ti == len(tiles) - 1))
                        else:
                            P_("dve", lambda: dve.scalar_tensor_tensor(out=vT[:, 0:n], in0=tnh[:, 0:n], scalar=1.0, in1=y[:, 0:n],
                                                                       op0=ALU.add, op1=ALU.mult), r=[tnh, y], w=[vT])
                        yield
                    PG = tP(tiles[0])
                    ncol8 = 2 * GT
                    P_("dve", lambda: dve.tensor_scalar(out=rs8[0:PG, 0:ncol8], in0=pV[0:PG, 3, 0:ncol8], scalar1=1.0,
                                                        scalar2=4.0 * EPS, op0=ALU.mult, op1=ALU.add), r=[pV], w=[rs8])
                    P_("pool", lambda: pool.tensor_tensor(out=rs8[0:PG, 0:ncol8], in0=rs8[0:PG, 0:ncol8], in1=negh[0:PG, 0:ncol8],
                                                          op=ALU.pow), r=[rs8, negh], w=[rs8])
                    yield
                    for b in range(2):
                        y = ycv[b]
                        for ti, i in enumerate(tiles):
                            P = tP(i)
                            lc = tcol(i)[0] - c0
                            dgb = dg[(b * GT + ti) % 2]
                            P_("dve", lambda: dve.tensor_scalar(out=dgb[0:P, 0:P], in0=ident_f[0:P, 0:P],
                                                                scalar1=rs8[0:P, b * GT + ti:b * GT + ti + 1], scalar2=None,
                                                                op0=ALU.mult), r=[ident_f, rs8], w=[dgb])
                            P_("pe", lambda: pe.matmul(pU[:, lc:lc + P], ones_b[0:P, :], dgb[0:P, 0:P], start=True, stop=True),
                               r=[ones_b, dgb], w=[pU])
                        dst = qT[par] if b == 0 else kT[par]
                        sc = float(128 ** -0.5) if b == 0 else 1.0
                        P_("dve", lambda: dve.scalar_tensor_tensor(out=dst[:, 0:n], in0=y[:, 0:n], scalar=sc, in1=pU[:, 0:n],
                                                                   op0=ALU.mult, op1=ALU.mult), r=[y, pU], w=[dst])
                        yield
                    for t0 in range(0, len(tiles), 2):
                        pr = tiles[t0:t0 + 2]
                        for q_, i in enumerate(pr):
                            P = tP(i)
                            lc = tcol(i)[0] - c0
                            P_("pe", lambda: pe.transpose(pT[0:P, 2 * q_, :], kT[par][:, lc:lc + P], ident_b[:]),
                               r=[kT[par], ident_b], w=[pT], inc=False)
                            P_("pe", lambda: pe.transpose(pT[0:P, 2 * q_ + 1, :], vT[:, lc:lc + P], ident_b[:]),
                               r=[vT, ident_b], w=[pT], inc=True)
                        for q_, i in enumerate(pr):
                            P = tP(i)
                            ti = t0 + q_
                            P_("act", lambda: act.copy(out=ktok[par][0:P, ti, :], in_=pT[0:P, 2 * q_, :]), r=[pT], w=[ktok[par]])
                            P_("act", lambda: act.activation(out=vtok[par][0:P, ti, :], in_=pT[0:P, 2 * q_ + 1, :], func=AF.Copy,
                                                             scale=0.5), r=[pT], w=[vtok[par]])
                        yield
                    for t0 in range(0, len(tiles), 2):
                        pr = tiles[t0:t0 + 2]
                        for q_, i in enumerate(pr):
                            P = tP(i)
                            ti = t0 + q_
                            lc = tcol(i)[0] - c0
                            sl = par * GT + ti
                            NP0 = NP[q_][0]
                            P_("dve", lambda: dve.tensor_scalar(out=Ag[q_][0:P, 0:P], in0=mask_le[0:P, 0:P],
                                                                scalar1=gg[0:P, i, h:h + 1], scalar2=None, op0=ALU.mult),
                               r=[mask_le, gg], w=[Ag[q_]])
                            P_("pe", lambda: pe.matmul(pV[0:P, 0, 0:P], bst[0:P, 0:P], Ag[q_][0:P, 0:P], start=True, stop=False),
                               r=[bst, Ag[q_]], w=[pV], inc=False)
                            P_("pe", lambda: pe.matmul(pV[0:P, 0, 0:P], ident_b[0:P, 0:P], mneg_b[0:P, 0:P], start=False, stop=True),
                               r=[ident_b, mneg_b], w=[pV], inc=False)
                            P_("pe", lambda: pe.matmul(pV[0:P, 1, 0:P], kT[par][:, lc:lc + P], kT[par][:, lc:lc + P], start=True,
                                                       stop=True), r=[kT[par]], w=[pV], inc=False)
                            P_("pe", lambda: pe.matmul(pV[0:P, 2, 0:P], kT[par][:, lc:lc + P], qT[par][:, lc:lc + P], start=True,
                                                       stop=True), r=[kT[par], qT[par]], w=[pV], inc=True)
                            P_("act", lambda: act.activation(out=Dge[q_][0:P, 0:P], in_=pV[0:P, 0, 0:P], func=AF.Exp),
                               r=[pV], w=[Dge[q_]])
                            P_("dve", lambda: dve.scalar_tensor_tensor(out=Dgt[q_][0:P, 0:P], in0=pV[0:P, 1, 0:P],
                                                                       scalar=nbeta[0:P, i, h:h + 1], in1=Dge[q_][0:P, 0:P],
                                                                       op0=ALU.mult, op1=ALU.mult),
                               r=[pV, nbeta, Dge[q_]], w=[Dgt[q_]])
                            P_("dve", lambda: dve.tensor_tensor(out=Asl[sl][0:P, 0:P], in0=pV[0:P, 2, 0:P], in1=Dge[q_][0:P, 0:P],
                                                                op=ALU.mult), r=[pV, Dge[q_]], w=[Asl[sl]])
                            P_("dve", lambda: dve.tensor_tensor(out=NP0[0:P, 1, 0:P], in0=Dgt[q_][0:P, 0:P], in1=mask_gt[0:P, 0:P],
                                                                op=ALU.mult), r=[Dgt[q_], mask_gt], w=[NP0])
                            P_("pe", lambda: pe.transpose(pT[0:P, q_, 0:P], NP0[0:P, 1, 0:P], ident_b[0:P, 0:P]),
                               r=[NP0, ident_b], w=[pT])
                            P_("act", lambda: act.copy(out=NP0[0:P, 0, 0:P], in_=pT[0:P, q_, 0:P]), r=[pT], w=[NP0])
                            P_("act", lambda: act.copy(out=NT0[q_][0:P, 0:P], in_=pT[0:P, q_, 0:P]), r=[pT], w=[NT0[q_]])
                            P_("dve", lambda: dve.tensor_tensor(out=PP[q_][0][0:P, :, 0:P], in0=NP0[0:P, :, 0:P],
                                                                in1=ident2[0:P, :, 0:P], op=ALU.add), r=[NP0, ident2], w=[PP[q_][0]])
                            yield
                        nlev = 6 if tP(pr[0]) == 128 else 3
                        for lv in range(1, nlev + 1):
                            last = lv == nlev
                            cur = (lv - 1) % 2
                            for q_, i in enumerate(pr):
                                P = tP(i)
                                Nc = NP[q_][cur]
                                pw = pW[q_]
                                P_("pe", lambda: pe.matmul(pw[0:P, 0, 0:P], Nc[0:P, 1, 0:P], Nc[0:P, 0, 0:P], start=True, stop=True),
                                   r=[Nc], w=[pw], inc=last)
                                if not last:
                                    P_("pe", lambda: pe.matmul(pw[0:P, 1, 0:P], Nc[0:P, 0, 0:P], Nc[0:P, 1, 0:P], start=True,
                                                               stop=True), r=[Nc], w=[pw], inc=True)
                            for q_, i in enumerate(pr):
                                P = tP(i)
                                Nx = NP[q_][1 - cur]
                                pw = pW[q_]
                                if not last:
                                    P_("act", lambda: act.copy(out=Nx[0:P, :, 0:P], in_=pw[0:P, 0:2, 0:P]), r=[pw], w=[Nx])
                                else:
                                    P_("act", lambda: act.copy(out=Nx[0:P, 0, 0:P], in_=pw[0:P, 0, 0:P]), r=[pw], w=[Nx])
                            for q_, i in enumerate(pr):
                                P = tP(i)
                                Nx = NP[q_][1 - cur]
                                Pc = PP[q_][cur]
                                pw = pW[q_]
                                P_("pe", lambda: pe.matmul(pw[0:P, 2, 0:P], Pc[0:P, 1, 0:P], Nx[0:P, 0, 0:P], start=True, stop=True),
                                   r=[Nx, Pc], w=[pw], inc=False)
                                P_("pe", lambda: pe.matmul(pw[0:P, 3, 0:P], Nx[0:P, 0, 0:P], Pc[0:P, 1, 0:P], start=True, stop=True),
                                   r=[Nx, Pc], w=[pw])
                            for q_, i in enumerate(pr):
                                P = tP(i)
                                Pc = PP[q_][cur]
                                Pn = PP[q_][1 - cur]
                                P_("dve", lambda: dve.tensor_tensor(out=Pn[0:P, :, 0:P], in0=pW[q_][0:P, 2:4, 0:P],
                                                                    in1=Pc[0:P, :, 0:P], op=ALU.add), r=[pW[q_], Pc], w=[Pn])
                            yield
                        fin = nlev % 2
                        for q_, i in enumerate(pr):
                            P = tP(i)
                            Y0 = PP[q_][fin]
                            pw = pW[q_]
                            P_("pe", lambda: pe.matmul(pw[0:P, 0, 0:P], NT0[q_][0:P, 0:P], Y0[0:P, 1, 0:P], start=True, stop=False),
                               r=[NT0[q_], Y0], w=[pw], inc=False)
                            P_("pe", lambda: pe.matmul(pw[0:P, 0, 0:P], ident_b[0:P, 0:P], ident_b[0:P, 0:P], start=False, stop=True),
                               r=[ident_b], w=[pw])
                        for q_, i in enumerate(pr):
                            P = tP(i)
                            Y0 = PP[q_][fin]
                            P_("dve", lambda: dve.tensor_tensor(out=Rn[q_][0:P, 0:P], in0=pW[q_][0:P, 0, 0:P], in1=Y0[0:P, 1, 0:P],
                                                                op=ALU.subtract), r=[pW[q_], Y0], w=[Rn[q_]])
                        for q_, i in enumerate(pr):
                            P = tP(i)
                            Y0 = PP[q_][fin]
                            P_("pe", lambda: pe.matmul(pW[q_][0:P, 1, 0:P], Y0[0:P, 0, 0:P], Rn[q_][0:P, 0:P], start=True, stop=True),
                               r=[Y0, Rn[q_]], w=[pW[q_]])
                        for q_, i in enumerate(pr):
                            P = tP(i)
                            sl = par * GT + t0 + q_
                            Y0 = PP[q_][fin]
                            P_("dve", lambda: dve.tensor_tensor(out=Ysl[sl][0:P, 0:P], in0=pW[q_][0:P, 1, 0:P], in1=Y0[0:P, 1, 0:P],
                                                                op=ALU.add), r=[pW[q_], Y0], w=[Ysl[sl]])
                        yield

                def gdn_chain(h, g, kf):
                    c0, c1 = gcol(g)
                    par = kf % 2
                    tiles = gtiles(g)
                    mt = mts[h % 2]
                    if g == 0:
                        P_("dve", lambda: dve.memset(Sf[:], 0.0), w=[Sf])
                        P_("dve", lambda: dve.memset(Sb[:], 0.0), w=[Sb])
                    for ti, i in enumerate(tiles):
                        P = tP(i)
                        lc = tcol(i)[0] - c0
                        tc0, tc1 = tcol(i)
                        sl = par * GT + ti
                        pc = pC
                        kTt = kT[par][:, lc:lc + P]
                        qTt = qT[par][:, lc:lc + P]
                        P_("pe", lambda: pe.matmul(pc[0:P, 0, :], kTt, Sb[:], start=True, stop=True), r=[kT[par], Sb], w=[pc],
                           inc=False)
                        P_("pe", lambda: pe.matmul(pc[0:P, 2, :], qTt, Sb[:], start=True, stop=True), r=[qT[par], Sb], w=[pc])
                        P_("dve", lambda: dve.scalar_tensor_tensor(out=Rb[0:P, :], in0=pc[0:P, 0, :], scalar=neG[0:P, i, h:h + 1],
                                                                   in1=vtok[par][0:P, ti, :], op0=ALU.mult, op1=ALU.add),
                           r=[pc, neG, vtok[par]], w=[Rb])
                        P_("pe", lambda: pe.matmul(pc[0:P, 1, :], Ysl[sl][0:P, 0:P], Rb[0:P, :], start=True, stop=True),
                           r=[Ysl[sl], Rb], w=[pc])
                        yield
                        P_("act", lambda: act.activation(out=vnd[0:P, :], in_=pc[0:P, 1, :], func=AF.Copy,
                                                         scale=bEl[0:P, i, h:h + 1]), r=[pc, bEl], w=[vnd])
                        P_("act", lambda: act.activation(out=vnew[0:P, :], in_=pc[0:P, 1, :], func=AF.Copy,
                                                         scale=beta[0:P, i, h:h + 1]), r=[pc, beta], w=[vnew])
                        P_("act", lambda: act.activation(out=t1o[0:P, :], in_=pc[0:P, 2, :], func=AF.Copy,
                                                         scale=eG[0:P, i, h:h + 1]), r=[pc, eG], w=[t1o])
                        P_("pe", lambda: pe.matmul(pc[:, 0, :], ktok[par][0:P, ti, :], vnd[0:P, :], start=True, stop=True),
                           r=[ktok[par], vnd], w=[pc], inc=False)
                        P_("pe", lambda: pe.matmul(pc[0:P, 3, :], Asl[sl][0:P, 0:P], vnew[0:P, :], start=True, stop=True),
                           r=[Asl[sl], vnew], w=[pc])
                        P_("dve", lambda: dve.scalar_tensor_tensor(out=Sf[:], in0=Sf[:], scalar=dec[:, i, h:h + 1], in1=pc[:, 0, :],
                                                                   op0=ALU.mult, op1=ALU.add), r=[Sf, dec, pc], w=[Sf])
                        P_("act", lambda: act.copy(out=Sb[:], in_=Sf[:]), r=[Sf], w=[Sb])
                        yield
                        P_("dve", lambda: dve.tensor_tensor(out=oo[0:P, :], in0=pc[0:P, 3, :], in1=t1o[0:P, :], op=ALU.add),
                           r=[pc, t1o], w=[oo])
                        P_("act", lambda: act.activation(out=junkg[0:P, :], in_=oo[0:P, :], func=AF.Square,
                                                         accum_out=sso[0:P, 0:1]), r=[oo], w=[junkg, sso])
                        P_("dve", lambda: dve.tensor_scalar(out=sso[0:P, 1:2], in0=sso[0:P, 0:1], scalar1=1.0 / 128, scalar2=EPS,
                                                            op0=ALU.mult, op1=ALU.add), r=[sso], w=[sso])
                        P_("pool", lambda: pool.tensor_tensor(out=sso[0:P, 1:2], in0=sso[0:P, 1:2], in1=negh[0:P, 0:1], op=ALU.pow),
                           r=[sso, negh], w=[sso])
                        P_("act", lambda: act.activation(out=onb[0:P, :], in_=oo[0:P, :], func=AF.Copy, scale=sso[0:P, 1:2]),
                           r=[oo, sso], w=[onb])
                        osl = 4 + (ti % 4)
                        P_("pe", lambda: pe.transpose(pT[:, osl, 0:P], onb[0:P, :], ident_b[0:P, 0:P]), r=[onb, ident_b], w=[pT])
                        P_("dve", lambda: dve.scalar_tensor_tensor(out=mt[:, tc0:tc1], in0=pT[:, osl, 0:P], scalar=ncs[:, 0:1],
                                                                   in1=szT[par][:, lc:lc + P], op0=ALU.mult, op1=ALU.mult),
                           r=[pT, ncs, szT[par]], w=[mt])
                        yield
                    if g == NG - 1:
                        S.dma("sp", mscr[h], mt[:], dms[h % 2], reads=[mt])
                        if mdbg is not None:
                            S.dma("sp", mdbg[h], mt[:], dms[h % 2], reads=[mt])

                P_("dve", lambda: dve.memset(pV[:, 3, :], 1.0), w=[pV])
                flat = [(h, g) for h in range(HG) for g in range(NG)]
                for _ in gdn_front(flat[0][0], flat[0][1], 0):
                    pass
                for kf in range(len(flat)):
                    fr = gdn_front(flat[kf + 1][0], flat[kf + 1][1], kf + 1) if kf + 1 < len(flat) else None
                    merge(fr, gdn_chain(flat[kf][0], flat[kf][1], kf), 3)
                S.barrier()

            with ExitStack() as pf:
                qTf = [sb(pf, "qTf%d" % i, [128, GW], BF16) for i in range(2)]
                kTf = [sb(pf, "kTf%d" % i, [128, L], BF16) for i in range(2)]
                vTf = sb(pf, "vTf", [128, GW], BF16)
                vext = [sb(pf, "vext%d" % i, [128, NTI, 132], BF16) for i in range(2)]
                sgT = [sb(pf, "sgT%d" % i, [128, GW], BF16) for i in range(2)]
                rawq = sb(pf, "rawq", [128, GW])
                rawk = sb(pf, "rawk", [128, GW])
                sqf = sb(pf, "sqf", [128, GW], BF16)
                rs8f = sb(pf, "rs8f", [128, 2 * GT])
                dgf = [sb(pf, "dgf%d" % i, [128, 128], BF16) for i in range(2)]
                tnf = sb(pf, "tnf", [128, GW])
                Af = [sb(pf, "Af%d" % i, [128, 128]) for i in range(2)]
                pTs = [sb(pf, "pTs%d" % i, [128, 128], BF16) for i in range(4)]
                tof = sb(pf, "tof", [128, 132])
                of = sb(pf, "of", [128, 132])
                rinv = sb(pf, "rinv", [128, 1])
                onf = sb(pf, "onf", [128, 128], BF16)
                pUf = ps(pf, "pUf", [128, 512])
                pTf = ps(pf, "pTf", [128, GT + 1, 128], BF16)
                pSSf = ps(pf, "pSSf", [128, 2 * GT])
                pS = [ps(pf, "pS%d" % i, [128, 128]) for i in range(2)]
                pO = ps(pf, "pO", [128, 2, 132])

                def fox_front(h, g, kf):
                    u = HG + h
                    c0, c1 = gcol(g)
                    n = c1 - c0
                    par = kf % 2
                    hp_ = h % 2
                    tiles = gtiles(g)
                    if g == 0 and u + 1 < NU:
                        load_w(u + 1)
                    for b in range(2):
                        pbk, _ = project(u, b, g)
                        rw = rawq if b == 0 else rawk
                        P_("act", lambda: act.copy(out=rw[:, 0:n], in_=pbk[:, 0:n]), r=[pbk], w=[rw])
                        P_("pool", lambda: pool.tensor_tensor(out=sqf[:, 0:n], in0=rw[:, 0:n], in1=rw[:, 0:n], op=ALU.mult),
                           r=[rw], w=[sqf])
                        for ti, i in enumerate(tiles):
                            P = tP(i)
                            lc = tcol(i)[0] - c0
                            P_("pe", lambda: pe.matmul(pSSf[0:P, b * GT + ti:b * GT + ti + 1], sqf[:, lc:lc + P], ones_b[:, 0:1],
                                                       start=True, stop=True), r=[sqf, ones_b], w=[pSSf],
                               inc=(ti == len(tiles) - 1))
                        yield
                    pbk, _ = project(u, 2, g)
                    P_("act", lambda: act.copy(out=vTf[:, 0:n], in_=pbk[:, 0:n]), r=[pbk], w=[vTf])
                    yield
                    pbk, _ = project(u, 3, g)
                    P_("act", lambda: act.activation(out=tnf[:, 0:n], in_=pbk[:, 0:n], func=AF.Tanh, scale=0.5), r=[pbk], w=[tnf])
                    P_("dve", lambda: dve.scalar_tensor_tensor(out=sgT[par][:, 0:n], in0=tnf[:, 0:n], scalar=1.0, in1=pbk[:, 0:n],
                                                               op0=ALU.add, op1=ALU.mult), r=[tnf, pbk], w=[sgT[par]])
                    yield
                    PG = tP(tiles[0])
                    ncol8 = 2 * GT
                    P_("dve", lambda: dve.tensor_scalar(out=rs8f[0:PG, 0:ncol8], in0=pSSf[0:PG, 0:ncol8], scalar1=1.0 / 128,
                                                        scalar2=EPS, op0=ALU.mult, op1=ALU.add), r=[pSSf], w=[rs8f])
                    P_("pool", lambda: pool.tensor_tensor(out=rs8f[0:PG, 0:ncol8], in0=rs8f[0:PG, 0:ncol8],
                                                          in1=negh[0:PG, 0:ncol8], op=ALU.pow), r=[rs8f, negh], w=[rs8f])
                    for b in range(2):
                        rw = rawq if b == 0 else rawk
                        for ti, i in enumerate(tiles):
                            P = tP(i)
                            lc = tcol(i)[0] - c0
                            dgb = dgf[(b * GT + ti) % 2]
                            P_("dve", lambda: dve.tensor_scalar(out=dgb[0:P, 0:P], in0=ident_f[0:P, 0:P],
                                                                scalar1=rs8f[0:P, b * GT + ti:b * GT + ti + 1], scalar2=None,
                                                                op0=ALU.mult), r=[ident_f, rs8f], w=[dgb])
                            P_("pe", lambda: pe.matmul(pUf[:, lc:lc + P], ones_b[0:P, :], dgb[0:P, 0:P], start=True, stop=True),
                               r=[ones_b, dgb], w=[pUf])
                        if b == 0:
                            P_("dve", lambda: dve.scalar_tensor_tensor(out=qTf[par][:, 0:n], in0=rw[:, 0:n], scalar=ncs[:, 1:2],
                                                                       in1=pUf[:, 0:n], op0=ALU.mult, op1=ALU.mult),
                               r=[rw, ncs, pUf], w=[qTf[par]])
                        else:
                            P_("dve", lambda: dve.scalar_tensor_tensor(out=kTf[hp_][:, c0:c1], in0=rw[:, 0:n], scalar=ncols[:, 2:3],
                                                                       in1=pUf[:, 0:n], op0=ALU.mult, op1=ALU.mult),
                               r=[rw, ncols, pUf], w=[kTf[hp_]])
                        yield
                    for ti, i in enumerate(tiles):
                        P = tP(i)
                        lc = tcol(i)[0] - c0
                        P_("pe", lambda: pe.transpose(pTf[0:P, ti, :], vTf[:, lc:lc + P], ident_b[:]), r=[vTf, ident_b], w=[pTf],
                           inc=(ti == len(tiles) - 1))
                    for ti, i in enumerate(tiles):
                        P = tP(i)
                        P_("act", lambda: act.copy(out=vext[hp_][0:P, i, 0:128], in_=pTf[0:P, ti, :]), r=[pTf], w=[vext[hp_]])
                    yield

                def fox_attn(h, g, kf):
                    c0, c1 = gcol(g)
                    par = kf % 2
                    hp_ = h % 2
                    tiles = gtiles(g)
                    mt = mts[(HG + h) % 2]
                    kTh, veh = kTf[hp_], vext[hp_]
                    nS = [0]
                    for ti, i in enumerate(tiles):
                        P = tP(i)
                        lc = tcol(i)[0] - c0
                        tc0, tc1 = tcol(i)
                        qt = qTf[par][:, lc:lc + P]
                        blocks = list(range(i)) + ["d"]
                        slot = {}

                        def emit_s(bi):
                            j = blocks[bi]
                            psj = pS[nS[0] % 2]
                            pts = pTs[nS[0] % 4]
                            nS[0] += 1
                            slot[bi] = (psj, pts)
                            if j == "d":
                                afb = Af[ti % 2]
                                P_("dve", lambda: dve.tensor_scalar(out=afb[0:P, 0:P], in0=mask_le[0:P, 0:P],
                                                                    scalar1=logf[0:P, i, h:h + 1], scalar2=None, op0=ALU.mult),
                                   r=[mask_le, logf], w=[afb])
                                P_("pe", lambda: pe.matmul(psj[0:P, 0:P], kTh[:, tc0:tc1], qt, start=True, stop=False),
                                   r=[kTh, qTf[par]], w=[psj], inc=False)
                                P_("pe", lambda: pe.matmul(psj[0:P, 0:P], bst[0:P, 0:P], afb[0:P, 0:P], start=False, stop=False),
                                   r=[bst, afb], w=[psj], inc=False)
                                P_("pe", lambda: pe.matmul(psj[0:P, 0:P], ident_b[0:P, 0:P], mneg_b[0:P, 0:P], start=False, stop=True),
                                   r=[ident_b, mneg_b], w=[psj])
                            else:
                                Pj = tP(j)
                                jc0, jc1 = tcol(j)
                                P_("pe", lambda: pe.matmul(psj[0:Pj, 0:P], kTh[:, jc0:jc1], qt, start=True, stop=True),
                                   r=[kTh, qTf[par]], w=[psj])

                        def emit_pv(bi):
                            j = blocks[bi]
                            psj, pts = slot[bi]
                            if j == "d":
                                P_("act", lambda: act.activation(out=pts[0:P, 0:P], in_=psj[0:P, 0:P], func=AF.Exp), r=[psj], w=[pts])
                                P_("pe", lambda: pe.matmul(pO[0:P, 1, 0:129], pts[0:P, 0:P], veh[0:P, i, 0:129], start=True,
                                                           stop=True), r=[pts, veh], w=[pO])
                            else:
                                Pj = tP(j)
                                P_("act", lambda: act.activation(out=pts[0:Pj, 0:P], in_=psj[0:Pj, 0:P], func=AF.Exp,
                                                                 bias=biasm[0:Pj, h, i, j:j + 1]), r=[psj, biasm], w=[pts])
                                P_("pe", lambda: pe.matmul(pO[0:P, 0, 0:129], pts[0:Pj, 0:P], veh[0:Pj, j, 0:129], start=(j == 0),
                                                           stop=(j == i - 1)), r=[pts, veh], w=[pO], inc=(j == i - 1))

                        nb = len(blocks)
                        emit_s(0)
                        if nb > 1:
                            emit_s(1)
                        for bi in range(nb):
                            emit_pv(bi)
                            if bi + 2 < nb:
                                emit_s(bi + 2)
                            if bi % 3 == 2:
                                yield
                        if i > 0:
                            P_("act", lambda: act.activation(out=tof[0:P, 0:129], in_=pO[0:P, 0, 0:129], func=AF.Copy,
                                                             scale=afac[0:P, i, h:h + 1]), r=[pO, afac], w=[tof])
                            P_("dve", lambda: dve.tensor_tensor(out=of[0:P, 0:129], in0=pO[0:P, 1, 0:129], in1=tof[0:P, 0:129],
                                                                op=ALU.add), r=[pO, tof], w=[of])
                        else:
                            P_("dve", lambda: dve.tensor_copy(out=of[0:P, 0:129], in_=pO[0:P, 1, 0:129]), r=[pO], w=[of])
                        P_("dve", lambda: dve.reciprocal(out=rinv[0:P, :], in_=of[0:P, 128:129]), r=[of], w=[rinv])
                        P_("act", lambda: act.activation(out=onf[0:P, :], in_=of[0:P, 0:128], func=AF.Copy, scale=rinv[0:P, 0:1]),
                           r=[of, rinv], w=[onf])
                        P_("pe", lambda: pe.transpose(pTf[:, GT, 0:P], onf[0:P, :], ident_b[0:P, 0:P]), r=[onf, ident_b], w=[pTf])
                        P_("dve", lambda: dve.scalar_tensor_tensor(out=mt[:, tc0:tc1], in0=pTf[:, GT, 0:P], scalar=0.5,
                                                                   in1=sgT[par][:, lc:lc + P], op0=ALU.mult, op1=ALU.mult),
                           r=[pTf, sgT[par]], w=[mt])
                        yield
                    if g == NG - 1:
                        u = HG + h
                        S.dma("sp", mscr[u], mt[:], dms[u % 2], reads=[mt])
                        if mdbg is not None:
                            S.dma("sp", mdbg[u], mt[:], dms[u % 2], reads=[mt])

                for vx in vext:
                    P_("pool", lambda: pool.memset(vx[:], 1.0), w=[vx])
                P_("dve", lambda: dve.memset(pSSf[:], 1.0), w=[pSSf])
                flat = [(h, g) for h in range(HF) for g in range(NG)]
                for _ in fox_front(flat[0][0], flat[0][1], 0):
                    pass
                for kf in range(len(flat)):
                    fr = fox_front(flat[kf + 1][0], flat[kf + 1][1], kf + 1) if kf + 1 < len(flat) else None
                    merge(fr, fox_attn(flat[kf][0], flat[kf][1], kf), 1)
                S.barrier()

        with ExitStack() as pc_:
            wo = sb(pc_, "wo", [128, NU, DM], BF16)
            postw = sb(pc_, "postw", [128, DM])
            ots = [sb(pc_, "ot%d" % i, [128, DM]) for i in range(2)]
            xrs = [sb(pc_, "xr%d" % i, [128, DM]) for i in range(2)]
            junkc = sb(pc_, "junkc", [128, 512], BF16)
            ss4 = sb(pc_, "ss4", [128, 8])
            rsc = sb(pc_, "rsc", [128, 2])
            pob = [ps(pc_, "pob%d" % i, [128, 512]) for i in range(2)]
            dwo = S.new_dma_sem("dwo")
            dmt = S.new_dma_sem("dmt")
            dxr = [S.new_dma_sem("dxr%d" % i) for i in range(2)]
            dst = [S.new_dma_sem("dst%d" % i) for i in range(2)]
            mT = xnT
            for u in range(NU):
                S.dma("sp", mT[:, u, :], mscr[u], dmt, writes=[mT])
            mT.d.w = (dmt["sem"], dmt["val"])
            S.dma("sp", postw[:], postw_d, S.new_dma_sem("dpostw"), writes=[postw])
            NQ = max(1, DM // 512)
            for q in range(NU):
                S.dma("pool", wo[:, q, :], wo_d[:, q, :], dwo, writes=[wo])
            wo.d.w = (dwo["sem"], dwo["val"])
            ncg = (DM + 511) // 512
            nmm = 0
            for i in range(1, NTI):
                tc0, tc1 = tcol(i)
                ot, xr = ots[i % 2], xrs[i % 2]
                S.dma("sp", xr[:], xin[(i - 1) * 128:i * 128, :], dxr[i % 2], writes=[xr])
                for cgi in range(ncg):
                    n0 = cgi * 512
                    nn = min(512, DM - n0)
                    pb_ = pob[nmm % 2]
                    nmm += 1
                    for k in range(NU):
                        P_("pe", lambda: pe.matmul(pb_[:, 0:nn], mT[:, k, tc0:tc1], wo[:, k, n0:n0 + nn], start=(k == 0),
                                                   stop=(k == NU - 1)), r=[mT, wo], w=[pb_], inc=(k == NU - 1))
                    P_("act", lambda: act.copy(out=ot[:, n0:n0 + nn], in_=pb_[:, 0:nn]), r=[pb_], w=[ot])
                    P_("act", lambda: act.activation(out=junkc[:, 0:nn], in_=pb_[:, 0:nn], func=AF.Square,
                                                     accum_out=ss4[:, cgi:cgi + 1]), r=[pb_], w=[junkc, ss4])
                P_("dve", lambda: dve.reduce_sum(out=rsc[:, 0:1], in_=ss4[:, 0:ncg], axis=mybir.AxisListType.X), r=[ss4], w=[rsc])
                P_("dve", lambda: dve.tensor_scalar(out=rsc[:, 1:2], in0=rsc[:, 0:1], scalar1=1.0 / DM, scalar2=EPS, op0=ALU.mult,
                                                    op1=ALU.add), r=[rsc], w=[rsc])
                P_("pool", lambda: pool.tensor_tensor(out=rsc[:, 1:2], in0=rsc[:, 1:2], in1=negh[:, 0:1], op=ALU.pow),
                   r=[rsc, negh], w=[rsc])
                P_("dve", lambda: dve.scalar_tensor_tensor(out=ot[:], in0=ot[:], scalar=rsc[:, 1:2], in1=postw[:], op0=ALU.mult,
                                                           op1=ALU.mult), r=[ot, rsc, postw], w=[ot])
                P_("pool", lambda: pool.tensor_tensor(out=ot[:], in0=ot[:], in1=xr[:], op=ALU.add), r=[ot, xr], w=[ot])
                S.dma("sp", yout[(i - 1) * 128:i * 128, :], ot[:], dst[i % 2], reads=[ot])
            for d in dst:
                if d["val"] > 0:
                    S.eng["sp"].wait_ge(d["sem"], d["val"])
            S.barrier()
    return nc


def prep_inputs(cfg, x, meta_tokens, pre_norm_w, w_in, conv_w, a_log, dt_bias, gdn_norm_w, fox_q_norm_w,
                fox_k_norm_w, fox_f_bias, w_out, post_norm_w):
    DM, NT, HG, HF = cfg["DM"], cfg["NT"], cfg["HG"], cfg["HF"]
    KC = DM // 128
    NU = HG + HF
    GWd, FWd = HG * 128, HF * 128
    w = np.asarray(w_in[0], np.float32)
    g_off = [0, GWd, 2 * GWd, 3 * GWd]
    sm_off = 4 * GWd
    f_base = 4 * GWd + 2 * HG
    f_off = [f_base, f_base + FWd, f_base + 2 * FWd, f_base + 3 * FWd]
    ff_off = f_base + 4 * FWd
    wu = np.empty((NU, 4, 128, KC, 128), np.float32)
    for u in range(NU):
        for b in range(4):
            c = (g_off[b] + u * 128) if u < HG else (f_off[b] + (u - HG) * 128)
            wu[u, b] = w[:, c:c + 128].reshape(KC, 128, 128).transpose(1, 0, 2)
    cols = list(range(sm_off, sm_off + 2 * HG)) + list(range(ff_off, ff_off + HF))
    wsm = np.ascontiguousarray(w[:, cols].reshape(KC, 128, len(cols)).transpose(1, 0, 2))
    cw = np.asarray(conv_w[0], np.float32)
    convw = np.empty((128, HG, 3, 4), np.float32)
    for h in range(HG):
        for b in range(3):
            convw[:, h, b, :] = cw[b * GWd + h * 128:b * GWd + (h + 1) * 128, :]
    convw = convw.reshape(128, HG * 12)
    hp = np.concatenate([np.asarray(a_log[0]), np.asarray(dt_bias[0]), np.asarray(fox_f_bias[0])]).astype(np.float32)
    hp = np.ascontiguousarray(np.broadcast_to(hp[None, :], (128, hp.shape[0])))
    ncols = np.stack([np.asarray(gdn_norm_w[0]), np.asarray(fox_q_norm_w[0]), np.asarray(fox_k_norm_w[0])], axis=1).astype(np.float32)
    prew = np.ascontiguousarray(np.broadcast_to(np.asarray(pre_norm_w[0], np.float32)[None, :], (128, DM)))
    postw = np.ascontiguousarray(np.broadcast_to(np.asarray(post_norm_w[0], np.float32)[None, :], (128, DM)))
    wo = np.ascontiguousarray(np.asarray(w_out[0], np.float32).reshape(NU, 128, DM).transpose(1, 0, 2))
    shared = dict(meta=np.ascontiguousarray(np.asarray(meta_tokens, np.float32)), wu=wu, wsm=wsm, convw=convw, hp=hp,
                  ncols=np.ascontiguousarray(ncols), prew=prew, postw=postw, wo=wo)
    x = np.asarray(x, np.float32)
    return [dict(shared, xin=np.ascontiguousarray(x[b])) for b in range(x.shape[0])]


def kernel(**inputs):
    cfg = FULL_CFG
    in_maps = prep_inputs(cfg, **inputs)
    nc = build(cfg)
    res = run_bass_kernel_spmd(nc, in_maps, core_ids=list(range(len(in_maps))))
    return np.stack([np.asarray(r["y"]) for r in res.results], axis=0).astype(np.float32)
```
